# Optimizing a Trainium2 kernel written in Bass

```python
import jax, jax.numpy as jnp
from jax import lax
import numpy as np

D_MODEL = 1024
BATCH = 8
SEQ = 4096
DEPTH = 1

HEAD_DIM = 64
N_HEADS_A = 8
DILATED_PATTERNS = ((128, 1), (512, 4), (2048, 16))
N_HEADS_B = 8
Q_LORA_RANK = 256
KV_LORA_RANK = 128
QK_NOPE_DIM = 64
QK_ROPE_DIM = 32
V_HEAD_DIM = 64
ROPE_THETA = 10000.0
Q_BLOCK = 128
N_BUCKETS = 32
MAX_DISTANCE = 1024
D_FF = 4 * D_MODEL
NORM_EPS = 1e-6
NEG_INF = -1e30

WIDTH_A = N_HEADS_A * HEAD_DIM
WIDTH_B = N_HEADS_B * V_HEAD_DIM
MIX_WIDTH = WIDTH_A + WIDTH_B
IN_SPLITS = (WIDTH_A, WIDTH_A, WIDTH_A, Q_LORA_RANK, KV_LORA_RANK, QK_ROPE_DIM)
IN_COLS = sum(IN_SPLITS)

kernel_name = "hybrid_dilated_mla_encoder_block"


def rms_norm(x, g):
    xf = x.astype(jnp.float32)
    y = xf * lax.rsqrt(jnp.mean(xf * xf, axis=-1, keepdims=True) + NORM_EPS)
    return (y * g.astype(jnp.float32)).astype(x.dtype)


def t5_buckets(rel):
    nb = N_BUCKETS // 2
    max_exact = nb // 2
    ret = (rel > 0).astype(np.int32) * nb
    n = np.abs(rel)
    large = max_exact + (np.log(np.maximum(n, 1) / max_exact)
                         / np.log(MAX_DISTANCE / max_exact) * (nb - max_exact)).astype(np.int32)
    large = np.minimum(large, nb - 1)
    return (ret + np.where(n < max_exact, n, large)).astype(np.int32)


def dilated_window_pattern(q, k, v, rel_bias, window, dilation):
    B, S, H, Dh = q.shape
    radius = window // (2 * dilation)
    blk = radius
    L = S // dilation
    nblk = -(-L // blk)
    pad = nblk * blk - L
    scale = HEAD_DIM ** -0.5

    def strided(t):
        return t.reshape(B, L, dilation, H, Dh).transpose(0, 3, 2, 1, 4)

    qb = jnp.pad(strided(q), ((0, 0), (0, 0), (0, 0), (0, pad), (0, 0)))
    qb = qb.reshape(B, H, dilation, nblk, blk, Dh)

    def key_windows(t):
        t = jnp.pad(strided(t), ((0, 0), (0, 0), (0, 0), (blk, pad + blk), (0, 0)))
        t = t.reshape(B, H, dilation, nblk + 2, blk, Dh)
        return jnp.concatenate([t[:, :, :, :-2], t[:, :, :, 1:-1], t[:, :, :, 2:]], axis=4)

    kw = key_windows(k)
    vw = key_windows(v)

    qi = np.arange(blk)[:, None]
    ki = np.arange(3 * blk)[None, :]
    rel = ki - blk - qi
    kidx = np.arange(nblk)[:, None, None] * blk - blk + ki[None]
    valid = (np.abs(rel) <= radius)[None] & (kidx >= 0) & (kidx < L)
    bias = rel_bias[jnp.asarray(t5_buckets(rel * dilation))]
    bias = jnp.transpose(bias, (2, 0, 1)).astype(jnp.float32)[:, None, None]

    s = jnp.einsum('bhrnqd,bhrnkd->bhrnqk', qb, kw).astype(jnp.float32) * scale + bias
    s = jnp.where(jnp.asarray(valid)[None, None, None], s, NEG_INF)
    m = jnp.max(s, axis=-1, keepdims=True)
    p = jnp.exp(s - m)
    den = jnp.sum(p, axis=-1, keepdims=True)
    num = jnp.einsum('bhrnqk,bhrnkd->bhrnqd', p, vw.astype(jnp.float32))

    def unstride(t):
        c = t.shape[-1]
        t = t.reshape(B, H, dilation, nblk * blk, c)[:, :, :, :L]
        return t.transpose(0, 3, 2, 1, 4).reshape(B, S, H, c)

    return unstride(num), unstride(m), unstride(den)


def dilated_attention(q, k, v, rel_bias):
    parts = [dilated_window_pattern(q, k, v, rel_bias, w, d) for (w, d) in DILATED_PATTERNS]
    m_all = parts[0][1]
    for _, m_i, _ in parts[1:]:
        m_all = jnp.maximum(m_all, m_i)
    num = 0.0
    den = 0.0
    for num_i, m_i, den_i in parts:
        w_i = jnp.exp(m_i - m_all)
        num = num + w_i * num_i
        den = den + w_i * den_i
    return (num / den).astype(q.dtype)


def rope_tables(S):
    inv_freq = ROPE_THETA ** (-jnp.arange(0, QK_ROPE_DIM, 2, dtype=jnp.float32) / QK_ROPE_DIM)
    pos = jnp.arange(S, dtype=jnp.float32)
    freqs = pos[:, None] * inv_freq[None, :]
    return jnp.cos(freqs), jnp.sin(freqs)


def apply_rope(t, cos, sin):
    tf = t.astype(jnp.float32)
    half = tf.shape[-1] // 2
    t1, t2 = tf[..., :half], tf[..., half:]
    return jnp.concatenate([t1 * cos - t2 * sin, t2 * cos + t1 * sin], axis=-1).astype(t.dtype)


def latent_attention(c_q, c_kv, k_rope, q_norm_g, w_q_b, kv_norm_g, w_kv_b, cos, sin):
    B, S, _ = c_q.shape
    H = N_HEADS_B
    dqk = QK_NOPE_DIM + QK_ROPE_DIM
    q = (rms_norm(c_q, q_norm_g) @ w_q_b).reshape(B, S, H, dqk)
    q_nope, q_pe = q[..., :QK_NOPE_DIM], q[..., QK_NOPE_DIM:]
    q_pe = apply_rope(q_pe, cos[:, None, :], sin[:, None, :])
    kv = (rms_norm(c_kv, kv_norm_g) @ w_kv_b).reshape(B, S, H, QK_NOPE_DIM + V_HEAD_DIM)
    k_nope, v = kv[..., :QK_NOPE_DIM], kv[..., QK_NOPE_DIM:]
    k_pe = apply_rope(k_rope, cos, sin)
    k = jnp.concatenate([k_nope, jnp.broadcast_to(k_pe[:, :, None, :], (B, S, H, QK_ROPE_DIM))], axis=-1)
    q = jnp.concatenate([q_nope, q_pe], axis=-1)

    scale = dqk ** -0.5
    qb = q.reshape(B, S // Q_BLOCK, Q_BLOCK, H, dqk).transpose(1, 0, 3, 2, 4)
    kh = k.transpose(0, 2, 1, 3)
    vh = v.transpose(0, 2, 1, 3).astype(jnp.float32)

    def attend(q_blk):
        s = jnp.einsum('bhqd,bhkd->bhqk', q_blk, kh).astype(jnp.float32) * scale
        p = jax.nn.softmax(s, axis=-1)
        return jnp.einsum('bhqk,bhkd->bhqd', p, vh).astype(v.dtype)

    o = lax.map(attend, qb)
    return o.transpose(1, 0, 3, 2, 4).reshape(B, S, H * V_HEAD_DIM)


def setup_inputs(seed: int = 0) -> dict:
    key = jax.random.key(seed)
    ks = jax.random.split(key, 16)
    f32 = jnp.float32

    def nrm(k, shape, fan_in):
        return jax.random.normal(k, shape, f32) * (fan_in ** -0.5)

    def gain(k, shape):
        return 1.0 + 0.02 * jax.random.normal(k, shape, f32)

    return {
        "x": jax.random.normal(ks[0], (BATCH, SEQ, D_MODEL), f32),
        "mix_norm_g": gain(ks[1], (DEPTH, D_MODEL)),
        "w_in": nrm(ks[2], (DEPTH, D_MODEL, IN_COLS), D_MODEL),
        "q_norm_g": gain(ks[3], (DEPTH, Q_LORA_RANK)),
        "w_q_b": nrm(ks[4], (DEPTH, Q_LORA_RANK, N_HEADS_B * (QK_NOPE_DIM + QK_ROPE_DIM)), Q_LORA_RANK),
        "kv_norm_g": gain(ks[5], (DEPTH, KV_LORA_RANK)),
        "w_kv_b": nrm(ks[6], (DEPTH, KV_LORA_RANK, N_HEADS_B * (QK_NOPE_DIM + V_HEAD_DIM)), KV_LORA_RANK),
        "w_out": nrm(ks[7], (DEPTH, MIX_WIDTH, D_MODEL), MIX_WIDTH),
        "mlp_norm_g": gain(ks[8], (DEPTH, D_MODEL)),
        "w_up": nrm(ks[9], (DEPTH, D_MODEL, D_FF), D_MODEL),
        "w_down": nrm(ks[10], (DEPTH, D_FF, D_MODEL), D_FF),
        "rel_bias": 0.5 * jax.random.normal(ks[11], (N_BUCKETS, N_HEADS_A), f32),
        "final_norm_g": gain(ks[12], (D_MODEL,)),
    }


def reference(x, mix_norm_g, w_in, q_norm_g, w_q_b, kv_norm_g, w_kv_b, w_out,
              mlp_norm_g, w_up, w_down, rel_bias, final_norm_g):
    B, S, _ = x.shape
    cos, sin = rope_tables(S)
    split_at = [int(v) for v in np.cumsum(IN_SPLITS)[:-1]]
    h = x
    for layer in range(DEPTH):
        u = rms_norm(h, mix_norm_g[layer])
        proj = u @ w_in[layer]
        q_a, k_a, v_a, c_q, c_kv, k_rope = jnp.split(proj, split_at, axis=-1)
        o_a = dilated_attention(q_a.reshape(B, S, N_HEADS_A, HEAD_DIM),
                                k_a.reshape(B, S, N_HEADS_A, HEAD_DIM),
                                v_a.reshape(B, S, N_HEADS_A, HEAD_DIM),
                                rel_bias).reshape(B, S, WIDTH_A)
        o_b = latent_attention(c_q, c_kv, k_rope, q_norm_g[layer], w_q_b[layer],
                               kv_norm_g[layer], w_kv_b[layer], cos, sin)
        h = h + jnp.concatenate([o_a, o_b], axis=-1) @ w_out[layer]
        u = rms_norm(h, mlp_norm_g[layer])
        h = h + jnp.square(jax.nn.relu(u @ w_up[layer])) @ w_down[layer]
    return rms_norm(h, final_norm_g)
```

```python
import numpy as np
from contextlib import ExitStack
import concourse.bass as bass
import concourse.mybir as mybir
from concourse.bass_utils import run_bass_kernel_spmd

F32 = mybir.dt.float32
BF16 = mybir.dt.bfloat16
ALU = mybir.AluOpType
AF = mybir.ActivationFunctionType

S = 4096
D = 1024
NCH = 8
CH = 512
INC = 1952
PAD = 1024
NEG = -30000.0
EPS = 1e-6
PATTERNS = ((128, 1), (512, 4), (2048, 16))
SKEW = 2
SKEW_A = 2
DFF = 4096


class Tracker:
    def __init__(self, nc, es):
        self.nc = nc
        self.es = es
        self.engs = {"pe": nc.tensor, "act": nc.scalar, "dve": nc.vector, "pool": nc.gpsimd, "sp": nc.sync}
        self.sems = {}
        self.cnt = {}
        for k in ("pe", "act", "dve", "pool"):
            self.sems[k] = es.enter_context(nc.semaphore("sem_" + k))
            self.cnt[k] = 0
        self.seen = {e: {} for e in self.engs}
        self.lastw = {}
        self.readers = {}
        self.nwaits = 0

    def chan(self, name):
        if name not in self.sems:
            self.sems[name] = self.es.enter_context(self.nc.semaphore("dma_" + name))
            self.cnt[name] = 0
        return name

    def _deps(self, eng, r, w):
        deps = {}

        def add(key, val, raw):
            if key == eng and not raw and eng in ("pe", "sp"):
                return
            if deps.get(key, 0) < val:
                deps[key] = val

        for res in r:
            lw = self.lastw.get(res)
            if lw is not None:
                add(lw[0], lw[1], True)
        for res in w:
            lw = self.lastw.get(res)
            if lw is not None:
                add(lw[0], lw[1], False)
            for k, v in self.readers.get(res, {}).items():
                add(k, v, False)
        return deps

    def _emit_waits(self, eng, deps):
        seen = self.seen[eng]
        for key, val in deps.items():
            if key not in ("pe", "act", "dve", "pool"):
                val = self.cnt[key]
            if seen.get(key, 0) < val:
                self.engs[eng].wait_ge(self.sems[key], val)
                seen[key] = val
                self.nwaits += 1

    def _update(self, key, seq, r, w):
        for res in w:
            self.lastw[res] = (key, seq)
            self.readers[res] = {}
        for res in r:
            d = self.readers.setdefault(res, {})
            if d.get(key, 0) < seq:
                d[key] = seq

    def op(self, eng, fn, r=(), w=(), inc=True):
        self._emit_waits(eng, self._deps(eng, r, w))
        ins = fn()
        if inc:
            self.cnt[eng] += 1
            ins.then_inc(self.sems[eng], 1)
            seq = self.cnt[eng]
        else:
            seq = self.cnt[eng] + 1
        self._update(eng, seq, r, w)
        return ins

    def dma(self, q, ch, out, in_, r=(), w=()):
        self.chan(ch)
        if q == "pool":
            self.swq = getattr(self, "swq", [])
            while len(self.swq) >= 3:
                k, v = self.swq.pop(0)
                if self.seen[q].get(k, 0) < v:
                    self.engs[q].wait_ge(self.sems[k], v)
                    self.seen[q][k] = v
            self.swq.append((ch, self.cnt[ch] + 16))
        self._emit_waits(q, self._deps(q, r, w))
        self.engs[q].dma_start(out=out, in_=in_).then_inc(self.sems[ch], 16)
        self.cnt[ch] += 16
        self._update(ch, self.cnt[ch], r, w)

    def barrier(self):
        keys = list(self.cnt.keys())
        for e in self.engs:
            self.wait_all(e, [k for k in keys if k != "sp"])

    def wait_all(self, eng, keys):
        for k in keys:
            if self.cnt[k] > 0 and self.seen[eng].get(k, 0) < self.cnt[k]:
                self.engs[eng].wait_ge(self.sems[k], self.cnt[k])
                self.seen[eng][k] = self.cnt[k]


def t5_buckets(rel):
    nb = 16
    max_exact = 8
    ret = (rel > 0).astype(np.int32) * nb
    n = np.abs(rel)
    large = max_exact + (np.log(np.maximum(n, 1) / max_exact) / np.log(1024 / max_exact) * (nb - max_exact)).astype(np.int32)
    large = np.minimum(large, nb - 1)
    return (ret + np.where(n < max_exact, n, large)).astype(np.int32)


def build_program(debug=False):
    nc = bass.Bass("TRN2", target_bir_lowering=False)

    def din(name, shape, dt=F32):
        return nc.dram_tensor(name, list(shape), dt, kind="ExternalInput").ap()

    xT = din("xT", [D, S])
    w_in = din("w_in", [D, INC])
    g_mix = din("g_mix", [128, 8])
    w_qb = din("w_qb", [256, 768])
    g_q = din("g_q", [128, 2])
    w_kvb = din("w_kvb", [128, 1024])
    g_kv = din("g_kv", [128, 1])
    w_out = din("w_out", [D, D])
    g_mlp = din("g_mlp", [128, 8])
    w_up = din("w_up", [D, DFF])
    w_down = din("w_down", [DFF, D])
    g_fin = din("g_fin", [128, 8])
    biasT = din("biasT", [8, 128, 7 * 512])
    ident_in = din("ident", [128, 128])
    cos_in = din("cosT", [128, S])
    sin_in = din("sinT", [128, S])
    outT = nc.dram_tensor("outT", [D, S], F32, kind="ExternalOutput").ap()
    mixT = nc.dram_tensor("mixT_scratch", [D, S], BF16).ap()
    dbg = {}
    if debug:
        for nm, shp in (("d_uT", [D, S]), ("d_cq", [256, S]), ("d_ckv", [128, S]), ("d_kpe", [128, S]),
                        ("d_mix", [D, S])):
            dbg[nm] = nc.dram_tensor(nm, shp, F32, kind="ExternalOutput").ap()

    with ExitStack() as es:
        T = Tracker(nc, es)
        op, dma = T.op, T.dma
        PE, ACT, DVE, POOL, SP = "pe", "act", "dve", "pool", "sp"
        eng = T.engs

        def sb(name, shape, dt, stack=es):
            return stack.enter_context(nc.sbuf_tensor("sb_" + name, list(shape), dt))

        psd = [es.enter_context(nc.psum_tensor("psd%d" % i, [128, 1024], F32)) for i in range(4)]
        banks = [psd[k // 2][:, (k % 2) * 512:(k % 2 + 1) * 512] for k in range(8)]
        bank7 = banks[7]
        bankT = bank7.bitcast(BF16)

        ident = sb("ident", [128, 128], BF16)
        ones = sb("ones", [128, 128], BF16)
        dma(POOL, "c_id", ident[:], ident_in, w=["ident"])
        op(POOL, lambda: eng[POOL].memset(ones[:], 1.0), w=["ones"])

        gq = sb("gq", [128, 2], F32)
        gkv = sb("gkv", [128, 1], F32)
        dma(SP, "c_gq", gq[:], g_q, w=["gq"])
        dma(SP, "c_gkv", gkv[:], g_kv, w=["gkv"])
        eps_t = sb("eps_t", [128, 1], F32)
        op(POOL, lambda: eng[POOL].memset(eps_t[:], EPS), w=["eps"])
        xT_v = xT.rearrange("(k p) t -> p k t", p=128)
        mixT_v = mixT.rearrange("(k p) t -> p k t", p=128)
        outT_v = outT.rearrange("(k p) t -> p k t", p=128)

        botA = sb("botA", [128, S], F32)
        botB = sb("botB", [128, S], BF16)
        botC = sb("botC", [128, S + 2 * PAD], BF16)
        with ExitStack() as esA:
            uT = sb("uT", [128, 8, S], BF16, esA)
            win = sb("win", [128, 8, INC], BF16, esA)
            wkr_sw = sb("wkr_sw", [128, 8, 96], BF16, esA)
            gmix = sb("gmix", [128, 8], F32, esA)
            dma(SP, "c_gmix", gmix[:], g_mix, w=["gmix"])
            op(POOL, lambda: eng[POOL].memset(wkr_sw[:], 0.0), w=["wkr_sw"])

            for kc in range(8):
                dma(POOL, "c_win", win[:, kc, :], w_in[kc * 128:(kc + 1) * 128, :], w=["win"])
            for kc in range(8):
                op(POOL, lambda: eng[POOL].tensor_scalar(out=wkr_sw[:, kc, 64:80], in0=win[:, kc, 1936:1952], scalar1=-1.0, scalar2=None, op0=ALU.mult),
                   r=["win"], w=["wkr_sw"])
                op(POOL, lambda: eng[POOL].tensor_copy(out=wkr_sw[:, kc, 80:96], in_=win[:, kc, 1920:1936]), r=["win"], w=["wkr_sw"])
            with ExitStack() as es0:
                xs = [sb("xs%d" % i, [128, 8, CH], F32, es0) for i in range(2)]
                sq = [sb("sq%d" % i, [128, CH], BF16, es0) for i in range(2)]
                rtmp = sb("rtmp", [128, CH], F32, es0)
                rstd = sb("rstd", [128, CH], F32, es0)
                for c in range(NCH):
                    s = c % 2
                    cs = slice(c * CH, (c + 1) * CH)
                    xres = "xs%d" % s
                    dma(SP, xres, xs[s][:], xT_v[:, :, cs], w=[xres])
                    for kc in range(8):
                        q = kc % 2
                        op(ACT, lambda: eng[ACT].activation(out=sq[q][:], in_=xs[s][:, kc, :], func=AF.Square), r=[xres], w=["sq%d" % q])
                        op(PE, lambda: eng[PE].matmul(banks[0][:], ones[:], sq[q][:], start=(kc == 0), stop=(kc == 7)),
                           r=["ones", "sq%d" % q], w=["bank0"])
                    op(ACT, lambda: eng[ACT].activation(out=rtmp[:], in_=banks[0][:], func=AF.Ln, scale=1.0 / D, bias=eps_t[:, 0:1]),
                       r=["bank0", "eps"], w=["rtmp"])
                    op(ACT, lambda: eng[ACT].activation(out=rstd[:], in_=rtmp[:], func=AF.Exp, scale=-0.5), r=["rtmp"], w=["rstd"])
                    for kc in range(8):
                        op(DVE, lambda: eng[DVE].scalar_tensor_tensor(out=uT[:, kc, cs], in0=xs[s][:, kc, :], scalar=gmix[:, kc:kc + 1], in1=rstd[:], op0=ALU.mult, op1=ALU.mult),
                           r=[xres, "rstd", "gmix"], w=["uT%d_%d" % (kc, c)])
                if debug:
                    dst = sb("dstage", [128, S], F32, es0)
                    for kc in range(8):
                        op(DVE, lambda: eng[DVE].tensor_copy(out=dst[:], in_=uT[:, kc, :]), r=["uT%d_%d" % (kc, c) for c in range(NCH)], w=["dstage"])
                        dma(SP, "dbg", dbg["d_uT"][kc * 128:(kc + 1) * 128, :], dst[:], r=["dstage"])
                T.barrier()

            with ExitStack() as e1:
                qT = botB
                kT = sb("kT", [128, S + 2 * PAD], BF16, e1)
                vT = botC
                NT3 = 33 + 36 + 48
                v3 = sb("v3", [128, NT3, 3, 64], BF16, e1)
                acc = botA
                bia = sb("bia", [128, 7, 512], BF16, e1)

                def load_bias(hb):
                    for (c0, c1, pn) in ((0, 3, 0), (3, 6, 1), (6, 7, 2)):
                        dma(POOL, "bia_p%d" % pn, bia[:, c0:c1, :], biasT[hb][:, c0 * 512:c1 * 512].rearrange("p (a b) -> p a b", b=512), w=["bia_p%d" % pn])
                pT = [sb("pT%d" % i, [128, 512], BF16, e1) for i in range(3)]
                qS = [sb("qS%d" % i, [128, 1024], BF16, e1) for i in range(2)]
                fillc = [0]
                ostA = [sb("ostA%d" % i, [128, CH], BF16, e1) for i in range(2)]
                rdenA = [sb("rdenA0", [128, CH], F32, e1)] * 2
                op(POOL, lambda: eng[POOL].memset(kT[:, 0:PAD], 0.0), w=["kT_pad"])
                op(POOL, lambda: eng[POOL].memset(kT[:, PAD + S:], 0.0), w=["kT_pad2"])
                op(POOL, lambda: eng[POOL].memset(vT[:, 0:PAD], 0.0), w=["vT_pad"])
                op(POOL, lambda: eng[POOL].memset(vT[:, PAD + S:], 0.0), w=["vT_pad2"])
                op(POOL, lambda: eng[POOL].memset(v3[:, :, 1, :], 1.0), w=["v3_%d" % bb for bb in range(15)])
                print("phase A sbuf free:", nc.sbuf_bytes_remaining)
                KT_ALL = ["kT_c%d" % cc for cc in range(NCH)] + ["kT_pad", "kT_pad2"]
                tiles = []
                for (win_, d) in PATTERNS:
                    L = S // d
                    for r_ in range(d):
                        for jp in range(L // 128 + 1):
                            tiles.append((PAD + (-64 + 128 * jp) * d + r_, d))
                assert len(tiles) == NT3
                ev = 0
                pcount = 0
                ocount = 0
                ncount = 0
                for hp in range(4):
                    def proj_chunk(which, col0, c):
                        nonlocal ev
                        cs = slice(c * CH, (c + 1) * CH)
                        bi = 1 + (ev % 3)
                        ev += 1
                        b = banks[bi]
                        bn = "bank%d" % bi
                        for kc in range(8):
                            op(PE, lambda: eng[PE].matmul(b[:], win[:, kc, col0:col0 + 128], uT[:, kc, cs], start=(kc == 0), stop=(kc == 7)),
                               r=["win", "uT%d_%d" % (kc, c)], w=[bn], inc=(kc == 7))
                        if which == "q":
                            op(ACT, lambda: eng[ACT].mul(out=qT[:, cs], in_=b[:], mul=0.125), r=[bn], w=["qT_c%d" % c])
                        elif which == "k":
                            op(DVE, lambda: eng[DVE].tensor_copy(out=kT[:, PAD + c * CH:PAD + (c + 1) * CH], in_=b[:]), r=[bn], w=["kT_c%d" % c])
                        else:
                            op(ACT, lambda: eng[ACT].copy(out=vT[:, PAD + c * CH:PAD + (c + 1) * CH], in_=b[:]), r=[bn], w=["vT_c%d" % c])

                    def transpose_batch(t0):
                        n = min(8, NT3 - t0)
                        for i in range(n):
                            off, d = tiles[t0 + i]
                            src = vT[:, off:off + 127 * d + 1:d]
                            op(PE, lambda: eng[PE].transpose(bankT[:, i * 128:(i + 1) * 128], src, ident[:]),
                               r=["vT_c%d" % cc for cc in range(NCH)] + ["vT_pad", "vT_pad2", "ident"], w=["bankT"], inc=(i == n - 1))
                        src_v = bankT[:, 0:n * 128].rearrange("p (t h e) -> p t h e", h=2, e=64)
                        op(DVE, lambda: eng[DVE].tensor_copy(out=v3[:, t0:t0 + n, 0:3:2, :], in_=src_v), r=["bankT"], w=["v3_%d" % (t0 // 8)])

                    for c in range(NCH):
                        proj_chunk("v", 1024 + hp * 128, c)
                    tb = list(range(0, NT3, 8))
                    qk = [("k", 512 + hp * 128, c) for c in range(NCH)] + [("q", hp * 128, c) for c in range(NCH)]
                    for i, (which, col0, c) in enumerate(qk):
                        proj_chunk(which, col0, c)
                        if i < len(tb):
                            transpose_batch(tb[i])
                    for i in range(len(qk), len(tb)):
                        transpose_batch(tb[i])
                    for hh in range(2):
                        h = 2 * hp + hh
                        prow = slice(hh * 64, hh * 64 + 64)
                        nbase, dbase = (0, 64) if hh == 0 else (64, 0)
                        if h == 0:
                            load_bias(0)
                        steps = []
                        tbase = 0
                        for pi, (win_, d) in enumerate(PATTERNS):
                            L = S // d
                            nqt = L // 128
                            gsz = min(4, nqt)
                            for r_ in range(d):
                                for g0 in range(0, nqt, gsz):
                                    obi = 4 + (ocount % 2)
                                    ocount += 1
                                    for q0 in range(g0, g0 + gsz, 2):
                                        first = (q0 == 0)
                                        last = (q0 + 1 == nqt - 1)
                                        if d == 16:
                                            combo = 6
                                        else:
                                            combo = pi * 3 + (0 if first else (2 if last else 1))
                                        fill = (pi, q0 // 8) if d == 1 else (pi, r_ if d == 4 else r_ // 4)
                                        steps.append(dict(pi=pi, d=d, r=r_, g0=g0, gsz=gsz, q0=q0, nqt=nqt, tbase=tbase, obi=obi,
                                                          combo=combo, glast=(q0 + 2 >= g0 + gsz), fill=fill))
                            tbase += d * (nqt + 1)

                        def emit_S(st):
                            nonlocal pcount
                            sbi = 1 + (pcount % 3)
                            st["sbi"] = sbi
                            st["pti"] = pcount % 3
                            pcount += 1
                            sbk = banks[sbi]
                            sbn = "bank%d" % sbi
                            d, r_ = st["d"], st["r"]
                            op(PE, lambda: eng[PE].matmul(sbk[:], ident[:], bia[:, st["combo"], :], start=True, stop=False, skip_group_check=True),
                               r=["ident", "bia_p%d" % st["pi"]], w=[sbn], inc=False)
                            for qq in range(2):
                                qt = st["q0"] + qq
                                slot = st["slot"]
                                if d == 1:
                                    qo = 128 * (qt % 8)
                                elif d == 4:
                                    qo = 128 * qt
                                else:
                                    qo = (r_ % 4) * 256 + 128 * qt
                                qap = qS[slot][:, qo:qo + 128]
                                qres = "qS%d" % slot
                                for j in range(2):
                                    l0 = 128 * qt - 64 + 128 * j
                                    koff = PAD + l0 * d + r_
                                    kap = kT[:, koff:koff + 127 * d + 1:d]
                                    dst = sbk[:, (qq * 2 + j) * 128:(qq * 2 + j + 1) * 128]
                                    op(PE, lambda: eng[PE].matmul(dst, kap, qap, start=False, stop=True, skip_group_check=True),
                                       r=KT_ALL + [qres], w=[sbn], inc=(qq == 1 and j == 1))
                            pt = pT[st["pti"]]
                            op(ACT, lambda: eng[ACT].activation(out=pt[:], in_=sbk[:], func=AF.Exp), r=[sbn], w=["pT%d" % st["pti"]])

                        def emit_PV(st):
                            d, r_, g0, gsz = st["d"], st["r"], st["g0"], st["gsz"]
                            ob = banks[st["obi"]]
                            obn = "bank%d" % st["obi"]
                            pt = pT[st["pti"]]
                            ptn = "pT%d" % st["pti"]
                            for qq in range(2):
                                qt = st["q0"] + qq
                                for j in range(2):
                                    ti = st["tbase"] + r_ * (st["nqt"] + 1) + qt + j
                                    lhs = v3[:, ti, 0:2, :] if hh == 0 else v3[:, ti, 1:3, :]
                                    oq = (qt - g0)
                                    op(PE, lambda: eng[PE].matmul(ob[:, oq * 128:(oq + 1) * 128], lhs.rearrange("p a b -> p (a b)"),
                                                                   pt[:, (qq * 2 + j) * 128:(qq * 2 + j + 1) * 128], start=(j == 0), stop=(j == 1)),
                                       r=["v3_%d" % (ti // 8), ptn], w=[obn], inc=(qq == 1 and j == 1))
                            if st["glast"]:
                                a0 = 128 * g0 * d + r_
                                a1 = a0 + (128 * gsz - 1) * d + 1
                                aview = acc[:, a0:a1:d]
                                accr = ["acc_c%d" % cc for cc in range(a0 // CH, (a1 - 1) // CH + 1)]
                                if st["pi"] == 0:
                                    op(DVE, lambda: eng[DVE].tensor_copy(out=aview, in_=ob[:, 0:128 * gsz]), r=[obn], w=accr)
                                else:
                                    op(DVE, lambda: eng[DVE].tensor_tensor(out=aview, in0=aview, in1=ob[:, 0:128 * gsz], op=ALU.add), r=[obn] + accr, w=accr)

                        fills = []
                        for st in steps:
                            if st["fill"] is not None and (not fills or fills[-1] != st["fill"]):
                                fills.append(st["fill"])
                        fslot = {}

                        def emit_fill(f):
                            pi_, fi = f
                            d_ = PATTERNS[pi_][1]
                            slot = fillc[0] % 2
                            fillc[0] += 1
                            fslot[f] = slot
                            if d_ == 1:
                                src = qT[prow, fi * 1024:(fi + 1) * 1024]
                                dstq = qS[slot][prow, :]
                            elif d_ == 4:
                                src = qT[prow, fi:fi + 4 * 1023 + 1:4]
                                dstq = qS[slot][prow, :]
                            else:
                                src = qT[prow, :].rearrange("p (l r) -> p r l", r=16)[:, 4 * fi:4 * fi + 4, :]
                                dstq = qS[slot][prow, :].rearrange("p (r l) -> p r l", l=256)
                            op(DVE, lambda: eng[DVE].tensor_copy(out=dstq, in_=src), r=["qT_c%d" % cc for cc in range(NCH)], w=["qS%d" % slot])

                        orow = slice(64 - hh * 64, 128 - hh * 64)
                        for sl_ in range(2):
                            op(POOL, lambda: eng[POOL].memset(qS[sl_][orow, :], 0.0), w=["qS%d" % sl_])
                        nf = 0
                        if fills:
                            emit_fill(fills[0])
                            nf = 1
                        pend = []
                        curf = None
                        for st in steps:
                            if st["fill"] is not None and st["fill"] != curf:
                                curf = st["fill"]
                                if nf < len(fills):
                                    emit_fill(fills[nf])
                                    nf += 1
                            if st["fill"] is not None:
                                st["slot"] = fslot[st["fill"]]
                            emit_S(st)
                            pend.append(st)
                            if len(pend) > SKEW_A:
                                emit_PV(pend.pop(0))
                        while pend:
                            emit_PV(pend.pop(0))
                        if h + 1 < 8:
                            load_bias(h + 1)
                        nrow = slice(nbase, nbase + 64)
                        drow = slice(dbase, dbase + 64)
                        for c in range(NCH):
                            cs = slice(c * CH, (c + 1) * CH)
                            s = ncount % 2
                            ncount += 1
                            an = "acc_c%d" % c
                            op(ACT, lambda: eng[ACT].activation(out=acc[drow, cs], in_=acc[drow, cs], func=AF.Ln), r=[an], w=[an])
                            op(ACT, lambda: eng[ACT].activation(out=acc[drow, cs], in_=acc[drow, cs], func=AF.Exp, scale=-1.0), r=[an], w=[an])
                            op(DVE, lambda: eng[DVE].tensor_copy(out=rdenA[s][nrow, :], in_=acc[drow, cs]), r=[an], w=["rdenA0"])
                            op(DVE, lambda: eng[DVE].tensor_tensor(out=ostA[s][nrow, :], in0=acc[nrow, cs], in1=rdenA[s][nrow, :], op=ALU.mult),
                               r=[an, "rdenA0"], w=["ostA%d" % s])
                            dma(SP, "ostA%d" % s, mixT[h * 64:(h + 1) * 64, cs], ostA[s][nrow, :], r=["ostA%d" % s], w=["mixT%d" % h])
                T.barrier()

            cqn = botA[:].bitcast(BF16).rearrange("p (k t) -> p k t", k=2)
            ckvn = botB[:]
            kpe = botC[:, 0:S]
            with ExitStack() as es0:
                sq = [sb("sqb%d" % i, [128, CH], BF16, es0) for i in range(2)]
                rtmp = [sb("rtmpb%d" % i, [128, CH], F32, es0) for i in range(2)]
                rstd = [sb("rstdb%d" % i, [128, CH], F32, es0) for i in range(2)]
                lat = [sb("lat%d" % i, [128, 3, CH], F32, es0) for i in range(2)]
                kr1 = [sb("kr1_%d" % i, [128, CH], F32, es0) for i in range(2)]
                kr2 = [sb("kr2_%d" % i, [128, CH], F32, es0) for i in range(2)]
                csc = [sb("csc%d" % i, [128, 2, CH], F32, es0) for i in range(2)]

                def lat_proj(c):
                    cs = slice(c * CH, (c + 1) * CH)
                    s = c % 2
                    dma(SP, "csc%d" % s, csc[s][64:96, 0, :], cos_in[64:96, cs], w=["csc%d" % s])
                    dma(SP, "csc%d" % s, csc[s][64:96, 1, :], sin_in[64:96, cs], w=["csc%d" % s])
                    ur = ["uT%d_%d" % (kc, c) for kc in range(8)]
                    for i, col0 in enumerate((1536, 1664, 1792)):
                        b = banks[1 + i]
                        bn = "bank%d" % (1 + i)
                        for kc in range(8):
                            op(PE, lambda: eng[PE].matmul(b[:], win[:, kc, col0:col0 + 128], uT[:, kc, cs], start=(kc == 0), stop=(kc == 7)),
                               r=["win"] + ur, w=[bn], inc=(kc == 7))
                        op(ACT, lambda: eng[ACT].copy(out=lat[s][:, i, :], in_=b[:]), r=[bn], w=["lat%d_%d" % (s, i)])
                    for kc in range(8):
                        op(PE, lambda: eng[PE].matmul(banks[4][0:96, :], win[:, kc, 1856:1952], uT[:, kc, cs], start=(kc == 0), stop=(kc == 7)),
                           r=["win"] + ur, w=["bank4"], inc=(kc == 7))
                    for kc in range(8):
                        op(PE, lambda: eng[PE].matmul(banks[5][0:96, :], wkr_sw[:, kc, :], uT[:, kc, cs], start=(kc == 0), stop=(kc == 7)),
                           r=["wkr_sw"] + ur, w=["bank5"], inc=(kc == 7))
                    op(DVE, lambda: eng[DVE].tensor_tensor(out=kr1[s][64:96, :], in0=banks[4][64:96, :], in1=csc[s][64:96, 0, :], op=ALU.mult),
                       r=["bank4", "csc%d" % s], w=["kr1_%d" % s])
                    op(DVE, lambda: eng[DVE].tensor_tensor(out=kr2[s][64:96, :], in0=banks[5][64:96, :], in1=csc[s][64:96, 1, :], op=ALU.mult),
                       r=["bank5", "csc%d" % s], w=["kr2_%d" % s])
                    op(POOL, lambda: eng[POOL].tensor_tensor(out=kpe[64:96, cs], in0=kr1[s][64:96, :], in1=kr2[s][64:96, :], op=ALU.add),
                       r=["kr1_%d" % s, "kr2_%d" % s], w=["kpe_c%d" % c])

                def lat_norm(c):
                    cs = slice(c * CH, (c + 1) * CH)
                    s = c % 2
                    for i in range(3):
                        q = i % 2
                        op(ACT, lambda: eng[ACT].activation(out=sq[q][:], in_=lat[s][:, i, :], func=AF.Square), r=["lat%d_%d" % (s, i)], w=["sqb%d" % q])
                        bsel = banks[6] if i < 2 else banks[0]
                        bname = "bank6" if i < 2 else "bank0"
                        op(PE, lambda: eng[PE].matmul(bsel[:], ones[:], sq[q][:], start=(i != 1), stop=(i != 0)),
                           r=["ones", "sqb%d" % q], w=[bname])
                    for k2, (bsel, bname, nfeat, idxs) in enumerate(((banks[6], "bank6", 256, (0, 1)), (banks[0], "bank0", 128, (2,)))):
                        op(ACT, lambda: eng[ACT].activation(out=rtmp[k2][:], in_=bsel[:], func=AF.Ln, scale=1.0 / nfeat, bias=eps_t[:, 0:1]),
                           r=[bname, "eps"], w=["rtmpb%d" % k2])
                        op(ACT, lambda: eng[ACT].activation(out=rstd[k2][:], in_=rtmp[k2][:], func=AF.Exp, scale=-0.5), r=["rtmpb%d" % k2], w=["rstdb%d" % k2])
                        for i in idxs:
                            dstl = cqn[:, i, cs] if i < 2 else ckvn[:, cs]
                            gsc = gq[:, i:i + 1] if i < 2 else gkv[:, 0:1]
                            op(DVE, lambda: eng[DVE].scalar_tensor_tensor(out=dstl, in0=lat[s][:, i, :], scalar=gsc, in1=rstd[k2][:], op0=ALU.mult, op1=ALU.mult),
                               r=["lat%d_%d" % (s, i), "rstdb%d" % k2, "gq", "gkv"], w=[("cqn_c%d" if i < 2 else "ckvn_c%d") % c])

                for c in range(NCH):
                    lat_proj(c)
                    if c > 0:
                        lat_norm(c - 1)
                lat_norm(NCH - 1)
                T.barrier()
        if debug:
            with ExitStack() as es0:
                dst = sb("dstage2", [128, S], F32, es0)
                for i in range(2):
                    op(DVE, lambda: eng[DVE].tensor_copy(out=dst[:], in_=cqn[:, i, :]), r=["cqn_c%d" % cc for cc in range(NCH)], w=["dstage"])
                    dma(SP, "dbg", dbg["d_cq"][i * 128:(i + 1) * 128, :], dst[:], r=["dstage"])
                op(DVE, lambda: eng[DVE].tensor_copy(out=dst[:], in_=ckvn[:]), r=["ckvn_c%d" % cc for cc in range(NCH)], w=["dstage"])
                dma(SP, "dbg", dbg["d_ckv"][:, :], dst[:], r=["dstage"])
                op(DVE, lambda: eng[DVE].memset(dst[:], 0.0), w=["dstage"])
                op(DVE, lambda: eng[DVE].tensor_copy(out=dst[64:96, :], in_=kpe[64:96, :]), r=["kpe_c%d" % cc for cc in range(NCH)], w=["dstage"])
                dma(SP, "dbg", dbg["d_kpe"][:, :], dst[:], r=["dstage"])
                T.barrier()

        NWA = 4
        wo = sb("wo", [128, 8, D], BF16)
        wuA = sb("wuA", [128, 8, NWA * 512], BF16)
        wd_v = w_down.rearrange("(j p) o -> p j o", p=128)
        wu_v = w_up.rearrange("(k p) f -> p k f", p=128)
        with ExitStack() as eB:
            csB = [sb("csB%d" % i, [128, 2, CH], F32, eB) for i in range(2)]
            wq = sb("wq", [128, 2, 768], BF16, eB)
            wqs = sb("wqs", [128, 2, 768], BF16, eB)
            wkv = sb("wkv", [128, 2, 8, 64], BF16, eB)
            vall = sb("vall", [128, 32, 8, 128], BF16, eB)
            qh = [sb("qh%d" % i, [128, S], BF16, eB) for i in range(2)]
            kh = [sb("kh%d" % i, [128, S], BF16, eB) for i in range(2)]
            stq = qh[1][:, :].bitcast(F32)
            stk = kh[1][:, :].bitcast(F32)
            for kc in range(2):
                dma(SP, "c_wq", stq[:, kc * 768:(kc + 1) * 768], w_qb[kc * 128:(kc + 1) * 128, :], w=["qh1"])
            dma(SP, "c_wkv", stk[:, 0:1024], w_kvb, w=["kh1"])
            for kc in range(8):
                dma(POOL, "c_wo", wo[:, kc, :], w_out[kc * 128:(kc + 1) * 128, :], w=["wo"])
            for b in range(NWA):
                dma(POOL, "c_wu%d" % b, wuA[:, :, b * 512:(b + 1) * 512], wu_v[:, :, b * 512:(b + 1) * 512], w=["wu%d" % b])
            op(DVE, lambda: eng[DVE].memset(wqs[:], 0.0), w=["wqs"])
            op(DVE, lambda: eng[DVE].tensor_copy(out=wq[:].rearrange("p k c -> p (k c)"), in_=stq[:, 0:1536]), r=["qh1"], w=["wq"])
            op(DVE, lambda: eng[DVE].tensor_copy(out=wkv[:], in_=stk[:, 0:1024].rearrange("p (h t e) -> p t h e", t=2, e=64)), r=["kh1"], w=["wkv"])
            wq_v = wq[:].rearrange("p k (h e) -> p k h e", e=96)
            wqs_v = wqs[:].rearrange("p k (h e) -> p k h e", e=96)
            for kc in range(2):
                op(DVE, lambda: eng[DVE].tensor_scalar(out=wqs_v[:, kc, :, 64:80], in0=wq_v[:, kc, :, 80:96], scalar1=-1.0, scalar2=None, op0=ALU.mult),
                   r=["wq"], w=["wqs"])
                op(DVE, lambda: eng[DVE].tensor_copy(out=wqs_v[:, kc, :, 80:96], in_=wq_v[:, kc, :, 64:80]), r=["wq"], w=["wqs"])

            def build_vall(t):
                bi = 1 + t % 2
                op(PE, lambda: eng[PE].matmul(banks[bi][:], ckvn[:, t * 128:(t + 1) * 128], wkv[:, 1, :, :].rearrange("p h e -> p (h e)"), start=True, stop=True),
                   r=["ckvn_c%d" % (t // 4), "wkv"], w=["bank%d" % bi])
                srcv = banks[bi][:].rearrange("p (h e) -> p h e", e=64)
                op(DVE, lambda: eng[DVE].memset(vall[:, t, :, 64:128], 1.0), w=["vall1_%d" % t])
                op(ACT, lambda: eng[ACT].copy(out=vall[:, t, :, 0:64], in_=srcv), r=["bank%d" % bi], w=["vall_%d" % t])

            t1 = sb("t1", [128, CH], F32, eB)
            t2 = sb("t2", [128, CH], F32, eB)
            pB = [sb("pB%d" % i, [128, 1024], BF16, eB) for i in range(2)]
            ostB = [sb("ostB0", [128, CH], BF16, eB)] * 2
            rdB = sb("rdB", [128, CH], F32, eB)
            scale_b = 96.0 ** -0.5
            print("phase B sbuf free:", nc.sbuf_bytes_remaining)
            pc = 0
            ncs = [0, 0]

            def emit_proj(h, c, pro=False):
                nonlocal pc
                s_ = h % 2
                qt_, kt_ = qh[s_], kh[s_]
                qn, kn = "qh%d" % s_, "kh%d" % s_
                cs = slice(c * CH, (c + 1) * CH)
                for kc in range(2):
                    op(PE, lambda: eng[PE].matmul(banks[6][0:96, :], wq[:, kc, h * 96:(h + 1) * 96], cqn[:, kc, cs], start=(kc == 0), stop=(kc == 1)),
                       r=["wq", "cqn_c%d" % c], w=["bank6"], inc=(kc == 1))
                for kc in range(2):
                    op(PE, lambda: eng[PE].matmul(banks[7][0:96, :], wqs[:, kc, h * 96:(h + 1) * 96], cqn[:, kc, cs], start=(kc == 0), stop=(kc == 1)),
                       r=["wqs", "cqn_c%d" % c], w=["bank7"], inc=False)
                op(PE, lambda: eng[PE].matmul(banks[7][0:64, :], wkv[:, 0, h, :], ckvn[:, cs], start=True, stop=True, skip_group_check=True),
                   r=["wkv", "ckvn_c%d" % c], w=["bank7"])
                sl = ncs[0] % 2
                ncs[0] += 1
                dma(SP, "csB%d" % sl, csB[sl][64:96, 0, :], cos_in[64:96, cs], w=["csB%d" % sl])
                dma(SP, "csB%d" % sl, csB[sl][64:96, 1, :], sin_in[64:96, cs], w=["csB%d" % sl])
                if pro:
                    op(ACT, lambda: eng[ACT].copy(out=qt_[0:64, cs], in_=banks[6][0:64, :]), r=["bank6"], w=[qn])
                else:
                    op(DVE, lambda: eng[DVE].tensor_copy(out=qt_[0:64, cs], in_=banks[6][0:64, :]), r=["bank6"], w=[qn])
                op(DVE, lambda: eng[DVE].tensor_tensor(out=t1[64:96, :], in0=banks[6][64:96, :], in1=csB[sl][64:96, 0, :], op=ALU.mult),
                   r=["bank6", "csB%d" % sl], w=["t1"])
                op(DVE, lambda: eng[DVE].tensor_tensor(out=t2[64:96, :], in0=banks[7][64:96, :], in1=csB[sl][64:96, 1, :], op=ALU.mult),
                   r=["bank7", "csB%d" % sl], w=["t2"])
                op(DVE, lambda: eng[DVE].tensor_tensor(out=qt_[64:96, cs], in0=t1[64:96, :], in1=t2[64:96, :], op=ALU.add),
                   r=["t1", "t2"], w=[qn])
                if pro:
                    op(ACT, lambda: eng[ACT].copy(out=kt_[0:64, cs], in_=banks[7][0:64, :]), r=["bank7"], w=[kn])
                else:
                    op(DVE, lambda: eng[DVE].tensor_copy(out=kt_[0:64, cs], in_=banks[7][0:64, :]), r=["bank7"], w=[kn])
                if c == NCH - 1:
                    op(DVE, lambda: eng[DVE].tensor_copy(out=kt_[64:96, :], in_=kpe[64:96, :]), r=["kpe_c%d" % cc for cc in range(NCH)], w=[kn])

            def emit_QK(st):
                h, c, kp = st["h"], st["c"], st["kp"]
                s_ = h % 2
                cs = slice(c * CH, (c + 1) * CH)
                sdi = ncs[1] % 2
                st["pti"] = ncs[1] % 2
                ncs[1] += 1
                sd = psd[sdi]
                bn = ["bank%d" % (2 * sdi), "bank%d" % (2 * sdi + 1)]
                for j in range(2):
                    kt = 2 * kp + j
                    op(PE, lambda: eng[PE].matmul(sd[:, j * 512:(j + 1) * 512], kh[s_][0:96, kt * 128:(kt + 1) * 128], qh[s_][0:96, cs], start=True, stop=True),
                       r=["kh%d" % s_, "qh%d" % s_], w=[bn[j]], inc=(j == 1))
                pt = pB[st["pti"]]
                op(ACT, lambda: eng[ACT].activation(out=pt[:], in_=sd[:], func=AF.Exp, scale=scale_b), r=bn, w=["pB%d" % st["pti"]])

            def emit_PVB(st):
                h, c, kp = st["h"], st["c"], st["kp"]
                cs = slice(c * CH, (c + 1) * CH)
                obi = 4 + (c % 2)
                ob = banks[obi]
                obn = "bank%d" % obi
                pt = pB[st["pti"]]
                for j in range(2):
                    kt = 2 * kp + j
                    op(PE, lambda: eng[PE].matmul(ob[:], vall[:, kt, h, :], pt[:, j * 512:(j + 1) * 512], start=(kt == 0), stop=(kt == 31)),
                       r=["vall_%d" % kt, "vall1_%d" % kt, "pB%d" % st["pti"]], w=[obn], inc=(j == 1))
                if kp == 15:
                    so = c % 2
                    op(DVE, lambda: eng[DVE].reciprocal(out=rdB[0:64, :], in_=ob[64:128, :]), r=[obn], w=["rdB"])
                    op(DVE, lambda: eng[DVE].tensor_tensor(out=ostB[so][0:64, :], in0=ob[0:64, :], in1=rdB[0:64, :], op=ALU.mult),
                       r=[obn, "rdB"], w=["ostB0"])
                    dma(SP, "ostB0", mixT[512 + h * 64:512 + (h + 1) * 64, cs], ostB[so][0:64, :], r=["ostB0"], w=["mixT%d" % (8 + h)])

            for c in range(NCH):
                emit_proj(0, c, pro=True)
                for t in range(4 * c, 4 * c + 4):
                    build_vall(t)
            pend = []
            for h in range(8):
                for c in range(NCH):
                    if h + 1 < 8:
                        emit_proj(h + 1, c)
                    for kp in range(16):
                        st = dict(h=h, c=c, kp=kp)
                        emit_QK(st)
                        pend.append(st)
                        if len(pend) > 1:
                            emit_PVB(pend.pop(0))
            while pend:
                emit_PVB(pend.pop(0))
            T.barrier()

        mix_all = ["mixT%d" % i for i in range(16)]
        if debug:
            with ExitStack() as eD:
                dm = sb("dm", [128, S], BF16, eD)
                dmf = sb("dmf", [128, S], F32, eD)
                for kc in range(8):
                    dma(SP, "dm", dm[:], mixT[kc * 128:(kc + 1) * 128, :], r=mix_all, w=["dm"])
                    op(DVE, lambda: eng[DVE].tensor_copy(out=dmf[:], in_=dm[:]), r=["dm"], w=["dmf"])
                    dma(SP, "dmo", dbg["d_mix"][kc * 128:(kc + 1) * 128, :], dmf[:], r=["dmf"])
                T.barrier()

        with ExitStack() as eC:
            wuB = sb("wuB", [128, 8, (8 - NWA) * 512], BF16, eC)
            wd = sb("wd", [128, 32, D], BF16, eC)
            gm = sb("gm", [128, 8], F32, eC)
            gf = sb("gf", [128, 8], F32, eC)
            dma(SP, "c_gm", gm[:], g_mlp, w=["gm"])
            dma(SP, "c_gf", gf[:], g_fin, w=["gf"])
            def load_wd(j4, after=()):
                dma(POOL, "c_wd%d" % j4, wd[:, 4 * j4:4 * j4 + 4, :], wd_v[:, 4 * j4:4 * j4 + 4, :], r=list(after), w=["wd%d" % j4])

            def load_rest_weights(after):
                load_wd(0, after)
                load_wd(1)
                for b in range(NWA, 8):
                    dma(POOL, "c_wu%d" % b, wuB[:, :, (b - NWA) * 512:(b - NWA + 1) * 512], wu_v[:, :, b * 512:(b + 1) * 512], w=["wu%d" % b])
                for j4 in range(2, 8):
                    load_wd(j4)

            def wu_cols(kc, j):
                if j // 4 < NWA:
                    return wuA[:, kc, j * 128:(j + 1) * 128]
                return wuB[:, kc, (j - NWA * 4) * 128:(j - NWA * 4 + 1) * 128]

            def wd_rows(j, oc):
                return wd[:, j, oc * 128:(oc + 1) * 128]

            hxs = [botA[:].rearrange("p (k t) -> p k t", k=8), sb("hxB", [128, 8, CH], F32, eC)[:]]
            mu = botB[:].rearrange("p (k t) -> p k t", k=8)
            act = botC[:, 0:S].rearrange("p (k t) -> p k t", k=8)
            sqc = [botC[:, S + i * CH:S + (i + 1) * CH] for i in range(2)]
            rl = [sb("rl%d" % i, [128, CH], F32, eC) for i in range(2)]
            rt = sb("rtC", [128, CH], F32, eC)
            rs = sb("rsC", [128, CH], F32, eC)
            bc = 0
            print("phase C sbuf free:", nc.sbuf_bytes_remaining)

            def hn(c, k):
                return "hx%d_%d" % (c % 2, k)

            rt2 = rt
            rs2 = sb("rsC2", [128, CH], F32, eC)
            sqn = [0]

            def stat_part(c, kc):
                hx = hxs[c % 2]
                q = sqn[0] % 2
                sqn[0] += 1
                op(ACT, lambda: eng[ACT].activation(out=sqc[q], in_=hx[:, kc, :], func=AF.Square), r=[hn(c, kc)], w=["sqc%d" % q])
                op(PE, lambda: eng[PE].matmul(banks[0][:], ones[:], sqc[q], start=(kc == 0), stop=(kc == 7)),
                   r=["ones", "sqc%d" % q], w=["bank0"])

            def stat_fin(which):
                rt_, rs_, nm = (rt, rs, "") if which == 1 else (rt2, rs2, "2")
                op(ACT, lambda: eng[ACT].activation(out=rt_[:], in_=banks[0][:], func=AF.Ln, scale=1.0 / D, bias=eps_t[:, 0:1]),
                   r=["bank0", "eps"], w=["rtC"])
                op(ACT, lambda: eng[ACT].activation(out=rs_[:], in_=rt_[:], func=AF.Exp, scale=-0.5), r=["rtC"], w=["rsC" + nm])

            def load_x(c):
                cs_ = slice(c * CH, (c + 1) * CH)
                for oc in range(8):
                    dma(SP, "hxl%d_%d" % (c % 2, oc), hxs[c % 2][:, oc, :], xT_v[:, oc, cs_], w=[hn(c, oc)])

            def load_mc(c):
                cs_ = slice(c * CH, (c + 1) * CH)
                dma(SP, "mc", mu, mixT_v[:, :, cs_], r=mix_all, w=["mu%d" % k for k in range(8)])

            UPB = [(banks[1][:], "bank1"), (banks[2][:], "bank2"), (banks[3][:], "bank3"), (bank7, "bankT")]
            DNB = [(banks[4][:], "bank4"), (banks[5][:], "bank5"), (banks[6][:], "bank6")]
            dc = 0

            def out_proj(c):
                nonlocal bc
                hx = hxs[c % 2]
                for oc in range(8):
                    bap, bnm = UPB[bc % 4]; bc += 1
                    for kc in range(8):
                        op(PE, lambda: eng[PE].matmul(bap, wo[:, kc, oc * 128:(oc + 1) * 128], mu[:, kc, :], start=(kc == 0), stop=(kc == 7)),
                           r=["wo", "mu%d" % kc], w=[bnm], inc=(kc == 7))
                    op(DVE, lambda: eng[DVE].tensor_tensor(out=hx[:, oc, :], in0=hx[:, oc, :], in1=bap, op=ALU.add),
                       r=[bnm, hn(c, oc)], w=[hn(c, oc)])
                    if oc >= 1:
                        stat_part(c, oc - 1)
                stat_part(c, 7)
                stat_fin(1)

            def final_norm(c):
                cs_ = slice(c * CH, (c + 1) * CH)
                hx = hxs[c % 2]
                for kc in range(8):
                    op(DVE, lambda: eng[DVE].scalar_tensor_tensor(out=hx[:, kc, :], in0=hx[:, kc, :], scalar=gf[:, kc:kc + 1], in1=rs2[:], op0=ALU.mult, op1=ALU.mult),
                       r=[hn(c, kc), "rsC2", "gf"], w=[hn(c, kc)])
                    dma(SP, "outst%d_%d" % (c % 2, kc), outT_v[:, kc, cs_], hx[:, kc, :], r=[hn(c, kc)], w=["out"])

            load_mc(0)
            load_x(0)
            load_rest_weights([hn(0, k) for k in range(8)] + ["mu%d" % k for k in range(8)])
            out_proj(0)
            load_x(1)
            for c in range(NCH):
                hx = hxs[c % 2]
                for kc in range(8):
                    op(DVE, lambda: eng[DVE].scalar_tensor_tensor(out=mu[:, kc, :], in0=hx[:, kc, :], scalar=gm[:, kc:kc + 1], in1=rs[:], op0=ALU.mult, op1=ALU.mult),
                       r=[hn(c, kc), "rsC", "gm"], w=["mu%d" % kc])
                for qf in range(4):
                    for jj in range(8):
                        j = qf * 8 + jj
                        bap, bnm = UPB[bc % 4]; bc += 1
                        for kc in range(8):
                            op(PE, lambda: eng[PE].matmul(bap, wu_cols(kc, j), mu[:, kc, :], start=(kc == 0), stop=(kc == 7)),
                               r=["wu%d" % (j // 4), "mu%d" % kc], w=[bnm], inc=(kc == 7))
                        q = j % 2
                        op(DVE, lambda: eng[DVE].tensor_scalar(out=rl[q][:], in0=bap, scalar1=0.0, scalar2=None, op0=ALU.max),
                           r=[bnm], w=["rl%d" % q])
                        op(ACT, lambda: eng[ACT].activation(out=act[:, jj, :], in_=rl[q][:], func=AF.Square), r=["rl%d" % q], w=["act%d" % jj])
                    if qf == 3 and c + 1 < NCH:
                        load_mc(c + 1)
                    for oc in range(8):
                        bap, bnm = DNB[dc % 3]; dc += 1
                        for jj in range(8):
                            j = qf * 8 + jj
                            op(PE, lambda: eng[PE].matmul(bap, wd_rows(j, oc), act[:, jj, :], start=(jj == 0), stop=(jj == 7)),
                               r=["wd%d" % (j // 4), "act%d" % jj], w=[bnm], inc=(jj == 7))
                        op(DVE, lambda: eng[DVE].tensor_tensor(out=hx[:, oc, :], in0=hx[:, oc, :], in1=bap, op=ALU.add),
                           r=[bnm, hn(c, oc)], w=[hn(c, oc)])
                        if qf == 3 and oc >= 1:
                            stat_part(c, oc - 1)
                stat_part(c, 7)
                stat_fin(2)
                if c + 1 < NCH:
                    out_proj(c + 1)
                final_norm(c)
                if c + 2 < NCH:
                    load_x(c + 2)

        T.barrier()
        print("instructions:", dict(T.cnt), "waits:", T.nwaits)
    return nc


_CACHE = {}


def _host_constants(rel_bias):
    rb = np.asarray(rel_bias, np.float32)
    i = np.arange(128)[:, None]
    m = np.arange(128)[None, :]
    out = np.full((8, 128, 7, 4, 128), NEG, np.float32)

    def tile(d, j, edge, h):
        rel = -64 + 128 * j + i - m
        v = np.abs(rel) <= 64
        if edge and j == 0:
            v = v & (i >= 64)
        if edge and j == 1:
            v = v & (i < 64)
        g = rb[t5_buckets(rel * d), h]
        return np.where(v, g, np.float32(NEG))

    for h in range(8):
        for pi, (win_, d) in enumerate(PATTERNS):
            if d == 16:
                combos = [(6, True, True)]
            else:
                combos = [(pi * 3 + 0, True, False), (pi * 3 + 1, False, False), (pi * 3 + 2, False, True)]
            for (ci, first, last) in combos:
                out[h, :, ci, 0, :] = tile(d, 0, first, h)
                out[h, :, ci, 1, :] = tile(d, 1, False, h)
                out[h, :, ci, 2, :] = tile(d, 0, False, h)
                out[h, :, ci, 3, :] = tile(d, 1, last, h)
    return out.reshape(8, 128, 7 * 512)


def _rope_tables():
    inv_freq = (np.float32(10000.0) ** (-np.arange(0, 32, 2, dtype=np.float32) / np.float32(32))).astype(np.float32)
    pos = np.arange(S, dtype=np.float32)
    fr = (pos[:, None] * inv_freq[None, :]).astype(np.float32)
    cos = np.cos(fr).astype(np.float32).T
    sin = np.sin(fr).astype(np.float32).T
    cT = np.ones((128, S), np.float32)
    sT = np.zeros((128, S), np.float32)
    cT[64:80] = cos; cT[80:96] = cos
    sT[64:80] = sin; sT[80:96] = sin
    return cT, sT


def _lay(v, k):
    return np.ascontiguousarray(np.asarray(v, np.float32).reshape(k, 128).T)


def kernel(x, mix_norm_g, w_in, q_norm_g, w_q_b, kv_norm_g, w_kv_b, w_out, mlp_norm_g, w_up, w_down,
           rel_bias, final_norm_g, _debug=False, _cores=8):
    x = np.asarray(x, np.float32)
    key = ("nc", _debug)
    if key not in _CACHE:
        _CACHE[key] = build_program(debug=_debug)
    nc = _CACHE[key]
    cT, sT = _rope_tables()
    shared = {
        "w_in": np.ascontiguousarray(np.asarray(w_in, np.float32)[0]),
        "g_mix": _lay(mix_norm_g, 8),
        "w_qb": np.ascontiguousarray(np.asarray(w_q_b, np.float32)[0]),
        "g_q": _lay(q_norm_g, 2),
        "w_kvb": np.ascontiguousarray(np.asarray(w_kv_b, np.float32)[0]),
        "g_kv": _lay(kv_norm_g, 1),
        "w_out": np.ascontiguousarray(np.asarray(w_out, np.float32)[0]),
        "g_mlp": _lay(mlp_norm_g, 8),
        "w_up": np.ascontiguousarray(np.asarray(w_up, np.float32)[0]),
        "w_down": np.ascontiguousarray(np.asarray(w_down, np.float32)[0]),
        "g_fin": _lay(final_norm_g, 8),
        "biasT": _host_constants(rel_bias),
        "ident": np.eye(128, dtype=np.float32),
        "cosT": cT,
        "sinT": sT,
    }
    in_maps = []
    for b in range(_cores):
        m = dict(shared)
        m["xT"] = np.ascontiguousarray(x[b].T)
        in_maps.append(m)
    res = run_bass_kernel_spmd(nc, in_maps, core_ids=list(range(_cores)))
    if _debug:
        return res.results
    out = np.stack([np.ascontiguousarray(r["outT"].T) for r in res.results], axis=0)
    return out.astype(np.float32)
```

```python
import numpy as np
from contextlib import ExitStack
import concourse.bass as bass
import concourse.mybir as mybir
from concourse.bass_utils import run_bass_kernel_spmd

F32 = mybir.dt.float32
BF16 = mybir.dt.bfloat16
ALU = mybir.AluOpType
AF = mybir.ActivationFunctionType

S = 4096
D = 1024
NCH = 8
CH = 512
INC = 1952
PAD = 1024
NEG = -30000.0
EPS = 1e-6
PATTERNS = ((128, 1), (512, 4), (2048, 16))
SKEW = 2
SKEW_A = 2
DFF = 4096


class Tracker:
    def __init__(self, nc, es):
        self.nc = nc
        self.es = es
        self.engs = {"pe": nc.tensor, "act": nc.scalar, "dve": nc.vector, "pool": nc.gpsimd, "sp": nc.sync}
        self.sems = {}
        self.cnt = {}
        for k in ("pe", "act", "dve", "pool"):
            self.sems[k] = es.enter_context(nc.semaphore("sem_" + k))
            self.cnt[k] = 0
        self.seen = {e: {} for e in self.engs}
        self.lastw = {}
        self.readers = {}
        self.nwaits = 0

    def chan(self, name):
        if name not in self.sems:
            self.sems[name] = self.es.enter_context(self.nc.semaphore("dma_" + name))
            self.cnt[name] = 0
        return name

    def _deps(self, eng, r, w):
        deps = {}

        def add(key, val, raw):
            if key == eng and not raw and eng in ("pe", "sp"):
                return
            if deps.get(key, 0) < val:
                deps[key] = val

        for res in r:
            lw = self.lastw.get(res)
            if lw is not None:
                add(lw[0], lw[1], True)
        for res in w:
            lw = self.lastw.get(res)
            if lw is not None:
                add(lw[0], lw[1], False)
            for k, v in self.readers.get(res, {}).items():
                add(k, v, False)
        return deps

    def _emit_waits(self, eng, deps):
        seen = self.seen[eng]
        for key, val in deps.items():
            if key not in ("pe", "act", "dve", "pool"):
                val = self.cnt[key]
            if seen.get(key, 0) < val:
                self.engs[eng].wait_ge(self.sems[key], val)
                seen[key] = val
                self.nwaits += 1

    def _update(self, key, seq, r, w):
        for res in w:
            self.lastw[res] = (key, seq)
            self.readers[res] = {}
        for res in r:
            d = self.readers.setdefault(res, {})
            if d.get(key, 0) < seq:
                d[key] = seq

    def op(self, eng, fn, r=(), w=(), inc=True):
        self._emit_waits(eng, self._deps(eng, r, w))
        ins = fn()
        if inc:
            self.cnt[eng] += 1
            ins.then_inc(self.sems[eng], 1)
            seq = self.cnt[eng]
        else:
            seq = self.cnt[eng] + 1
        self._update(eng, seq, r, w)
        return ins

    def dma(self, q, ch, out, in_, r=(), w=()):
        self.chan(ch)
        if q == "pool":
            self.swq = getattr(self, "swq", [])
            while len(self.swq) >= 3:
                k, v = self.swq.pop(0)
                if self.seen[q].get(k, 0) < v:
                    self.engs[q].wait_ge(self.sems[k], v)
                    self.seen[q][k] = v
            self.swq.append((ch, self.cnt[ch] + 16))
        self._emit_waits(q, self._deps(q, r, w))
        self.engs[q].dma_start(out=out, in_=in_).then_inc(self.sems[ch], 16)
        self.cnt[ch] += 16
        self._update(ch, self.cnt[ch], r, w)

    def barrier(self):
        keys = list(self.cnt.keys())
        for e in self.engs:
            self.wait_all(e, [k for k in keys if k != "sp"])

    def wait_all(self, eng, keys):
        for k in keys:
            if self.cnt[k] > 0 and self.seen[eng].get(k, 0) < self.cnt[k]:
                self.engs[eng].wait_ge(self.sems[k], self.cnt[k])
                self.seen[eng][k] = self.cnt[k]


def t5_buckets(rel):
    nb = 16
    max_exact = 8
    ret = (rel > 0).astype(np.int32) * nb
    n = np.abs(rel)
    large = max_exact + (np.log(np.maximum(n, 1) / max_exact) / np.log(1024 / max_exact) * (nb - max_exact)).astype(np.int32)
    large = np.minimum(large, nb - 1)
    return (ret + np.where(n < max_exact, n, large)).astype(np.int32)


def build_program(debug=False):
    nc = bass.Bass("TRN2", target_bir_lowering=False)

    def din(name, shape, dt=F32):
        return nc.dram_tensor(name, list(shape), dt, kind="ExternalInput").ap()

    xT = din("xT", [D, S])
    w_in = din("w_in", [D, INC])
    g_mix = din("g_mix", [128, 8])
    w_qb = din("w_qb", [256, 768])
    g_q = din("g_q", [128, 2])
    w_kvb = din("w_kvb", [128, 1024])
    g_kv = din("g_kv", [128, 1])
    w_out = din("w_out", [D, D])
    g_mlp = din("g_mlp", [128, 8])
    w_up = din("w_up", [D, DFF])
    w_down = din("w_down", [DFF, D])
    g_fin = din("g_fin", [128, 8])
    biasT = din("biasT", [8, 128, 7 * 512])
    ident_in = din("ident", [128, 128])
    cos_in = din("cosT", [128, S])
    sin_in = din("sinT", [128, S])
    outT = nc.dram_tensor("outT", [D, S], F32, kind="ExternalOutput").ap()
    mixT = nc.dram_tensor("mixT_scratch", [D, S], BF16).ap()
    dbg = {}
    if debug:
        for nm, shp in (("d_uT", [D, S]), ("d_cq", [256, S]), ("d_ckv", [128, S]), ("d_kpe", [128, S]),
                        ("d_mix", [D, S])):
            dbg[nm] = nc.dram_tensor(nm, shp, F32, kind="ExternalOutput").ap()

    with ExitStack() as es:
        T = Tracker(nc, es)
        op, dma = T.op, T.dma
        PE, ACT, DVE, POOL, SP = "pe", "act", "dve", "pool", "sp"
        eng = T.engs

        def sb(name, shape, dt, stack=es):
            return stack.enter_context(nc.sbuf_tensor("sb_" + name, list(shape), dt))

        banks = [es.enter_context(nc.psum_tensor("bank%d" % i, [128, 512], F32)) for i in range(7)]
        bankT = es.enter_context(nc.psum_tensor("bankT", [128, 1024], BF16))
        bank7 = bankT[:].bitcast(F32)

        ident = sb("ident", [128, 128], BF16)
        ones = sb("ones", [128, 128], BF16)
        dma(POOL, "c_id", ident[:], ident_in, w=["ident"])
        op(POOL, lambda: eng[POOL].memset(ones[:], 1.0), w=["ones"])

        gq = sb("gq", [128, 2], F32)
        gkv = sb("gkv", [128, 1], F32)
        dma(SP, "c_gq", gq[:], g_q, w=["gq"])
        dma(SP, "c_gkv", gkv[:], g_kv, w=["gkv"])
        eps_t = sb("eps_t", [128, 1], F32)
        op(POOL, lambda: eng[POOL].memset(eps_t[:], EPS), w=["eps"])
        xT_v = xT.rearrange("(k p) t -> p k t", p=128)
        mixT_v = mixT.rearrange("(k p) t -> p k t", p=128)
        outT_v = outT.rearrange("(k p) t -> p k t", p=128)

        botA = sb("botA", [128, S], F32)
        botB = sb("botB", [128, S], BF16)
        botC = sb("botC", [128, S + 2 * PAD], BF16)
        with ExitStack() as esA:
            uT = sb("uT", [128, 8, S], BF16, esA)
            win = sb("win", [128, 8, INC], BF16, esA)
            wkr_sw = sb("wkr_sw", [128, 8, 96], BF16, esA)
            gmix = sb("gmix", [128, 8], F32, esA)
            dma(SP, "c_gmix", gmix[:], g_mix, w=["gmix"])
            op(POOL, lambda: eng[POOL].memset(wkr_sw[:], 0.0), w=["wkr_sw"])

            w_in_v = w_in.rearrange("(k p) c -> p k c", p=128)

            def load_win_pair(hp_):
                for col0 in (hp_ * 128, 512 + hp_ * 128, 1024 + hp_ * 128):
                    dma(POOL, "c_win_p%d" % hp_, win[:, :, col0:col0 + 128], w_in_v[:, :, col0:col0 + 128], w=["win_p%d" % hp_])

            def load_win_rest():
                for hp_ in range(1, 4):
                    load_win_pair(hp_)
                dma(POOL, "c_win_lat", win[:, :, 1536:INC], w_in_v[:, :, 1536:INC], w=["win_lat"])
                for kc in range(8):
                    op(POOL, lambda: eng[POOL].tensor_scalar(out=wkr_sw[:, kc, 64:80], in0=win[:, kc, 1936:1952], scalar1=-1.0, scalar2=None, op0=ALU.mult),
                       r=["win_lat"], w=["wkr_sw"])
                    op(POOL, lambda: eng[POOL].tensor_copy(out=wkr_sw[:, kc, 80:96], in_=win[:, kc, 1920:1936]), r=["win_lat"], w=["wkr_sw"])

            load_win_pair(0)
            with ExitStack() as es0:
                xs = [sb("xs%d" % i, [128, 8, CH], F32, es0) for i in range(2)]
                sq = [sb("sq%d" % i, [128, CH], BF16, es0) for i in range(2)]
                rtmp = sb("rtmp", [128, CH], F32, es0)
                rstd = sb("rstd", [128, CH], F32, es0)
                for c in range(NCH):
                    s = c % 2
                    cs = slice(c * CH, (c + 1) * CH)
                    xres = "xs%d" % s
                    dma(SP, xres, xs[s][:], xT_v[:, :, cs], w=[xres])
                    for kc in range(8):
                        q = kc % 2
                        op(ACT, lambda: eng[ACT].activation(out=sq[q][:], in_=xs[s][:, kc, :], func=AF.Square), r=[xres], w=["sq%d" % q])
                        op(PE, lambda: eng[PE].matmul(banks[0][:], ones[:], sq[q][:], start=(kc == 0), stop=(kc == 7)),
                           r=["ones", "sq%d" % q], w=["bank0"])
                    op(ACT, lambda: eng[ACT].activation(out=rtmp[:], in_=banks[0][:], func=AF.Ln, scale=1.0 / D, bias=eps_t[:, 0:1]),
                       r=["bank0", "eps"], w=["rtmp"])
                    op(ACT, lambda: eng[ACT].activation(out=rstd[:], in_=rtmp[:], func=AF.Exp, scale=-0.5), r=["rtmp"], w=["rstd"])
                    for kc in range(8):
                        op(DVE, lambda: eng[DVE].scalar_tensor_tensor(out=uT[:, kc, cs], in0=xs[s][:, kc, :], scalar=gmix[:, kc:kc + 1], in1=rstd[:], op0=ALU.mult, op1=ALU.mult),
                           r=[xres, "rstd", "gmix"], w=["uT%d_%d" % (kc, c)])
                if debug:
                    dst = sb("dstage", [128, S], F32, es0)
                    for kc in range(8):
                        op(DVE, lambda: eng[DVE].tensor_copy(out=dst[:], in_=uT[:, kc, :]), r=["uT%d_%d" % (kc, c) for c in range(NCH)], w=["dstage"])
                        dma(SP, "dbg", dbg["d_uT"][kc * 128:(kc + 1) * 128, :], dst[:], r=["dstage"])
                T.barrier()

            with ExitStack() as e1:
                qT = botB
                kT = sb("kT", [128, S + 2 * PAD], BF16, e1)
                vT = botC
                NT3 = 33 + 36 + 48
                v3 = sb("v3", [128, NT3, 3, 64], BF16, e1)
                acc = botA
                bia = sb("bia", [128, 7, 512], BF16, e1)

                def load_bias(hb):
                    for (c0, c1, pn) in ((0, 3, 0), (3, 6, 1), (6, 7, 2)):
                        dma(POOL, "bia_p%d" % pn, bia[:, c0:c1, :], biasT[hb][:, c0 * 512:c1 * 512].rearrange("p (a b) -> p a b", b=512), w=["bia_p%d" % pn])
                pT = [sb("pT%d" % i, [128, 512], BF16, e1) for i in range(3)]
                qS = [sb("qS%d" % i, [128, 1024], BF16, e1) for i in range(2)]
                fillc = [0]
                ostA = [sb("ostA%d" % i, [128, CH], BF16, e1) for i in range(2)]
                rdenA = [sb("rdenA0", [128, CH], F32, e1)] * 2
                op(POOL, lambda: eng[POOL].memset(kT[:, 0:PAD], 0.0), w=["kT_pad"])
                op(POOL, lambda: eng[POOL].memset(kT[:, PAD + S:], 0.0), w=["kT_pad2"])
                op(POOL, lambda: eng[POOL].memset(vT[:, 0:PAD], 0.0), w=["vT_pad"])
                op(POOL, lambda: eng[POOL].memset(vT[:, PAD + S:], 0.0), w=["vT_pad2"])
                op(POOL, lambda: eng[POOL].memset(v3[:, :, 1, :], 1.0), w=["v3_%d" % bb for bb in range(15)])
                load_win_rest()
                print("phase A sbuf free:", nc.sbuf_bytes_remaining)
                KT_ALL = ["kT_c%d" % cc for cc in range(NCH)] + ["kT_pad", "kT_pad2"]
                tiles = []
                for (win_, d) in PATTERNS:
                    L = S // d
                    for r_ in range(d):
                        for jp in range(L // 128 + 1):
                            tiles.append((PAD + (-64 + 128 * jp) * d + r_, d))
                assert len(tiles) == NT3
                ev = 0
                pcount = 0
                ocount = 0
                ncount = 0
                for hp in range(4):
                    def proj_chunk(which, col0, c):
                        nonlocal ev
                        cs = slice(c * CH, (c + 1) * CH)
                        bi = 1 + (ev % 3)
                        ev += 1
                        b = banks[bi]
                        bn = "bank%d" % bi
                        for kc in range(8):
                            op(PE, lambda: eng[PE].matmul(b[:], win[:, kc, col0:col0 + 128], uT[:, kc, cs], start=(kc == 0), stop=(kc == 7)),
                               r=["win_p%d" % hp, "uT%d_%d" % (kc, c)], w=[bn], inc=(kc == 7))
                        if which == "q":
                            op(ACT, lambda: eng[ACT].mul(out=qT[:, cs], in_=b[:], mul=0.125), r=[bn], w=["qT_c%d" % c])
                        elif which == "k":
                            op(DVE, lambda: eng[DVE].tensor_copy(out=kT[:, PAD + c * CH:PAD + (c + 1) * CH], in_=b[:]), r=[bn], w=["kT_c%d" % c])
                        else:
                            op(ACT, lambda: eng[ACT].copy(out=vT[:, PAD + c * CH:PAD + (c + 1) * CH], in_=b[:]), r=[bn], w=["vT_c%d" % c])

                    def transpose_batch(t0):
                        n = min(8, NT3 - t0)
                        for i in range(n):
                            off, d = tiles[t0 + i]
                            src = vT[:, off:off + 127 * d + 1:d]
                            op(PE, lambda: eng[PE].transpose(bankT[:, i * 128:(i + 1) * 128], src, ident[:]),
                               r=["vT_c%d" % cc for cc in range(NCH)] + ["vT_pad", "vT_pad2", "ident"], w=["bankT"], inc=(i == n - 1))
                        src_v = bankT[:, 0:n * 128].rearrange("p (t h e) -> p t h e", h=2, e=64)
                        op(DVE, lambda: eng[DVE].tensor_copy(out=v3[:, t0:t0 + n, 0:3:2, :], in_=src_v), r=["bankT"], w=["v3_%d" % (t0 // 8)])

                    for c in range(NCH):
                        proj_chunk("v", 1024 + hp * 128, c)
                    tb = list(range(0, NT3, 8))
                    qk = [("k", 512 + hp * 128, c) for c in range(NCH)] + [("q", hp * 128, c) for c in range(NCH)]
                    for i, (which, col0, c) in enumerate(qk):
                        proj_chunk(which, col0, c)
                        if i < len(tb):
                            transpose_batch(tb[i])
                    for i in range(len(qk), len(tb)):
                        transpose_batch(tb[i])
                    for hh in range(2):
                        h = 2 * hp + hh
                        prow = slice(hh * 64, hh * 64 + 64)
                        nbase, dbase = (0, 64) if hh == 0 else (64, 0)
                        if h == 0:
                            load_bias(0)
                        steps = []
                        tbase = 0
                        for pi, (win_, d) in enumerate(PATTERNS):
                            L = S // d
                            nqt = L // 128
                            gsz = min(4, nqt)
                            for r_ in range(d):
                                for g0 in range(0, nqt, gsz):
                                    obi = 4 + (ocount % 2)
                                    ocount += 1
                                    for q0 in range(g0, g0 + gsz, 2):
                                        first = (q0 == 0)
                                        last = (q0 + 1 == nqt - 1)
                                        if d == 16:
                                            combo = 6
                                        else:
                                            combo = pi * 3 + (0 if first else (2 if last else 1))
                                        fill = (pi, q0 // 8) if d == 1 else (pi, r_ if d == 4 else r_ // 4)
                                        steps.append(dict(pi=pi, d=d, r=r_, g0=g0, gsz=gsz, q0=q0, nqt=nqt, tbase=tbase, obi=obi,
                                                          combo=combo, glast=(q0 + 2 >= g0 + gsz), fill=fill))
                            tbase += d * (nqt + 1)

                        def emit_S(st):
                            nonlocal pcount
                            sbi = 1 + (pcount % 3)
                            st["sbi"] = sbi
                            st["pti"] = pcount % 3
                            pcount += 1
                            sbk = banks[sbi]
                            sbn = "bank%d" % sbi
                            d, r_ = st["d"], st["r"]
                            op(PE, lambda: eng[PE].matmul(sbk[:], ident[:], bia[:, st["combo"], :], start=True, stop=False, skip_group_check=True),
                               r=["ident", "bia_p%d" % st["pi"]], w=[sbn], inc=False)
                            for qq in range(2):
                                qt = st["q0"] + qq
                                slot = st["slot"]
                                if d == 1:
                                    qo = 128 * (qt % 8)
                                elif d == 4:
                                    qo = 128 * qt
                                else:
                                    qo = (r_ % 4) * 256 + 128 * qt
                                qap = qS[slot][:, qo:qo + 128]
                                qres = "qS%d" % slot
                                for j in range(2):
                                    l0 = 128 * qt - 64 + 128 * j
                                    koff = PAD + l0 * d + r_
                                    kap = kT[:, koff:koff + 127 * d + 1:d]
                                    dst = sbk[:, (qq * 2 + j) * 128:(qq * 2 + j + 1) * 128]
                                    op(PE, lambda: eng[PE].matmul(dst, kap, qap, start=False, stop=True, skip_group_check=True),
                                       r=KT_ALL + [qres], w=[sbn], inc=(qq == 1 and j == 1))
                            pt = pT[st["pti"]]
                            op(ACT, lambda: eng[ACT].activation(out=pt[:], in_=sbk[:], func=AF.Exp), r=[sbn], w=["pT%d" % st["pti"]])

                        def emit_PV(st):
                            d, r_, g0, gsz = st["d"], st["r"], st["g0"], st["gsz"]
                            ob = banks[st["obi"]]
                            obn = "bank%d" % st["obi"]
                            pt = pT[st["pti"]]
                            ptn = "pT%d" % st["pti"]
                            for qq in range(2):
                                qt = st["q0"] + qq
                                for j in range(2):
                                    ti = st["tbase"] + r_ * (st["nqt"] + 1) + qt + j
                                    lhs = v3[:, ti, 0:2, :] if hh == 0 else v3[:, ti, 1:3, :]
                                    oq = (qt - g0)
                                    op(PE, lambda: eng[PE].matmul(ob[:, oq * 128:(oq + 1) * 128], lhs.rearrange("p a b -> p (a b)"),
                                                                   pt[:, (qq * 2 + j) * 128:(qq * 2 + j + 1) * 128], start=(j == 0), stop=(j == 1)),
                                       r=["v3_%d" % (ti // 8), ptn], w=[obn], inc=(qq == 1 and j == 1))
                            if st["glast"]:
                                a0 = 128 * g0 * d + r_
                                a1 = a0 + (128 * gsz - 1) * d + 1
                                aview = acc[:, a0:a1:d]
                                accr = ["acc_c%d" % cc for cc in range(a0 // CH, (a1 - 1) // CH + 1)]
                                if st["pi"] == 0:
                                    op(DVE, lambda: eng[DVE].tensor_copy(out=aview, in_=ob[:, 0:128 * gsz]), r=[obn], w=accr)
                                else:
                                    op(DVE, lambda: eng[DVE].tensor_tensor(out=aview, in0=aview, in1=ob[:, 0:128 * gsz], op=ALU.add), r=[obn] + accr, w=accr)

                        fills = []
                        for st in steps:
                            if st["fill"] is not None and (not fills or fills[-1] != st["fill"]):
                                fills.append(st["fill"])
                        fslot = {}

                        def emit_fill(f):
                            pi_, fi = f
                            d_ = PATTERNS[pi_][1]
                            slot = fillc[0] % 2
                            fillc[0] += 1
                            fslot[f] = slot
                            if d_ == 1:
                                src = qT[prow, fi * 1024:(fi + 1) * 1024]
                                dstq = qS[slot][prow, :]
                            elif d_ == 4:
                                src = qT[prow, fi:fi + 4 * 1023 + 1:4]
                                dstq = qS[slot][prow, :]
                            else:
                                src = qT[prow, :].rearrange("p (l r) -> p r l", r=16)[:, 4 * fi:4 * fi + 4, :]
                                dstq = qS[slot][prow, :].rearrange("p (r l) -> p r l", l=256)
                            op(DVE, lambda: eng[DVE].tensor_copy(out=dstq, in_=src), r=["qT_c%d" % cc for cc in range(NCH)], w=["qS%d" % slot])

                        orow = slice(64 - hh * 64, 128 - hh * 64)
                        for sl_ in range(2):
                            op(POOL, lambda: eng[POOL].memset(qS[sl_][orow, :], 0.0), w=["qS%d" % sl_])
                        nf = 0
                        if fills:
                            emit_fill(fills[0])
                            nf = 1
                        pend = []
                        curf = None
                        for st in steps:
                            if st["fill"] is not None and st["fill"] != curf:
                                curf = st["fill"]
                                if nf < len(fills):
                                    emit_fill(fills[nf])
                                    nf += 1
                            if st["fill"] is not None:
                                st["slot"] = fslot[st["fill"]]
                            emit_S(st)
                            pend.append(st)
                            if len(pend) > SKEW_A:
                                emit_PV(pend.pop(0))
                        while pend:
                            emit_PV(pend.pop(0))
                        if h + 1 < 8:
                            load_bias(h + 1)
                        nrow = slice(nbase, nbase + 64)
                        drow = slice(dbase, dbase + 64)
                        for c in range(NCH):
                            cs = slice(c * CH, (c + 1) * CH)
                            s = ncount % 2
                            ncount += 1
                            an = "acc_c%d" % c
                            op(ACT, lambda: eng[ACT].activation(out=acc[drow, cs], in_=acc[drow, cs], func=AF.Ln), r=[an], w=[an])
                            op(ACT, lambda: eng[ACT].activation(out=acc[drow, cs], in_=acc[drow, cs], func=AF.Exp, scale=-1.0), r=[an], w=[an])
                            op(DVE, lambda: eng[DVE].tensor_copy(out=rdenA[s][nrow, :], in_=acc[drow, cs]), r=[an], w=["rdenA0"])
                            op(DVE, lambda: eng[DVE].tensor_tensor(out=ostA[s][nrow, :], in0=acc[nrow, cs], in1=rdenA[s][nrow, :], op=ALU.mult),
                               r=[an, "rdenA0"], w=["ostA%d" % s])
                            dma(SP, "ostA%d" % s, mixT[h * 64:(h + 1) * 64, cs], ostA[s][nrow, :], r=["ostA%d" % s], w=["mixT%d" % h])
                T.barrier()

            cqn = botA[:].bitcast(BF16).rearrange("p (k t) -> p k t", k=2)
            ckvn = botB[:]
            kpe = botC[:, 0:S]
            with ExitStack() as es0:
                sq = [sb("sqb%d" % i, [128, CH], BF16, es0) for i in range(2)]
                rtmp = [sb("rtmpb%d" % i, [128, CH], F32, es0) for i in range(2)]
                rstd = [sb("rstdb%d" % i, [128, CH], F32, es0) for i in range(2)]
                lat = [sb("lat%d" % i, [128, 3, CH], F32, es0) for i in range(2)]
                kr1 = [sb("kr1_%d" % i, [128, CH], F32, es0) for i in range(2)]
                kr2 = [sb("kr2_%d" % i, [128, CH], F32, es0) for i in range(2)]
                csc = [sb("csc%d" % i, [128, 2, CH], F32, es0) for i in range(2)]

                def lat_proj(c):
                    cs = slice(c * CH, (c + 1) * CH)
                    s = c % 2
                    dma(SP, "csc%d" % s, csc[s][64:96, 0, :], cos_in[64:96, cs], w=["csc%d" % s])
                    dma(SP, "csc%d" % s, csc[s][64:96, 1, :], sin_in[64:96, cs], w=["csc%d" % s])
                    ur = ["uT%d_%d" % (kc, c) for kc in range(8)]
                    for i, col0 in enumerate((1536, 1664, 1792)):
                        b = banks[1 + i]
                        bn = "bank%d" % (1 + i)
                        for kc in range(8):
                            op(PE, lambda: eng[PE].matmul(b[:], win[:, kc, col0:col0 + 128], uT[:, kc, cs], start=(kc == 0), stop=(kc == 7)),
                               r=["win_lat"] + ur, w=[bn], inc=(kc == 7))
                        op(ACT, lambda: eng[ACT].copy(out=lat[s][:, i, :], in_=b[:]), r=[bn], w=["lat%d_%d" % (s, i)])
                    for kc in range(8):
                        op(PE, lambda: eng[PE].matmul(banks[4][0:96, :], win[:, kc, 1856:1952], uT[:, kc, cs], start=(kc == 0), stop=(kc == 7)),
                           r=["win_lat"] + ur, w=["bank4"], inc=(kc == 7))
                    for kc in range(8):
                        op(PE, lambda: eng[PE].matmul(banks[5][0:96, :], wkr_sw[:, kc, :], uT[:, kc, cs], start=(kc == 0), stop=(kc == 7)),
                           r=["wkr_sw"] + ur, w=["bank5"], inc=(kc == 7))
                    op(DVE, lambda: eng[DVE].tensor_tensor(out=kr1[s][64:96, :], in0=banks[4][64:96, :], in1=csc[s][64:96, 0, :], op=ALU.mult),
                       r=["bank4", "csc%d" % s], w=["kr1_%d" % s])
                    op(DVE, lambda: eng[DVE].tensor_tensor(out=kr2[s][64:96, :], in0=banks[5][64:96, :], in1=csc[s][64:96, 1, :], op=ALU.mult),
                       r=["bank5", "csc%d" % s], w=["kr2_%d" % s])
                    op(POOL, lambda: eng[POOL].tensor_tensor(out=kpe[64:96, cs], in0=kr1[s][64:96, :], in1=kr2[s][64:96, :], op=ALU.add),
                       r=["kr1_%d" % s, "kr2_%d" % s], w=["kpe_c%d" % c])

                def lat_norm(c):
                    cs = slice(c * CH, (c + 1) * CH)
                    s = c % 2
                    for i in range(3):
                        q = i % 2
                        op(ACT, lambda: eng[ACT].activation(out=sq[q][:], in_=lat[s][:, i, :], func=AF.Square), r=["lat%d_%d" % (s, i)], w=["sqb%d" % q])
                        bsel = banks[6] if i < 2 else banks[0]
                        bname = "bank6" if i < 2 else "bank0"
                        op(PE, lambda: eng[PE].matmul(bsel[:], ones[:], sq[q][:], start=(i != 1), stop=(i != 0)),
                           r=["ones", "sqb%d" % q], w=[bname])
                    for k2, (bsel, bname, nfeat, idxs) in enumerate(((banks[6], "bank6", 256, (0, 1)), (banks[0], "bank0", 128, (2,)))):
                        op(ACT, lambda: eng[ACT].activation(out=rtmp[k2][:], in_=bsel[:], func=AF.Ln, scale=1.0 / nfeat, bias=eps_t[:, 0:1]),
                           r=[bname, "eps"], w=["rtmpb%d" % k2])
                        op(ACT, lambda: eng[ACT].activation(out=rstd[k2][:], in_=rtmp[k2][:], func=AF.Exp, scale=-0.5), r=["rtmpb%d" % k2], w=["rstdb%d" % k2])
                        for i in idxs:
                            dstl = cqn[:, i, cs] if i < 2 else ckvn[:, cs]
                            gsc = gq[:, i:i + 1] if i < 2 else gkv[:, 0:1]
                            op(DVE, lambda: eng[DVE].scalar_tensor_tensor(out=dstl, in0=lat[s][:, i, :], scalar=gsc, in1=rstd[k2][:], op0=ALU.mult, op1=ALU.mult),
                               r=["lat%d_%d" % (s, i), "rstdb%d" % k2, "gq", "gkv"], w=[("cqn_c%d" if i < 2 else "ckvn_c%d") % c])

                for c in range(NCH):
                    lat_proj(c)
                    if c > 0:
                        lat_norm(c - 1)
                lat_norm(NCH - 1)
                T.barrier()
        if debug:
            with ExitStack() as es0:
                dst = sb("dstage2", [128, S], F32, es0)
                for i in range(2):
                    op(DVE, lambda: eng[DVE].tensor_copy(out=dst[:], in_=cqn[:, i, :]), r=["cqn_c%d" % cc for cc in range(NCH)], w=["dstage"])
                    dma(SP, "dbg", dbg["d_cq"][i * 128:(i + 1) * 128, :], dst[:], r=["dstage"])
                op(DVE, lambda: eng[DVE].tensor_copy(out=dst[:], in_=ckvn[:]), r=["ckvn_c%d" % cc for cc in range(NCH)], w=["dstage"])
                dma(SP, "dbg", dbg["d_ckv"][:, :], dst[:], r=["dstage"])
                op(DVE, lambda: eng[DVE].memset(dst[:], 0.0), w=["dstage"])
                op(DVE, lambda: eng[DVE].tensor_copy(out=dst[64:96, :], in_=kpe[64:96, :]), r=["kpe_c%d" % cc for cc in range(NCH)], w=["dstage"])
                dma(SP, "dbg", dbg["d_kpe"][:, :], dst[:], r=["dstage"])
                T.barrier()

        NWA = 4
        wo = sb("wo", [128, 8, D], BF16)
        wuA = sb("wuA", [128, 8, NWA * 512], BF16)
        wd_v = w_down.rearrange("(j p) o -> p j o", p=128)
        wu_v = w_up.rearrange("(k p) f -> p k f", p=128)
        with ExitStack() as eB:
            csB = [sb("csB%d" % i, [128, 2, CH], F32, eB) for i in range(2)]
            wq = sb("wq", [128, 2, 768], BF16, eB)
            wqs = sb("wqs", [128, 2, 768], BF16, eB)
            wkv = sb("wkv", [128, 2, 8, 64], BF16, eB)
            vall = sb("vall", [128, 32, 8, 128], BF16, eB)
            qh = [sb("qh%d" % i, [128, S], BF16, eB) for i in range(2)]
            kh = [sb("kh%d" % i, [128, S], BF16, eB) for i in range(2)]
            stq = qh[1][:, :].bitcast(F32)
            stk = kh[1][:, :].bitcast(F32)
            for kc in range(2):
                dma(SP, "c_wq", stq[:, kc * 768:(kc + 1) * 768], w_qb[kc * 128:(kc + 1) * 128, :], w=["qh1"])
            dma(SP, "c_wkv", stk[:, 0:1024], w_kvb, w=["kh1"])
            for kc in range(8):
                dma(POOL, "c_wo", wo[:, kc, :], w_out[kc * 128:(kc + 1) * 128, :], w=["wo"])
            for b in range(NWA):
                dma(POOL, "c_wu%d" % b, wuA[:, :, b * 512:(b + 1) * 512], wu_v[:, :, b * 512:(b + 1) * 512], w=["wu%d" % b])
            op(DVE, lambda: eng[DVE].memset(wqs[:], 0.0), w=["wqs"])
            op(DVE, lambda: eng[DVE].tensor_copy(out=wq[:].rearrange("p k c -> p (k c)"), in_=stq[:, 0:1536]), r=["qh1"], w=["wq"])
            op(DVE, lambda: eng[DVE].tensor_copy(out=wkv[:], in_=stk[:, 0:1024].rearrange("p (h t e) -> p t h e", t=2, e=64)), r=["kh1"], w=["wkv"])
            wq_v = wq[:].rearrange("p k (h e) -> p k h e", e=96)
            wqs_v = wqs[:].rearrange("p k (h e) -> p k h e", e=96)
            for kc in range(2):
                op(DVE, lambda: eng[DVE].tensor_scalar(out=wqs_v[:, kc, :, 64:80], in0=wq_v[:, kc, :, 80:96], scalar1=-1.0, scalar2=None, op0=ALU.mult),
                   r=["wq"], w=["wqs"])
                op(DVE, lambda: eng[DVE].tensor_copy(out=wqs_v[:, kc, :, 80:96], in_=wq_v[:, kc, :, 64:80]), r=["wq"], w=["wqs"])

            def build_vall(t):
                bi = 1 + t % 2
                op(PE, lambda: eng[PE].matmul(banks[bi][:], ckvn[:, t * 128:(t + 1) * 128], wkv[:, 1, :, :].rearrange("p h e -> p (h e)"), start=True, stop=True),
                   r=["ckvn_c%d" % (t // 4), "wkv"], w=["bank%d" % bi])
                srcv = banks[bi][:].rearrange("p (h e) -> p h e", e=64)
                op(DVE, lambda: eng[DVE].memset(vall[:, t, :, 64:128], 1.0), w=["vall1_%d" % t])
                op(ACT, lambda: eng[ACT].copy(out=vall[:, t, :, 0:64], in_=srcv), r=["bank%d" % bi], w=["vall_%d" % t])

            t1 = sb("t1", [128, CH], F32, eB)
            t2 = sb("t2", [128, CH], F32, eB)
            pB = [sb("pB%d" % i, [128, 512], BF16, eB) for i in range(3)]
            ostB = [sb("ostB%d" % i, [128, CH], BF16, eB) for i in range(2)]
            rdB = sb("rdB", [128, CH], F32, eB)
            scale_b = 96.0 ** -0.5
            print("phase B sbuf free:", nc.sbuf_bytes_remaining)
            pc = 0
            ncs = [0, 0]

            def emit_proj(h, c, pro=False):
                nonlocal pc
                s_ = h % 2
                qt_, kt_ = qh[s_], kh[s_]
                qn, kn = "qh%d" % s_, "kh%d" % s_
                cs = slice(c * CH, (c + 1) * CH)
                for kc in range(2):
                    op(PE, lambda: eng[PE].matmul(banks[4][0:96, :], wq[:, kc, h * 96:(h + 1) * 96], cqn[:, kc, cs], start=(kc == 0), stop=(kc == 1)),
                       r=["wq", "cqn_c%d" % c], w=["bank4"], inc=(kc == 1))
                for kc in range(2):
                    op(PE, lambda: eng[PE].matmul(banks[5][0:96, :], wqs[:, kc, h * 96:(h + 1) * 96], cqn[:, kc, cs], start=(kc == 0), stop=(kc == 1)),
                       r=["wqs", "cqn_c%d" % c], w=["bank5"], inc=(kc == 1))
                op(PE, lambda: eng[PE].matmul(bank7[0:64, :], wkv[:, 0, h, :], ckvn[:, cs], start=True, stop=True),
                   r=["wkv", "ckvn_c%d" % c], w=["bankT"])
                sl = ncs[0] % 2
                ncs[0] += 1
                dma(SP, "csB%d" % sl, csB[sl][64:96, 0, :], cos_in[64:96, cs], w=["csB%d" % sl])
                dma(SP, "csB%d" % sl, csB[sl][64:96, 1, :], sin_in[64:96, cs], w=["csB%d" % sl])
                if pro:
                    op(ACT, lambda: eng[ACT].copy(out=qt_[0:64, cs], in_=banks[4][0:64, :]), r=["bank4"], w=[qn])
                else:
                    op(DVE, lambda: eng[DVE].tensor_copy(out=qt_[0:64, cs], in_=banks[4][0:64, :]), r=["bank4"], w=[qn])
                op(DVE, lambda: eng[DVE].tensor_tensor(out=t1[64:96, :], in0=banks[4][64:96, :], in1=csB[sl][64:96, 0, :], op=ALU.mult),
                   r=["bank4", "csB%d" % sl], w=["t1"])
                op(DVE, lambda: eng[DVE].tensor_tensor(out=t2[64:96, :], in0=banks[5][64:96, :], in1=csB[sl][64:96, 1, :], op=ALU.mult),
                   r=["bank5", "csB%d" % sl], w=["t2"])
                op(DVE, lambda: eng[DVE].tensor_tensor(out=qt_[64:96, cs], in0=t1[64:96, :], in1=t2[64:96, :], op=ALU.add),
                   r=["t1", "t2"], w=[qn])
                if pro:
                    op(ACT, lambda: eng[ACT].copy(out=kt_[0:64, cs], in_=bank7[0:64, :]), r=["bankT"], w=[kn])
                else:
                    op(DVE, lambda: eng[DVE].tensor_copy(out=kt_[0:64, cs], in_=bank7[0:64, :]), r=["bankT"], w=[kn])
                if c == NCH - 1:
                    op(DVE, lambda: eng[DVE].tensor_copy(out=kt_[64:96, :], in_=kpe[64:96, :]), r=["kpe_c%d" % cc for cc in range(NCH)], w=[kn])

            def emit_QK(st):
                nonlocal pc
                h, c, kt = st["h"], st["c"], st["kt"]
                s_ = h % 2
                cs = slice(c * CH, (c + 1) * CH)
                sbi = 1 + (pc % 3)
                pc += 1
                st["pti"] = ncs[1] % 3
                ncs[1] += 1
                op(PE, lambda: eng[PE].matmul(banks[sbi][:], kh[s_][0:96, kt * 128:(kt + 1) * 128], qh[s_][0:96, cs], start=True, stop=True),
                   r=["kh%d" % s_, "qh%d" % s_], w=["bank%d" % sbi])
                pt = pB[st["pti"]]
                op(ACT, lambda: eng[ACT].activation(out=pt[:], in_=banks[sbi][:], func=AF.Exp, scale=scale_b), r=["bank%d" % sbi], w=["pB%d" % st["pti"]])

            def emit_PVB(st):
                h, c, kt = st["h"], st["c"], st["kt"]
                cs = slice(c * CH, (c + 1) * CH)
                obi = 0 if c % 2 == 0 else 6
                ob = banks[obi]
                obn = "bank%d" % obi
                pt = pB[st["pti"]]
                op(PE, lambda: eng[PE].matmul(ob[:], vall[:, kt, h, :], pt[:], start=(kt == 0), stop=(kt == 31)),
                   r=["vall_%d" % kt, "vall1_%d" % kt, "pB%d" % st["pti"]], w=[obn], inc=(kt == 31))
                if kt == 31:
                    so = c % 2
                    op(DVE, lambda: eng[DVE].reciprocal(out=rdB[0:64, :], in_=ob[64:128, :]), r=[obn], w=["rdB"])
                    op(DVE, lambda: eng[DVE].tensor_tensor(out=ostB[so][0:64, :], in0=ob[0:64, :], in1=rdB[0:64, :], op=ALU.mult),
                       r=[obn, "rdB"], w=["ostB%d" % so])
                    dma(SP, "ostB%d" % so, mixT[512 + h * 64:512 + (h + 1) * 64, cs], ostB[so][0:64, :], r=["ostB%d" % so], w=["mixT%d" % (8 + h)])

            for c in range(NCH):
                emit_proj(0, c, pro=True)
                for t in range(4 * c, 4 * c + 4):
                    build_vall(t)
            pend = []
            for h in range(8):
                for c in range(NCH):
                    if h + 1 < 8:
                        emit_proj(h + 1, c)
                    for kt in range(32):
                        st = dict(h=h, c=c, kt=kt)
                        emit_QK(st)
                        pend.append(st)
                        if len(pend) > SKEW:
                            emit_PVB(pend.pop(0))
            while pend:
                emit_PVB(pend.pop(0))
            T.barrier()

        mix_all = ["mixT%d" % i for i in range(16)]
        if debug:
            with ExitStack() as eD:
                dm = sb("dm", [128, S], BF16, eD)
                dmf = sb("dmf", [128, S], F32, eD)
                for kc in range(8):
                    dma(SP, "dm", dm[:], mixT[kc * 128:(kc + 1) * 128, :], r=mix_all, w=["dm"])
                    op(DVE, lambda: eng[DVE].tensor_copy(out=dmf[:], in_=dm[:]), r=["dm"], w=["dmf"])
                    dma(SP, "dmo", dbg["d_mix"][kc * 128:(kc + 1) * 128, :], dmf[:], r=["dmf"])
                T.barrier()

        with ExitStack() as eC:
            wuB = sb("wuB", [128, 8, (8 - NWA) * 512], BF16, eC)
            wd = sb("wd", [128, 32, D], BF16, eC)
            gm = sb("gm", [128, 8], F32, eC)
            gf = sb("gf", [128, 8], F32, eC)
            dma(SP, "c_gm", gm[:], g_mlp, w=["gm"])
            dma(SP, "c_gf", gf[:], g_fin, w=["gf"])
            def load_wd(j4, after=()):
                dma(POOL, "c_wd%d" % j4, wd[:, 4 * j4:4 * j4 + 4, :], wd_v[:, 4 * j4:4 * j4 + 4, :], r=list(after), w=["wd%d" % j4])

            def load_rest_weights(after):
                load_wd(0, after)
                load_wd(1)
                for b in range(NWA, 8):
                    dma(POOL, "c_wu%d" % b, wuB[:, :, (b - NWA) * 512:(b - NWA + 1) * 512], wu_v[:, :, b * 512:(b + 1) * 512], w=["wu%d" % b])
                for j4 in range(2, 8):
                    load_wd(j4)

            def wu_cols(kc, j):
                if j // 4 < NWA:
                    return wuA[:, kc, j * 128:(j + 1) * 128]
                return wuB[:, kc, (j - NWA * 4) * 128:(j - NWA * 4 + 1) * 128]

            def wd_rows(j, oc):
                return wd[:, j, oc * 128:(oc + 1) * 128]

            hxs = [botA[:].rearrange("p (k t) -> p k t", k=8), sb("hxB", [128, 8, CH], F32, eC)[:]]
            mu = botB[:].rearrange("p (k t) -> p k t", k=8)
            act = botC[:, 0:S].rearrange("p (k t) -> p k t", k=8)
            sqc = [botC[:, S + i * CH:S + (i + 1) * CH] for i in range(2)]
            rl = [sb("rl%d" % i, [128, CH], F32, eC) for i in range(2)]
            rt = sb("rtC", [128, CH], F32, eC)
            rs = sb("rsC", [128, CH], F32, eC)
            bc = 0
            print("phase C sbuf free:", nc.sbuf_bytes_remaining)

            def hn(c, k):
                return "hx%d_%d" % (c % 2, k)

            rt2 = rt
            rs2 = sb("rsC2", [128, CH], F32, eC)
            sqn = [0]

            def stat_part(c, kc):
                hx = hxs[c % 2]
                q = sqn[0] % 2
                sqn[0] += 1
                op(ACT, lambda: eng[ACT].activation(out=sqc[q], in_=hx[:, kc, :], func=AF.Square), r=[hn(c, kc)], w=["sqc%d" % q])
                op(PE, lambda: eng[PE].matmul(banks[0][:], ones[:], sqc[q], start=(kc == 0), stop=(kc == 7)),
                   r=["ones", "sqc%d" % q], w=["bank0"])

            def stat_fin(which):
                rt_, rs_, nm = (rt, rs, "") if which == 1 else (rt2, rs2, "2")
                op(ACT, lambda: eng[ACT].activation(out=rt_[:], in_=banks[0][:], func=AF.Ln, scale=1.0 / D, bias=eps_t[:, 0:1]),
                   r=["bank0", "eps"], w=["rtC"])
                op(ACT, lambda: eng[ACT].activation(out=rs_[:], in_=rt_[:], func=AF.Exp, scale=-0.5), r=["rtC"], w=["rsC" + nm])

            def load_x(c):
                cs_ = slice(c * CH, (c + 1) * CH)
                for oc in range(8):
                    dma(SP, "hxl%d_%d" % (c % 2, oc), hxs[c % 2][:, oc, :], xT_v[:, oc, cs_], w=[hn(c, oc)])

            def load_mc(c):
                cs_ = slice(c * CH, (c + 1) * CH)
                dma(SP, "mc", mu, mixT_v[:, :, cs_], r=mix_all, w=["mu%d" % k for k in range(8)])

            UPB = [(banks[1][:], "bank1"), (banks[2][:], "bank2"), (banks[3][:], "bank3"), (bank7, "bankT")]
            DNB = [(banks[4][:], "bank4"), (banks[5][:], "bank5"), (banks[6][:], "bank6")]
            dc = 0

            def out_proj(c):
                nonlocal bc
                hx = hxs[c % 2]
                for oc in range(8):
                    bap, bnm = UPB[bc % 4]; bc += 1
                    for kc in range(8):
                        op(PE, lambda: eng[PE].matmul(bap, wo[:, kc, oc * 128:(oc + 1) * 128], mu[:, kc, :], start=(kc == 0), stop=(kc == 7)),
                           r=["wo", "mu%d" % kc], w=[bnm], inc=(kc == 7))
                    op(DVE, lambda: eng[DVE].tensor_tensor(out=hx[:, oc, :], in0=hx[:, oc, :], in1=bap, op=ALU.add),
                       r=[bnm, hn(c, oc)], w=[hn(c, oc)])
                    if oc >= 1:
                        stat_part(c, oc - 1)
                stat_part(c, 7)
                stat_fin(1)

            def final_norm(c):
                cs_ = slice(c * CH, (c + 1) * CH)
                hx = hxs[c % 2]
                for kc in range(8):
                    op(DVE, lambda: eng[DVE].scalar_tensor_tensor(out=hx[:, kc, :], in0=hx[:, kc, :], scalar=gf[:, kc:kc + 1], in1=rs2[:], op0=ALU.mult, op1=ALU.mult),
                       r=[hn(c, kc), "rsC2", "gf"], w=[hn(c, kc)])
                    dma(SP, "outst%d_%d" % (c % 2, kc), outT_v[:, kc, cs_], hx[:, kc, :], r=[hn(c, kc)], w=["out"])

            load_mc(0)
            load_x(0)
            load_rest_weights([hn(0, k) for k in range(8)] + ["mu%d" % k for k in range(8)])
            out_proj(0)
            load_x(1)
            for c in range(NCH):
                hx = hxs[c % 2]
                for kc in range(8):
                    op(DVE, lambda: eng[DVE].scalar_tensor_tensor(out=mu[:, kc, :], in0=hx[:, kc, :], scalar=gm[:, kc:kc + 1], in1=rs[:], op0=ALU.mult, op1=ALU.mult),
                       r=[hn(c, kc), "rsC", "gm"], w=["mu%d" % kc])
                for qf in range(4):
                    for jj in range(8):
                        j = qf * 8 + jj
                        bap, bnm = UPB[bc % 4]; bc += 1
                        for kc in range(8):
                            op(PE, lambda: eng[PE].matmul(bap, wu_cols(kc, j), mu[:, kc, :], start=(kc == 0), stop=(kc == 7)),
                               r=["wu%d" % (j // 4), "mu%d" % kc], w=[bnm], inc=(kc == 7))
                        q = j % 2
                        op(DVE, lambda: eng[DVE].tensor_scalar(out=rl[q][:], in0=bap, scalar1=0.0, scalar2=None, op0=ALU.max),
                           r=[bnm], w=["rl%d" % q])
                        op(ACT, lambda: eng[ACT].activation(out=act[:, jj, :], in_=rl[q][:], func=AF.Square), r=["rl%d" % q], w=["act%d" % jj])
                    if qf == 3 and c + 1 < NCH:
                        load_mc(c + 1)
                    for oc in range(8):
                        bap, bnm = DNB[dc % 3]; dc += 1
                        for jj in range(8):
                            j = qf * 8 + jj
                            op(PE, lambda: eng[PE].matmul(bap, wd_rows(j, oc), act[:, jj, :], start=(jj == 0), stop=(jj == 7)),
                               r=["wd%d" % (j // 4), "act%d" % jj], w=[bnm], inc=(jj == 7))
                        op(DVE, lambda: eng[DVE].tensor_tensor(out=hx[:, oc, :], in0=hx[:, oc, :], in1=bap, op=ALU.add),
                           r=[bnm, hn(c, oc)], w=[hn(c, oc)])
                        if qf == 3 and oc >= 1:
                            stat_part(c, oc - 1)
                stat_part(c, 7)
                stat_fin(2)
                if c + 1 < NCH:
                    out_proj(c + 1)
                final_norm(c)
                if c + 2 < NCH:
                    load_x(c + 2)

        T.barrier()
        print("instructions:", dict(T.cnt), "waits:", T.nwaits)
    return nc


_CACHE = {}


def _host_constants(rel_bias):
    rb = np.asarray(rel_bias, np.float32)
    i = np.arange(128)[:, None]
    m = np.arange(128)[None, :]
    out = np.full((8, 128, 7, 4, 128), NEG, np.float32)

    def tile(d, j, edge, h):
        rel = -64 + 128 * j + i - m
        v = np.abs(rel) <= 64
        if edge and j == 0:
            v = v & (i >= 64)
        if edge and j == 1:
            v = v & (i < 64)
        g = rb[t5_buckets(rel * d), h]
        return np.where(v, g, np.float32(NEG))

    for h in range(8):
        for pi, (win_, d) in enumerate(PATTERNS):
            if d == 16:
                combos = [(6, True, True)]
            else:
                combos = [(pi * 3 + 0, True, False), (pi * 3 + 1, False, False), (pi * 3 + 2, False, True)]
            for (ci, first, last) in combos:
                out[h, :, ci, 0, :] = tile(d, 0, first, h)
                out[h, :, ci, 1, :] = tile(d, 1, False, h)
                out[h, :, ci, 2, :] = tile(d, 0, False, h)
                out[h, :, ci, 3, :] = tile(d, 1, last, h)
    return out.reshape(8, 128, 7 * 512)


def _rope_tables():
    inv_freq = (np.float32(10000.0) ** (-np.arange(0, 32, 2, dtype=np.float32) / np.float32(32))).astype(np.float32)
    pos = np.arange(S, dtype=np.float32)
    fr = (pos[:, None] * inv_freq[None, :]).astype(np.float32)
    cos = np.cos(fr).astype(np.float32).T
    sin = np.sin(fr).astype(np.float32).T
    cT = np.ones((128, S), np.float32)
    sT = np.zeros((128, S), np.float32)
    cT[64:80] = cos; cT[80:96] = cos
    sT[64:80] = sin; sT[80:96] = sin
    return cT, sT


def _lay(v, k):
    return np.ascontiguousarray(np.asarray(v, np.float32).reshape(k, 128).T)


def kernel(x, mix_norm_g, w_in, q_norm_g, w_q_b, kv_norm_g, w_kv_b, w_out, mlp_norm_g, w_up, w_down,
           rel_bias, final_norm_g, _debug=False, _cores=8):
    x = np.asarray(x, np.float32)
    key = ("nc", _debug)
    if key not in _CACHE:
        _CACHE[key] = build_program(debug=_debug)
    nc = _CACHE[key]
    cT, sT = _rope_tables()
    shared = {
        "w_in": np.ascontiguousarray(np.asarray(w_in, np.float32)[0]),
        "g_mix": _lay(mix_norm_g, 8),
        "w_qb": np.ascontiguousarray(np.asarray(w_q_b, np.float32)[0]),
        "g_q": _lay(q_norm_g, 2),
        "w_kvb": np.ascontiguousarray(np.asarray(w_kv_b, np.float32)[0]),
        "g_kv": _lay(kv_norm_g, 1),
        "w_out": np.ascontiguousarray(np.asarray(w_out, np.float32)[0]),
        "g_mlp": _lay(mlp_norm_g, 8),
        "w_up": np.ascontiguousarray(np.asarray(w_up, np.float32)[0]),
        "w_down": np.ascontiguousarray(np.asarray(w_down, np.float32)[0]),
        "g_fin": _lay(final_norm_g, 8),
        "biasT": _host_constants(rel_bias),
        "ident": np.eye(128, dtype=np.float32),
        "cosT": cT,
        "sinT": sT,
    }
    in_maps = []
    for b in range(_cores):
        m = dict(shared)
        m["xT"] = np.ascontiguousarray(x[b].T)
        in_maps.append(m)
    res = run_bass_kernel_spmd(nc, in_maps, core_ids=list(range(_cores)))
    if _debug:
        return res.results
    out = np.stack([np.ascontiguousarray(r["outT"].T) for r in res.results], axis=0)
    return out.astype(np.float32)
```

```python
import numpy as np
from contextlib import ExitStack
import concourse.bass as bass
import concourse.mybir as mybir
from concourse.bass_utils import run_bass_kernel_spmd

F32 = mybir.dt.float32
BF16 = mybir.dt.bfloat16
ALU = mybir.AluOpType
AF = mybir.ActivationFunctionType

S = 4096
D = 1024
NCH = 8
CH = 512
INC = 1952
PAD = 1024
NEG = -30000.0
EPS = 1e-6
PATTERNS = ((128, 1), (512, 4), (2048, 16))
SKEW = 2
SKEW_A = 2
DFF = 4096


class Tracker:
    def __init__(self, nc, es):
        self.nc = nc
        self.es = es
        self.engs = {"pe": nc.tensor, "act": nc.scalar, "dve": nc.vector, "pool": nc.gpsimd, "sp": nc.sync}
        self.sems = {}
        self.cnt = {}
        for k in ("pe", "act", "dve", "pool"):
            self.sems[k] = es.enter_context(nc.semaphore("sem_" + k))
            self.cnt[k] = 0
        self.seen = {e: {} for e in self.engs}
        self.lastw = {}
        self.readers = {}
        self.nwaits = 0

    def chan(self, name):
        if name not in self.sems:
            self.sems[name] = self.es.enter_context(self.nc.semaphore("dma_" + name))
            self.cnt[name] = 0
        return name

    def _deps(self, eng, r, w):
        deps = {}

        def add(key, val, raw):
            if key == eng and not raw and eng in ("pe", "sp"):
                return
            if deps.get(key, 0) < val:
                deps[key] = val

        for res in r:
            lw = self.lastw.get(res)
            if lw is not None:
                add(lw[0], lw[1], True)
        for res in w:
            lw = self.lastw.get(res)
            if lw is not None:
                add(lw[0], lw[1], False)
            for k, v in self.readers.get(res, {}).items():
                add(k, v, False)
        return deps

    def _emit_waits(self, eng, deps):
        seen = self.seen[eng]
        for key, val in deps.items():
            if key not in ("pe", "act", "dve", "pool"):
                val = self.cnt[key]
            if seen.get(key, 0) < val:
                self.engs[eng].wait_ge(self.sems[key], val)
                seen[key] = val
                self.nwaits += 1

    def _update(self, key, seq, r, w):
        for res in w:
            self.lastw[res] = (key, seq)
            self.readers[res] = {}
        for res in r:
            d = self.readers.setdefault(res, {})
            if d.get(key, 0) < seq:
                d[key] = seq

    def op(self, eng, fn, r=(), w=(), inc=True):
        self._emit_waits(eng, self._deps(eng, r, w))
        ins = fn()
        if inc:
            self.cnt[eng] += 1
            ins.then_inc(self.sems[eng], 1)
            seq = self.cnt[eng]
        else:
            seq = self.cnt[eng] + 1
        self._update(eng, seq, r, w)
        return ins

    def dma(self, q, ch, out, in_, r=(), w=()):
        self.chan(ch)
        if q == "pool":
            self.swq = getattr(self, "swq", [])
            while len(self.swq) >= 3:
                k, v = self.swq.pop(0)
                if self.seen[q].get(k, 0) < v:
                    self.engs[q].wait_ge(self.sems[k], v)
                    self.seen[q][k] = v
            self.swq.append((ch, self.cnt[ch] + 16))
        self._emit_waits(q, self._deps(q, r, w))
        self.engs[q].dma_start(out=out, in_=in_).then_inc(self.sems[ch], 16)
        self.cnt[ch] += 16
        self._update(ch, self.cnt[ch], r, w)

    def barrier(self):
        keys = list(self.cnt.keys())
        for e in self.engs:
            self.wait_all(e, [k for k in keys if k != "sp"])

    def wait_all(self, eng, keys):
        for k in keys:
            if self.cnt[k] > 0 and self.seen[eng].get(k, 0) < self.cnt[k]:
                self.engs[eng].wait_ge(self.sems[k], self.cnt[k])
                self.seen[eng][k] = self.cnt[k]


def t5_buckets(rel):
    nb = 16
    max_exact = 8
    ret = (rel > 0).astype(np.int32) * nb
    n = np.abs(rel)
    large = max_exact + (np.log(np.maximum(n, 1) / max_exact) / np.log(1024 / max_exact) * (nb - max_exact)).astype(np.int32)
    large = np.minimum(large, nb - 1)
    return (ret + np.where(n < max_exact, n, large)).astype(np.int32)


def build_program(debug=False):
    nc = bass.Bass("TRN2", target_bir_lowering=False)

    def din(name, shape, dt=F32):
        return nc.dram_tensor(name, list(shape), dt, kind="ExternalInput").ap()

    xT = din("xT", [D, S])
    w_in = din("w_in", [D, INC])
    g_mix = din("g_mix", [128, 8])
    w_qb = din("w_qb", [256, 768])
    g_q = din("g_q", [128, 2])
    w_kvb = din("w_kvb", [128, 1024])
    g_kv = din("g_kv", [128, 1])
    w_out = din("w_out", [D, D])
    g_mlp = din("g_mlp", [128, 8])
    w_up = din("w_up", [D, DFF])
    w_down = din("w_down", [DFF, D])
    g_fin = din("g_fin", [128, 8])
    biasT = din("biasT", [8, 128, 7 * 512])
    ident_in = din("ident", [128, 128])
    cos_in = din("cosT", [128, S])
    sin_in = din("sinT", [128, S])
    outT = nc.dram_tensor("outT", [D, S], F32, kind="ExternalOutput").ap()
    mixT = nc.dram_tensor("mixT_scratch", [D, S], BF16).ap()
    dbg = {}
    if debug:
        for nm, shp in (("d_uT", [D, S]), ("d_cq", [256, S]), ("d_ckv", [128, S]), ("d_kpe", [128, S]),
                        ("d_mix", [D, S])):
            dbg[nm] = nc.dram_tensor(nm, shp, F32, kind="ExternalOutput").ap()

    with ExitStack() as es:
        T = Tracker(nc, es)
        op, dma = T.op, T.dma
        PE, ACT, DVE, POOL, SP = "pe", "act", "dve", "pool", "sp"
        eng = T.engs

        def sb(name, shape, dt, stack=es):
            return stack.enter_context(nc.sbuf_tensor("sb_" + name, list(shape), dt))

        banks = [es.enter_context(nc.psum_tensor("bank%d" % i, [128, 512], F32)) for i in range(7)]
        bankT = es.enter_context(nc.psum_tensor("bankT", [128, 1024], BF16))
        bank7 = bankT[:].bitcast(F32)

        ident = sb("ident", [128, 128], BF16)
        ones = sb("ones", [128, 128], BF16)
        dma(POOL, "c_id", ident[:], ident_in, w=["ident"])
        op(POOL, lambda: eng[POOL].memset(ones[:], 1.0), w=["ones"])

        gq = sb("gq", [128, 2], F32)
        gkv = sb("gkv", [128, 1], F32)
        dma(SP, "c_gq", gq[:], g_q, w=["gq"])
        dma(SP, "c_gkv", gkv[:], g_kv, w=["gkv"])
        eps_t = sb("eps_t", [128, 1], F32)
        op(POOL, lambda: eng[POOL].memset(eps_t[:], EPS), w=["eps"])
        xT_v = xT.rearrange("(k p) t -> p k t", p=128)
        mixT_v = mixT.rearrange("(k p) t -> p k t", p=128)
        outT_v = outT.rearrange("(k p) t -> p k t", p=128)

        botA = sb("botA", [128, S], F32)
        botB = sb("botB", [128, S], BF16)
        botC = sb("botC", [128, S + 2 * PAD], BF16)
        with ExitStack() as esA:
            uT = sb("uT", [128, 8, S], BF16, esA)
            win = sb("win", [128, 8, INC], BF16, esA)
            wkr_sw = sb("wkr_sw", [128, 8, 96], BF16, esA)
            gmix = sb("gmix", [128, 8], F32, esA)
            dma(SP, "c_gmix", gmix[:], g_mix, w=["gmix"])
            op(POOL, lambda: eng[POOL].memset(wkr_sw[:], 0.0), w=["wkr_sw"])

            for kc in range(8):
                dma(POOL, "c_win", win[:, kc, :], w_in[kc * 128:(kc + 1) * 128, :], w=["win"])
            for kc in range(8):
                op(POOL, lambda: eng[POOL].tensor_scalar(out=wkr_sw[:, kc, 64:80], in0=win[:, kc, 1936:1952], scalar1=-1.0, scalar2=None, op0=ALU.mult),
                   r=["win"], w=["wkr_sw"])
                op(POOL, lambda: eng[POOL].tensor_copy(out=wkr_sw[:, kc, 80:96], in_=win[:, kc, 1920:1936]), r=["win"], w=["wkr_sw"])
            with ExitStack() as es0:
                xs = [sb("xs%d" % i, [128, 8, CH], F32, es0) for i in range(2)]
                sq = [sb("sq%d" % i, [128, CH], BF16, es0) for i in range(2)]
                rtmp = sb("rtmp", [128, CH], F32, es0)
                rstd = sb("rstd", [128, CH], F32, es0)
                for c in range(NCH):
                    s = c % 2
                    cs = slice(c * CH, (c + 1) * CH)
                    xres = "xs%d" % s
                    dma(SP, xres, xs[s][:], xT_v[:, :, cs], w=[xres])
                    for kc in range(8):
                        q = kc % 2
                        op(ACT, lambda: eng[ACT].activation(out=sq[q][:], in_=xs[s][:, kc, :], func=AF.Square), r=[xres], w=["sq%d" % q])
                        op(PE, lambda: eng[PE].matmul(banks[0][:], ones[:], sq[q][:], start=(kc == 0), stop=(kc == 7)),
                           r=["ones", "sq%d" % q], w=["bank0"])
                    op(ACT, lambda: eng[ACT].activation(out=rtmp[:], in_=banks[0][:], func=AF.Ln, scale=1.0 / D, bias=eps_t[:, 0:1]),
                       r=["bank0", "eps"], w=["rtmp"])
                    op(ACT, lambda: eng[ACT].activation(out=rstd[:], in_=rtmp[:], func=AF.Exp, scale=-0.5), r=["rtmp"], w=["rstd"])
                    for kc in range(8):
                        op(DVE, lambda: eng[DVE].scalar_tensor_tensor(out=uT[:, kc, cs], in0=xs[s][:, kc, :], scalar=gmix[:, kc:kc + 1], in1=rstd[:], op0=ALU.mult, op1=ALU.mult),
                           r=[xres, "rstd", "gmix"], w=["uT%d_%d" % (kc, c)])
                if debug:
                    dst = sb("dstage", [128, S], F32, es0)
                    for kc in range(8):
                        op(DVE, lambda: eng[DVE].tensor_copy(out=dst[:], in_=uT[:, kc, :]), r=["uT%d_%d" % (kc, c) for c in range(NCH)], w=["dstage"])
                        dma(SP, "dbg", dbg["d_uT"][kc * 128:(kc + 1) * 128, :], dst[:], r=["dstage"])
                T.barrier()

            with ExitStack() as e1:
                qT = botB
                kT = sb("kT", [128, S + 2 * PAD], BF16, e1)
                vT = botC
                NT3 = 33 + 36 + 48
                v3 = sb("v3", [128, NT3, 3, 64], BF16, e1)
                acc = botA
                bia = sb("bia", [128, 7, 512], BF16, e1)

                def load_bias(hb):
                    for (c0, c1, pn) in ((0, 3, 0), (3, 6, 1), (6, 7, 2)):
                        dma(POOL, "bia_p%d" % pn, bia[:, c0:c1, :], biasT[hb][:, c0 * 512:c1 * 512].rearrange("p (a b) -> p a b", b=512), w=["bia_p%d" % pn])
                pT = [sb("pT%d" % i, [128, 512], BF16, e1) for i in range(3)]
                qS = [sb("qS%d" % i, [128, 1024], BF16, e1) for i in range(2)]
                fillc = [0]
                ostA = [sb("ostA%d" % i, [128, CH], BF16, e1) for i in range(2)]
                rdenA = [sb("rdenA0", [128, CH], F32, e1)] * 2
                op(POOL, lambda: eng[POOL].memset(kT[:, 0:PAD], 0.0), w=["kT_pad"])
                op(POOL, lambda: eng[POOL].memset(kT[:, PAD + S:], 0.0), w=["kT_pad2"])
                op(POOL, lambda: eng[POOL].memset(vT[:, 0:PAD], 0.0), w=["vT_pad"])
                op(POOL, lambda: eng[POOL].memset(vT[:, PAD + S:], 0.0), w=["vT_pad2"])
                op(POOL, lambda: eng[POOL].memset(v3[:, :, 1, :], 1.0), w=["v3_%d" % bb for bb in range(15)])
                print("phase A sbuf free:", nc.sbuf_bytes_remaining)
                KT_ALL = ["kT_c%d" % cc for cc in range(NCH)] + ["kT_pad", "kT_pad2"]
                tiles = []
                for (win_, d) in PATTERNS:
                    L = S // d
                    for r_ in range(d):
                        for jp in range(L // 128 + 1):
                            tiles.append((PAD + (-64 + 128 * jp) * d + r_, d))
                assert len(tiles) == NT3
                deferred = []
                ncnt = [0]
                ev = 0
                pcount = 0
                ocount = 0
                ncount = 0
                for hp in range(4):
                    def proj_chunk(which, col0, c):
                        nonlocal ev
                        cs = slice(c * CH, (c + 1) * CH)
                        bi = 1 + (ev % 3)
                        ev += 1
                        b = banks[bi]
                        bn = "bank%d" % bi
                        for kc in range(8):
                            op(PE, lambda: eng[PE].matmul(b[:], win[:, kc, col0:col0 + 128], uT[:, kc, cs], start=(kc == 0), stop=(kc == 7)),
                               r=["win", "uT%d_%d" % (kc, c)], w=[bn], inc=(kc == 7))
                        if which == "q":
                            op(ACT, lambda: eng[ACT].mul(out=qT[:, cs], in_=b[:], mul=0.125), r=[bn], w=["qT_c%d" % c])
                        elif which == "k":
                            op(DVE, lambda: eng[DVE].tensor_copy(out=kT[:, PAD + c * CH:PAD + (c + 1) * CH], in_=b[:]), r=[bn], w=["kT_c%d" % c])
                        else:
                            op(ACT, lambda: eng[ACT].copy(out=vT[:, PAD + c * CH:PAD + (c + 1) * CH], in_=b[:]), r=[bn], w=["vT_c%d" % c])

                    def transpose_batch(t0):
                        n = min(8, NT3 - t0)
                        for i in range(n):
                            off, d = tiles[t0 + i]
                            src = vT[:, off:off + 127 * d + 1:d]
                            op(PE, lambda: eng[PE].transpose(bankT[:, i * 128:(i + 1) * 128], src, ident[:]),
                               r=["vT_c%d" % cc for cc in range(NCH)] + ["vT_pad", "vT_pad2", "ident"], w=["bankT"], inc=(i == n - 1))
                        src_v = bankT[:, 0:n * 128].rearrange("p (t h e) -> p t h e", h=2, e=64)
                        op(DVE, lambda: eng[DVE].tensor_copy(out=v3[:, t0:t0 + n, 0:3:2, :], in_=src_v), r=["bankT"], w=["v3_%d" % (t0 // 8)])

                    for c in range(NCH):
                        proj_chunk("v", 1024 + hp * 128, c)
                    tb = list(range(0, NT3, 8))
                    qk = [("k", 512 + hp * 128, c) for c in range(NCH)] + [("q", hp * 128, c) for c in range(NCH)]
                    for i, (which, col0, c) in enumerate(qk):
                        proj_chunk(which, col0, c)
                        if i < len(tb):
                            transpose_batch(tb[i])
                    for i in range(len(qk), len(tb)):
                        transpose_batch(tb[i])
                    for hh in range(2):
                        h = 2 * hp + hh
                        prow = slice(hh * 64, hh * 64 + 64)
                        nbase, dbase = (0, 64) if hh == 0 else (64, 0)
                        if h == 0:
                            load_bias(0)
                        steps = []
                        tbase = 0
                        for pi, (win_, d) in enumerate(PATTERNS):
                            L = S // d
                            nqt = L // 128
                            gsz = min(4, nqt)
                            for r_ in range(d):
                                for g0 in range(0, nqt, gsz):
                                    obi = 4 + (ocount % 2)
                                    ocount += 1
                                    for q0 in range(g0, g0 + gsz, 2):
                                        first = (q0 == 0)
                                        last = (q0 + 1 == nqt - 1)
                                        if d == 16:
                                            combo = 6
                                        else:
                                            combo = pi * 3 + (0 if first else (2 if last else 1))
                                        fill = (pi, q0 // 8) if d == 1 else (pi, r_ if d == 4 else r_ // 4)
                                        steps.append(dict(pi=pi, d=d, r=r_, g0=g0, gsz=gsz, q0=q0, nqt=nqt, tbase=tbase, obi=obi,
                                                          combo=combo, glast=(q0 + 2 >= g0 + gsz), fill=fill))
                            tbase += d * (nqt + 1)

                        def emit_S(st):
                            nonlocal pcount
                            sbi = 1 + (pcount % 3)
                            st["sbi"] = sbi
                            st["pti"] = pcount % 3
                            pcount += 1
                            sbk = banks[sbi]
                            sbn = "bank%d" % sbi
                            d, r_ = st["d"], st["r"]
                            op(PE, lambda: eng[PE].matmul(sbk[:], ident[:], bia[:, st["combo"], :], start=True, stop=False, skip_group_check=True),
                               r=["ident", "bia_p%d" % st["pi"]], w=[sbn], inc=False)
                            for qq in range(2):
                                qt = st["q0"] + qq
                                slot = st["slot"]
                                if d == 1:
                                    qo = 128 * (qt % 8)
                                elif d == 4:
                                    qo = 128 * qt
                                else:
                                    qo = (r_ % 4) * 256 + 128 * qt
                                qap = qS[slot][:, qo:qo + 128]
                                qres = "qS%d" % slot
                                for j in range(2):
                                    l0 = 128 * qt - 64 + 128 * j
                                    koff = PAD + l0 * d + r_
                                    kap = kT[:, koff:koff + 127 * d + 1:d]
                                    dst = sbk[:, (qq * 2 + j) * 128:(qq * 2 + j + 1) * 128]
                                    op(PE, lambda: eng[PE].matmul(dst, kap, qap, start=False, stop=True, skip_group_check=True),
                                       r=KT_ALL + [qres], w=[sbn], inc=(qq == 1 and j == 1))
                            pt = pT[st["pti"]]
                            op(ACT, lambda: eng[ACT].activation(out=pt[:], in_=sbk[:], func=AF.Exp), r=[sbn], w=["pT%d" % st["pti"]])

                        def emit_PV(st):
                            d, r_, g0, gsz = st["d"], st["r"], st["g0"], st["gsz"]
                            ob = banks[st["obi"]]
                            obn = "bank%d" % st["obi"]
                            pt = pT[st["pti"]]
                            ptn = "pT%d" % st["pti"]
                            for qq in range(2):
                                qt = st["q0"] + qq
                                for j in range(2):
                                    ti = st["tbase"] + r_ * (st["nqt"] + 1) + qt + j
                                    lhs = v3[:, ti, 0:2, :] if hh == 0 else v3[:, ti, 1:3, :]
                                    oq = (qt - g0)
                                    op(PE, lambda: eng[PE].matmul(ob[:, oq * 128:(oq + 1) * 128], lhs.rearrange("p a b -> p (a b)"),
                                                                   pt[:, (qq * 2 + j) * 128:(qq * 2 + j + 1) * 128], start=(j == 0), stop=(j == 1)),
                                       r=["v3_%d" % (ti // 8), ptn], w=[obn], inc=(qq == 1 and j == 1))
                            if st["glast"]:
                                a0 = 128 * g0 * d + r_
                                a1 = a0 + (128 * gsz - 1) * d + 1
                                aview = acc[:, a0:a1:d]
                                accr = ["acc_c%d" % cc for cc in range(a0 // CH, (a1 - 1) // CH + 1)]
                                if st["pi"] == 0:
                                    op(DVE, lambda: eng[DVE].tensor_copy(out=aview, in_=ob[:, 0:128 * gsz]), r=[obn], w=accr)
                                else:
                                    op(DVE, lambda: eng[DVE].tensor_tensor(out=aview, in0=aview, in1=ob[:, 0:128 * gsz], op=ALU.add), r=[obn] + accr, w=accr)

                        fills = []
                        for st in steps:
                            if st["fill"] is not None and (not fills or fills[-1] != st["fill"]):
                                fills.append(st["fill"])
                        fslot = {}

                        def emit_fill(f):
                            pi_, fi = f
                            d_ = PATTERNS[pi_][1]
                            slot = fillc[0] % 2
                            fillc[0] += 1
                            fslot[f] = slot
                            if d_ == 1:
                                src = qT[prow, fi * 1024:(fi + 1) * 1024]
                                dstq = qS[slot][prow, :]
                            elif d_ == 4:
                                src = qT[prow, fi:fi + 4 * 1023 + 1:4]
                                dstq = qS[slot][prow, :]
                            else:
                                src = qT[prow, :].rearrange("p (l r) -> p r l", r=16)[:, 4 * fi:4 * fi + 4, :]
                                dstq = qS[slot][prow, :].rearrange("p (r l) -> p r l", l=256)
                            op(DVE, lambda: eng[DVE].tensor_copy(out=dstq, in_=src), r=["qT_c%d" % cc for cc in range(NCH)], w=["qS%d" % slot])

                        orow = slice(64 - hh * 64, 128 - hh * 64)
                        for sl_ in range(2):
                            op(POOL, lambda: eng[POOL].memset(qS[sl_][orow, :], 0.0), w=["qS%d" % sl_])
                        nf = 0
                        if fills:
                            emit_fill(fills[0])
                            nf = 1
                        pend = []
                        curf = None
                        for st in steps:
                            if st["fill"] is not None and st["fill"] != curf:
                                curf = st["fill"]
                                if nf < len(fills):
                                    emit_fill(fills[nf])
                                    nf += 1
                            if st["fill"] is not None:
                                st["slot"] = fslot[st["fill"]]
                            if deferred and st["pi"] == 0 and st["q0"] == st["g0"]:
                                deferred.pop(0)()
                            emit_S(st)
                            pend.append(st)
                            if len(pend) > SKEW_A:
                                emit_PV(pend.pop(0))
                        while pend:
                            emit_PV(pend.pop(0))
                        if h + 1 < 8:
                            load_bias(h + 1)
                        nrow = slice(nbase, nbase + 64)
                        drow = slice(dbase, dbase + 64)

                        def norm_chunk(c, h=h, nrow=nrow, drow=drow):
                            cs = slice(c * CH, (c + 1) * CH)
                            s = ncnt[0] % 2
                            ncnt[0] += 1
                            an = "acc_c%d" % c
                            op(ACT, lambda: eng[ACT].activation(out=acc[drow, cs], in_=acc[drow, cs], func=AF.Ln), r=[an], w=[an])
                            op(ACT, lambda: eng[ACT].activation(out=acc[drow, cs], in_=acc[drow, cs], func=AF.Exp, scale=-1.0), r=[an], w=[an])
                            op(DVE, lambda: eng[DVE].tensor_copy(out=rdenA[s][nrow, :], in_=acc[drow, cs]), r=[an], w=["rdenA0"])
                            op(DVE, lambda: eng[DVE].tensor_tensor(out=ostA[s][nrow, :], in0=acc[nrow, cs], in1=rdenA[s][nrow, :], op=ALU.mult),
                               r=[an, "rdenA0"], w=["ostA%d" % s])
                            dma(SP, "ostA%d" % s, mixT[h * 64:(h + 1) * 64, cs], ostA[s][nrow, :], r=["ostA%d" % s], w=["mixT%d" % h])

                        if hh == 0:
                            deferred.extend([(lambda c=c, f=norm_chunk: f(c)) for c in range(NCH)])
                        else:
                            for c in range(NCH):
                                norm_chunk(c)
                T.barrier()

            cqn = botA[:].bitcast(BF16).rearrange("p (k t) -> p k t", k=2)
            ckvn = botB[:]
            kpe = botC[:, 0:S]
            with ExitStack() as es0:
                sq = [sb("sqb%d" % i, [128, CH], BF16, es0) for i in range(2)]
                rtmp = [sb("rtmpb%d" % i, [128, CH], F32, es0) for i in range(2)]
                rstd = [sb("rstdb%d" % i, [128, CH], F32, es0) for i in range(2)]
                lat = [sb("lat%d" % i, [128, 3, CH], F32, es0) for i in range(2)]
                kr1 = [sb("kr1_%d" % i, [128, CH], F32, es0) for i in range(2)]
                kr2 = [sb("kr2_%d" % i, [128, CH], F32, es0) for i in range(2)]
                csc = [sb("csc%d" % i, [128, 2, CH], F32, es0) for i in range(2)]

                def lat_proj(c):
                    cs = slice(c * CH, (c + 1) * CH)
                    s = c % 2
                    dma(SP, "csc%d" % s, csc[s][64:96, 0, :], cos_in[64:96, cs], w=["csc%d" % s])
                    dma(SP, "csc%d" % s, csc[s][64:96, 1, :], sin_in[64:96, cs], w=["csc%d" % s])
                    ur = ["uT%d_%d" % (kc, c) for kc in range(8)]
                    for i, col0 in enumerate((1536, 1664, 1792)):
                        b = banks[1 + i]
                        bn = "bank%d" % (1 + i)
                        for kc in range(8):
                            op(PE, lambda: eng[PE].matmul(b[:], win[:, kc, col0:col0 + 128], uT[:, kc, cs], start=(kc == 0), stop=(kc == 7)),
                               r=["win"] + ur, w=[bn], inc=(kc == 7))
                        op(ACT, lambda: eng[ACT].copy(out=lat[s][:, i, :], in_=b[:]), r=[bn], w=["lat%d_%d" % (s, i)])
                    for kc in range(8):
                        op(PE, lambda: eng[PE].matmul(banks[4][0:96, :], win[:, kc, 1856:1952], uT[:, kc, cs], start=(kc == 0), stop=(kc == 7)),
                           r=["win"] + ur, w=["bank4"], inc=(kc == 7))
                    for kc in range(8):
                        op(PE, lambda: eng[PE].matmul(banks[5][0:96, :], wkr_sw[:, kc, :], uT[:, kc, cs], start=(kc == 0), stop=(kc == 7)),
                           r=["wkr_sw"] + ur, w=["bank5"], inc=(kc == 7))
                    op(DVE, lambda: eng[DVE].tensor_tensor(out=kr1[s][64:96, :], in0=banks[4][64:96, :], in1=csc[s][64:96, 0, :], op=ALU.mult),
                       r=["bank4", "csc%d" % s], w=["kr1_%d" % s])
                    op(DVE, lambda: eng[DVE].tensor_tensor(out=kr2[s][64:96, :], in0=banks[5][64:96, :], in1=csc[s][64:96, 1, :], op=ALU.mult),
                       r=["bank5", "csc%d" % s], w=["kr2_%d" % s])
                    op(POOL, lambda: eng[POOL].tensor_tensor(out=kpe[64:96, cs], in0=kr1[s][64:96, :], in1=kr2[s][64:96, :], op=ALU.add),
                       r=["kr1_%d" % s, "kr2_%d" % s], w=["kpe_c%d" % c])

                def lat_norm(c):
                    cs = slice(c * CH, (c + 1) * CH)
                    s = c % 2
                    for i in range(3):
                        q = i % 2
                        op(ACT, lambda: eng[ACT].activation(out=sq[q][:], in_=lat[s][:, i, :], func=AF.Square), r=["lat%d_%d" % (s, i)], w=["sqb%d" % q])
                        bsel = banks[6] if i < 2 else banks[0]
                        bname = "bank6" if i < 2 else "bank0"
                        op(PE, lambda: eng[PE].matmul(bsel[:], ones[:], sq[q][:], start=(i != 1), stop=(i != 0)),
                           r=["ones", "sqb%d" % q], w=[bname])
                    for k2, (bsel, bname, nfeat, idxs) in enumerate(((banks[6], "bank6", 256, (0, 1)), (banks[0], "bank0", 128, (2,)))):
                        op(ACT, lambda: eng[ACT].activation(out=rtmp[k2][:], in_=bsel[:], func=AF.Ln, scale=1.0 / nfeat, bias=eps_t[:, 0:1]),
                           r=[bname, "eps"], w=["rtmpb%d" % k2])
                        op(ACT, lambda: eng[ACT].activation(out=rstd[k2][:], in_=rtmp[k2][:], func=AF.Exp, scale=-0.5), r=["rtmpb%d" % k2], w=["rstdb%d" % k2])
                        for i in idxs:
                            dstl = cqn[:, i, cs] if i < 2 else ckvn[:, cs]
                            gsc = gq[:, i:i + 1] if i < 2 else gkv[:, 0:1]
                            op(DVE, lambda: eng[DVE].scalar_tensor_tensor(out=dstl, in0=lat[s][:, i, :], scalar=gsc, in1=rstd[k2][:], op0=ALU.mult, op1=ALU.mult),
                               r=["lat%d_%d" % (s, i), "rstdb%d" % k2, "gq", "gkv"], w=[("cqn_c%d" if i < 2 else "ckvn_c%d") % c])

                for c in range(NCH):
                    lat_proj(c)
                    if c > 0:
                        lat_norm(c - 1)
                lat_norm(NCH - 1)
                T.barrier()
        if debug:
            with ExitStack() as es0:
                dst = sb("dstage2", [128, S], F32, es0)
                for i in range(2):
                    op(DVE, lambda: eng[DVE].tensor_copy(out=dst[:], in_=cqn[:, i, :]), r=["cqn_c%d" % cc for cc in range(NCH)], w=["dstage"])
                    dma(SP, "dbg", dbg["d_cq"][i * 128:(i + 1) * 128, :], dst[:], r=["dstage"])
                op(DVE, lambda: eng[DVE].tensor_copy(out=dst[:], in_=ckvn[:]), r=["ckvn_c%d" % cc for cc in range(NCH)], w=["dstage"])
                dma(SP, "dbg", dbg["d_ckv"][:, :], dst[:], r=["dstage"])
                op(DVE, lambda: eng[DVE].memset(dst[:], 0.0), w=["dstage"])
                op(DVE, lambda: eng[DVE].tensor_copy(out=dst[64:96, :], in_=kpe[64:96, :]), r=["kpe_c%d" % cc for cc in range(NCH)], w=["dstage"])
                dma(SP, "dbg", dbg["d_kpe"][:, :], dst[:], r=["dstage"])
                T.barrier()

        NWA = 4
        wo = sb("wo", [128, 8, D], BF16)
        wuA = sb("wuA", [128, 8, NWA * 512], BF16)
        wd_v = w_down.rearrange("(j p) o -> p j o", p=128)
        wu_v = w_up.rearrange("(k p) f -> p k f", p=128)
        with ExitStack() as eB:
            csB = [sb("csB%d" % i, [128, 2, CH], F32, eB) for i in range(2)]
            wq = sb("wq", [128, 2, 768], BF16, eB)
            wqs = sb("wqs", [128, 2, 768], BF16, eB)
            wkv = sb("wkv", [128, 2, 8, 64], BF16, eB)
            vall = sb("vall", [128, 32, 8, 128], BF16, eB)
            qh = [sb("qh%d" % i, [128, S], BF16, eB) for i in range(2)]
            kh = [sb("kh%d" % i, [128, S], BF16, eB) for i in range(2)]
            stq = qh[1][:, :].bitcast(F32)
            stk = kh[1][:, :].bitcast(F32)
            for kc in range(2):
                dma(SP, "c_wq", stq[:, kc * 768:(kc + 1) * 768], w_qb[kc * 128:(kc + 1) * 128, :], w=["qh1"])
            dma(SP, "c_wkv", stk[:, 0:1024], w_kvb, w=["kh1"])
            for kc in range(8):
                dma(POOL, "c_wo", wo[:, kc, :], w_out[kc * 128:(kc + 1) * 128, :], w=["wo"])
            for b in range(NWA):
                dma(POOL, "c_wu%d" % b, wuA[:, :, b * 512:(b + 1) * 512], wu_v[:, :, b * 512:(b + 1) * 512], w=["wu%d" % b])
            op(DVE, lambda: eng[DVE].memset(wqs[:], 0.0), w=["wqs"])
            op(DVE, lambda: eng[DVE].tensor_copy(out=wq[:].rearrange("p k c -> p (k c)"), in_=stq[:, 0:1536]), r=["qh1"], w=["wq"])
            op(DVE, lambda: eng[DVE].tensor_copy(out=wkv[:], in_=stk[:, 0:1024].rearrange("p (h t e) -> p t h e", t=2, e=64)), r=["kh1"], w=["wkv"])
            wq_v = wq[:].rearrange("p k (h e) -> p k h e", e=96)
            wqs_v = wqs[:].rearrange("p k (h e) -> p k h e", e=96)
            for kc in range(2):
                op(DVE, lambda: eng[DVE].tensor_scalar(out=wqs_v[:, kc, :, 64:80], in0=wq_v[:, kc, :, 80:96], scalar1=-1.0, scalar2=None, op0=ALU.mult),
                   r=["wq"], w=["wqs"])
                op(DVE, lambda: eng[DVE].tensor_copy(out=wqs_v[:, kc, :, 80:96], in_=wq_v[:, kc, :, 64:80]), r=["wq"], w=["wqs"])

            def build_vall(t):
                bi = 1 + t % 2
                op(PE, lambda: eng[PE].matmul(banks[bi][:], ckvn[:, t * 128:(t + 1) * 128], wkv[:, 1, :, :].rearrange("p h e -> p (h e)"), start=True, stop=True),
                   r=["ckvn_c%d" % (t // 4), "wkv"], w=["bank%d" % bi])
                srcv = banks[bi][:].rearrange("p (h e) -> p h e", e=64)
                op(DVE, lambda: eng[DVE].memset(vall[:, t, :, 64:128], 1.0), w=["vall1_%d" % t])
                op(ACT, lambda: eng[ACT].copy(out=vall[:, t, :, 0:64], in_=srcv), r=["bank%d" % bi], w=["vall_%d" % t])

            t1 = sb("t1", [128, CH], F32, eB)
            t2 = sb("t2", [128, CH], F32, eB)
            pB = [sb("pB%d" % i, [128, 512], BF16, eB) for i in range(3)]
            ostB = [sb("ostB%d" % i, [128, CH], BF16, eB) for i in range(2)]
            rdB = sb("rdB", [128, CH], F32, eB)
            scale_b = 96.0 ** -0.5
            print("phase B sbuf free:", nc.sbuf_bytes_remaining)
            pc = 0
            ncs = [0, 0]

            def emit_proj(h, c, pro=False):
                nonlocal pc
                s_ = h % 2
                qt_, kt_ = qh[s_], kh[s_]
                qn, kn = "qh%d" % s_, "kh%d" % s_
                cs = slice(c * CH, (c + 1) * CH)
                for kc in range(2):
                    op(PE, lambda: eng[PE].matmul(banks[4][0:96, :], wq[:, kc, h * 96:(h + 1) * 96], cqn[:, kc, cs], start=(kc == 0), stop=(kc == 1)),
                       r=["wq", "cqn_c%d" % c], w=["bank4"], inc=(kc == 1))
                for kc in range(2):
                    op(PE, lambda: eng[PE].matmul(banks[5][0:96, :], wqs[:, kc, h * 96:(h + 1) * 96], cqn[:, kc, cs], start=(kc == 0), stop=(kc == 1)),
                       r=["wqs", "cqn_c%d" % c], w=["bank5"], inc=(kc == 1))
                op(PE, lambda: eng[PE].matmul(bank7[0:64, :], wkv[:, 0, h, :], ckvn[:, cs], start=True, stop=True),
                   r=["wkv", "ckvn_c%d" % c], w=["bankT"])
                sl = ncs[0] % 2
                ncs[0] += 1
                dma(SP, "csB%d" % sl, csB[sl][64:96, 0, :], cos_in[64:96, cs], w=["csB%d" % sl])
                dma(SP, "csB%d" % sl, csB[sl][64:96, 1, :], sin_in[64:96, cs], w=["csB%d" % sl])
                if pro:
                    op(ACT, lambda: eng[ACT].copy(out=qt_[0:64, cs], in_=banks[4][0:64, :]), r=["bank4"], w=[qn])
                else:
                    op(DVE, lambda: eng[DVE].tensor_copy(out=qt_[0:64, cs], in_=banks[4][0:64, :]), r=["bank4"], w=[qn])
                op(DVE, lambda: eng[DVE].tensor_tensor(out=t1[64:96, :], in0=banks[4][64:96, :], in1=csB[sl][64:96, 0, :], op=ALU.mult),
                   r=["bank4", "csB%d" % sl], w=["t1"])
                op(DVE, lambda: eng[DVE].tensor_tensor(out=t2[64:96, :], in0=banks[5][64:96, :], in1=csB[sl][64:96, 1, :], op=ALU.mult),
                   r=["bank5", "csB%d" % sl], w=["t2"])
                op(DVE, lambda: eng[DVE].tensor_tensor(out=qt_[64:96, cs], in0=t1[64:96, :], in1=t2[64:96, :], op=ALU.add),
                   r=["t1", "t2"], w=[qn])
                if pro:
                    op(ACT, lambda: eng[ACT].copy(out=kt_[0:64, cs], in_=bank7[0:64, :]), r=["bankT"], w=[kn])
                else:
                    op(DVE, lambda: eng[DVE].tensor_copy(out=kt_[0:64, cs], in_=bank7[0:64, :]), r=["bankT"], w=[kn])
                if c == NCH - 1:
                    op(DVE, lambda: eng[DVE].tensor_copy(out=kt_[64:96, :], in_=kpe[64:96, :]), r=["kpe_c%d" % cc for cc in range(NCH)], w=[kn])

            def emit_QK(st):
                nonlocal pc
                h, c, kt = st["h"], st["c"], st["kt"]
                s_ = h % 2
                cs = slice(c * CH, (c + 1) * CH)
                sbi = 1 + (pc % 3)
                pc += 1
                st["pti"] = ncs[1] % 3
                ncs[1] += 1
                op(PE, lambda: eng[PE].matmul(banks[sbi][:], kh[s_][0:96, kt * 128:(kt + 1) * 128], qh[s_][0:96, cs], start=True, stop=True),
                   r=["kh%d" % s_, "qh%d" % s_], w=["bank%d" % sbi])
                pt = pB[st["pti"]]
                op(ACT, lambda: eng[ACT].activation(out=pt[:], in_=banks[sbi][:], func=AF.Exp, scale=scale_b), r=["bank%d" % sbi], w=["pB%d" % st["pti"]])

            def emit_PVB(st):
                h, c, kt = st["h"], st["c"], st["kt"]
                cs = slice(c * CH, (c + 1) * CH)
                obi = 0 if c % 2 == 0 else 6
                ob = banks[obi]
                obn = "bank%d" % obi
                pt = pB[st["pti"]]
                op(PE, lambda: eng[PE].matmul(ob[:], vall[:, kt, h, :], pt[:], start=(kt == 0), stop=(kt == 31)),
                   r=["vall_%d" % kt, "vall1_%d" % kt, "pB%d" % st["pti"]], w=[obn], inc=(kt == 31))
                if kt == 31:
                    so = c % 2
                    op(DVE, lambda: eng[DVE].reciprocal(out=rdB[0:64, :], in_=ob[64:128, :]), r=[obn], w=["rdB"])
                    op(DVE, lambda: eng[DVE].tensor_tensor(out=ostB[so][0:64, :], in0=ob[0:64, :], in1=rdB[0:64, :], op=ALU.mult),
                       r=[obn, "rdB"], w=["ostB%d" % so])
                    dma(SP, "ostB%d" % so, mixT[512 + h * 64:512 + (h + 1) * 64, cs], ostB[so][0:64, :], r=["ostB%d" % so], w=["mixT%d" % (8 + h)])

            for c in range(NCH):
                emit_proj(0, c, pro=True)
                for t in range(4 * c, 4 * c + 4):
                    build_vall(t)
            pend = []
            for h in range(8):
                for c in range(NCH):
                    if h + 1 < 8:
                        emit_proj(h + 1, c)
                    for kt in range(32):
                        st = dict(h=h, c=c, kt=kt)
                        emit_QK(st)
                        pend.append(st)
                        if len(pend) > SKEW:
                            emit_PVB(pend.pop(0))
            while pend:
                emit_PVB(pend.pop(0))
            T.barrier()

        mix_all = ["mixT%d" % i for i in range(16)]
        if debug:
            with ExitStack() as eD:
                dm = sb("dm", [128, S], BF16, eD)
                dmf = sb("dmf", [128, S], F32, eD)
                for kc in range(8):
                    dma(SP, "dm", dm[:], mixT[kc * 128:(kc + 1) * 128, :], r=mix_all, w=["dm"])
                    op(DVE, lambda: eng[DVE].tensor_copy(out=dmf[:], in_=dm[:]), r=["dm"], w=["dmf"])
                    dma(SP, "dmo", dbg["d_mix"][kc * 128:(kc + 1) * 128, :], dmf[:], r=["dmf"])
                T.barrier()

        with ExitStack() as eC:
            wuB = sb("wuB", [128, 8, (8 - NWA) * 512], BF16, eC)
            wd = sb("wd", [128, 32, D], BF16, eC)
            gm = sb("gm", [128, 8], F32, eC)
            gf = sb("gf", [128, 8], F32, eC)
            dma(SP, "c_gm", gm[:], g_mlp, w=["gm"])
            dma(SP, "c_gf", gf[:], g_fin, w=["gf"])
            def load_wd(j4, after=()):
                dma(POOL, "c_wd%d" % j4, wd[:, 4 * j4:4 * j4 + 4, :], wd_v[:, 4 * j4:4 * j4 + 4, :], r=list(after), w=["wd%d" % j4])

            def load_rest_weights(after):
                load_wd(0, after)
                load_wd(1)
                for b in range(NWA, 8):
                    dma(POOL, "c_wu%d" % b, wuB[:, :, (b - NWA) * 512:(b - NWA + 1) * 512], wu_v[:, :, b * 512:(b + 1) * 512], w=["wu%d" % b])
                for j4 in range(2, 8):
                    load_wd(j4)

            def wu_cols(kc, j):
                if j // 4 < NWA:
                    return wuA[:, kc, j * 128:(j + 1) * 128]
                return wuB[:, kc, (j - NWA * 4) * 128:(j - NWA * 4 + 1) * 128]

            def wd_rows(j, oc):
                return wd[:, j, oc * 128:(oc + 1) * 128]

            hxs = [botA[:].rearrange("p (k t) -> p k t", k=8), sb("hxB", [128, 8, CH], F32, eC)[:]]
            mu = botB[:].rearrange("p (k t) -> p k t", k=8)
            act = botC[:, 0:S].rearrange("p (k t) -> p k t", k=8)
            sqc = [botC[:, S + i * CH:S + (i + 1) * CH] for i in range(2)]
            rl = [sb("rl%d" % i, [128, CH], F32, eC) for i in range(2)]
            rt = sb("rtC", [128, CH], F32, eC)
            rs = sb("rsC", [128, CH], F32, eC)
            bc = 0
            print("phase C sbuf free:", nc.sbuf_bytes_remaining)

            def hn(c, k):
                return "hx%d_%d" % (c % 2, k)

            rt2 = rt
            rs2 = sb("rsC2", [128, CH], F32, eC)
            sqn = [0]

            def stat_part(c, kc):
                hx = hxs[c % 2]
                q = sqn[0] % 2
                sqn[0] += 1
                op(ACT, lambda: eng[ACT].activation(out=sqc[q], in_=hx[:, kc, :], func=AF.Square), r=[hn(c, kc)], w=["sqc%d" % q])
                op(PE, lambda: eng[PE].matmul(banks[0][:], ones[:], sqc[q], start=(kc == 0), stop=(kc == 7)),
                   r=["ones", "sqc%d" % q], w=["bank0"])

            def stat_fin(which):
                rt_, rs_, nm = (rt, rs, "") if which == 1 else (rt2, rs2, "2")
                op(ACT, lambda: eng[ACT].activation(out=rt_[:], in_=banks[0][:], func=AF.Ln, scale=1.0 / D, bias=eps_t[:, 0:1]),
                   r=["bank0", "eps"], w=["rtC"])
                op(ACT, lambda: eng[ACT].activation(out=rs_[:], in_=rt_[:], func=AF.Exp, scale=-0.5), r=["rtC"], w=["rsC" + nm])

            def load_x(c):
                cs_ = slice(c * CH, (c + 1) * CH)
                for oc in range(8):
                    dma(SP, "hxl%d_%d" % (c % 2, oc), hxs[c % 2][:, oc, :], xT_v[:, oc, cs_], w=[hn(c, oc)])

            def load_mc(c):
                cs_ = slice(c * CH, (c + 1) * CH)
                dma(SP, "mc", mu, mixT_v[:, :, cs_], r=mix_all, w=["mu%d" % k for k in range(8)])

            UPB = [(banks[1][:], "bank1"), (banks[2][:], "bank2"), (banks[3][:], "bank3"), (bank7, "bankT")]
            DNB = [(banks[4][:], "bank4"), (banks[5][:], "bank5"), (banks[6][:], "bank6")]
            dc = 0

            def out_proj(c):
                nonlocal bc
                hx = hxs[c % 2]
                for oc in range(8):
                    bap, bnm = UPB[bc % 4]; bc += 1
                    for kc in range(8):
                        op(PE, lambda: eng[PE].matmul(bap, wo[:, kc, oc * 128:(oc + 1) * 128], mu[:, kc, :], start=(kc == 0), stop=(kc == 7)),
                           r=["wo", "mu%d" % kc], w=[bnm], inc=(kc == 7))
                    op(DVE, lambda: eng[DVE].tensor_tensor(out=hx[:, oc, :], in0=hx[:, oc, :], in1=bap, op=ALU.add),
                       r=[bnm, hn(c, oc)], w=[hn(c, oc)])
                    if oc >= 1:
                        stat_part(c, oc - 1)
                stat_part(c, 7)
                stat_fin(1)

            def final_norm(c):
                cs_ = slice(c * CH, (c + 1) * CH)
                hx = hxs[c % 2]
                for kc in range(8):
                    op(DVE, lambda: eng[DVE].scalar_tensor_tensor(out=hx[:, kc, :], in0=hx[:, kc, :], scalar=gf[:, kc:kc + 1], in1=rs2[:], op0=ALU.mult, op1=ALU.mult),
                       r=[hn(c, kc), "rsC2", "gf"], w=[hn(c, kc)])
                    dma(SP, "outst%d_%d" % (c % 2, kc), outT_v[:, kc, cs_], hx[:, kc, :], r=[hn(c, kc)], w=["out"])

            load_mc(0)
            load_x(0)
            load_rest_weights([hn(0, k) for k in range(8)] + ["mu%d" % k for k in range(8)])
            out_proj(0)
            load_x(1)
            for c in range(NCH):
                hx = hxs[c % 2]
                for kc in range(8):
                    op(DVE, lambda: eng[DVE].scalar_tensor_tensor(out=mu[:, kc, :], in0=hx[:, kc, :], scalar=gm[:, kc:kc + 1], in1=rs[:], op0=ALU.mult, op1=ALU.mult),
                       r=[hn(c, kc), "rsC", "gm"], w=["mu%d" % kc])
                for qf in range(4):
                    for jj in range(8):
                        j = qf * 8 + jj
                        bap, bnm = UPB[bc % 4]; bc += 1
                        for kc in range(8):
                            op(PE, lambda: eng[PE].matmul(bap, wu_cols(kc, j), mu[:, kc, :], start=(kc == 0), stop=(kc == 7)),
                               r=["wu%d" % (j // 4), "mu%d" % kc], w=[bnm], inc=(kc == 7))
                        q = j % 2
                        op(DVE, lambda: eng[DVE].tensor_scalar(out=rl[q][:], in0=bap, scalar1=0.0, scalar2=None, op0=ALU.max),
                           r=[bnm], w=["rl%d" % q])
                        op(ACT, lambda: eng[ACT].activation(out=act[:, jj, :], in_=rl[q][:], func=AF.Square), r=["rl%d" % q], w=["act%d" % jj])
                    if qf == 3 and c + 1 < NCH:
                        load_mc(c + 1)
                    for oc in range(8):
                        bap, bnm = DNB[dc % 3]; dc += 1
                        for jj in range(8):
                            j = qf * 8 + jj
                            op(PE, lambda: eng[PE].matmul(bap, wd_rows(j, oc), act[:, jj, :], start=(jj == 0), stop=(jj == 7)),
                               r=["wd%d" % (j // 4), "act%d" % jj], w=[bnm], inc=(jj == 7))
                        op(DVE, lambda: eng[DVE].tensor_tensor(out=hx[:, oc, :], in0=hx[:, oc, :], in1=bap, op=ALU.add),
                           r=[bnm, hn(c, oc)], w=[hn(c, oc)])
                        if qf == 3 and oc >= 1:
                            stat_part(c, oc - 1)
                stat_part(c, 7)
                stat_fin(2)
                if c + 1 < NCH:
                    out_proj(c + 1)
                final_norm(c)
                if c + 2 < NCH:
                    load_x(c + 2)

        T.barrier()
        print("instructions:", dict(T.cnt), "waits:", T.nwaits)
    return nc


_CACHE = {}


def _host_constants(rel_bias):
    rb = np.asarray(rel_bias, np.float32)
    i = np.arange(128)[:, None]
    m = np.arange(128)[None, :]
    out = np.full((8, 128, 7, 4, 128), NEG, np.float32)

    def tile(d, j, edge, h):
        rel = -64 + 128 * j + i - m
        v = np.abs(rel) <= 64
        if edge and j == 0:
            v = v & (i >= 64)
        if edge and j == 1:
            v = v & (i < 64)
        g = rb[t5_buckets(rel * d), h]
        return np.where(v, g, np.float32(NEG))

    for h in range(8):
        for pi, (win_, d) in enumerate(PATTERNS):
            if d == 16:
                combos = [(6, True, True)]
            else:
                combos = [(pi * 3 + 0, True, False), (pi * 3 + 1, False, False), (pi * 3 + 2, False, True)]
            for (ci, first, last) in combos:
                out[h, :, ci, 0, :] = tile(d, 0, first, h)
                out[h, :, ci, 1, :] = tile(d, 1, False, h)
                out[h, :, ci, 2, :] = tile(d, 0, False, h)
                out[h, :, ci, 3, :] = tile(d, 1, last, h)
    return out.reshape(8, 128, 7 * 512)


def _rope_tables():
    inv_freq = (np.float32(10000.0) ** (-np.arange(0, 32, 2, dtype=np.float32) / np.float32(32))).astype(np.float32)
    pos = np.arange(S, dtype=np.float32)
    fr = (pos[:, None] * inv_freq[None, :]).astype(np.float32)
    cos = np.cos(fr).astype(np.float32).T
    sin = np.sin(fr).astype(np.float32).T
    cT = np.ones((128, S), np.float32)
    sT = np.zeros((128, S), np.float32)
    cT[64:80] = cos; cT[80:96] = cos
    sT[64:80] = sin; sT[80:96] = sin
    return cT, sT


def _lay(v, k):
    return np.ascontiguousarray(np.asarray(v, np.float32).reshape(k, 128).T)


def kernel(x, mix_norm_g, w_in, q_norm_g, w_q_b, kv_norm_g, w_kv_b, w_out, mlp_norm_g, w_up, w_down,
           rel_bias, final_norm_g, _debug=False, _cores=8):
    x = np.asarray(x, np.float32)
    key = ("nc", _debug)
    if key not in _CACHE:
        _CACHE[key] = build_program(debug=_debug)
    nc = _CACHE[key]
    cT, sT = _rope_tables()
    shared = {
        "w_in": np.ascontiguousarray(np.asarray(w_in, np.float32)[0]),
        "g_mix": _lay(mix_norm_g, 8),
        "w_qb": np.ascontiguousarray(np.asarray(w_q_b, np.float32)[0]),
        "g_q": _lay(q_norm_g, 2),
        "w_kvb": np.ascontiguousarray(np.asarray(w_kv_b, np.float32)[0]),
        "g_kv": _lay(kv_norm_g, 1),
        "w_out": np.ascontiguousarray(np.asarray(w_out, np.float32)[0]),
        "g_mlp": _lay(mlp_norm_g, 8),
        "w_up": np.ascontiguousarray(np.asarray(w_up, np.float32)[0]),
        "w_down": np.ascontiguousarray(np.asarray(w_down, np.float32)[0]),
        "g_fin": _lay(final_norm_g, 8),
        "biasT": _host_constants(rel_bias),
        "ident": np.eye(128, dtype=np.float32),
        "cosT": cT,
        "sinT": sT,
    }
    in_maps = []
    for b in range(_cores):
        m = dict(shared)
        m["xT"] = np.ascontiguousarray(x[b].T)
        in_maps.append(m)
    res = run_bass_kernel_spmd(nc, in_maps, core_ids=list(range(_cores)))
    if _debug:
        return res.results
    out = np.stack([np.ascontiguousarray(r["outT"].T) for r in res.results], axis=0)
    return out.astype(np.float32)
```

```python
import numpy as np
from contextlib import ExitStack
import concourse.bass as bass
import concourse.mybir as mybir
from concourse.bass_utils import run_bass_kernel_spmd

F32 = mybir.dt.float32
BF16 = mybir.dt.bfloat16
ALU = mybir.AluOpType
AF = mybir.ActivationFunctionType

S = 4096
D = 1024
NCH = 8
CH = 512
INC = 1952
PAD = 1024
NEG = -30000.0
EPS = 1e-6
PATTERNS = ((128, 1), (512, 4), (2048, 16))
SKEW = 2
SKEW_A = 3
DFF = 4096


class Tracker:
    def __init__(self, nc, es):
        self.nc = nc
        self.es = es
        self.engs = {"pe": nc.tensor, "act": nc.scalar, "dve": nc.vector, "pool": nc.gpsimd, "sp": nc.sync}
        self.sems = {}
        self.cnt = {}
        for k in ("pe", "act", "dve", "pool"):
            self.sems[k] = es.enter_context(nc.semaphore("sem_" + k))
            self.cnt[k] = 0
        self.seen = {e: {} for e in self.engs}
        self.lastw = {}
        self.readers = {}
        self.nwaits = 0

    def chan(self, name):
        if name not in self.sems:
            self.sems[name] = self.es.enter_context(self.nc.semaphore("dma_" + name))
            self.cnt[name] = 0
        return name

    def _deps(self, eng, r, w):
        deps = {}

        def add(key, val, raw):
            if key == eng and not raw and eng in ("pe", "sp"):
                return
            if deps.get(key, 0) < val:
                deps[key] = val

        for res in r:
            lw = self.lastw.get(res)
            if lw is not None:
                add(lw[0], lw[1], True)
        for res in w:
            lw = self.lastw.get(res)
            if lw is not None:
                add(lw[0], lw[1], False)
            for k, v in self.readers.get(res, {}).items():
                add(k, v, False)
        return deps

    def _emit_waits(self, eng, deps):
        seen = self.seen[eng]
        for key, val in deps.items():
            if key not in ("pe", "act", "dve", "pool"):
                val = self.cnt[key]
            if seen.get(key, 0) < val:
                self.engs[eng].wait_ge(self.sems[key], val)
                seen[key] = val
                self.nwaits += 1

    def _update(self, key, seq, r, w):
        for res in w:
            self.lastw[res] = (key, seq)
            self.readers[res] = {}
        for res in r:
            d = self.readers.setdefault(res, {})
            if d.get(key, 0) < seq:
                d[key] = seq

    def op(self, eng, fn, r=(), w=(), inc=True):
        self._emit_waits(eng, self._deps(eng, r, w))
        ins = fn()
        if inc:
            self.cnt[eng] += 1
            ins.then_inc(self.sems[eng], 1)
            seq = self.cnt[eng]
        else:
            seq = self.cnt[eng] + 1
        self._update(eng, seq, r, w)
        return ins

    def dma(self, q, ch, out, in_, r=(), w=()):
        self.chan(ch)
        if q == "pool":
            self.swq = getattr(self, "swq", [])
            while len(self.swq) >= 3:
                k, v = self.swq.pop(0)
                if self.seen[q].get(k, 0) < v:
                    self.engs[q].wait_ge(self.sems[k], v)
                    self.seen[q][k] = v
            self.swq.append((ch, self.cnt[ch] + 16))
        self._emit_waits(q, self._deps(q, r, w))
        self.engs[q].dma_start(out=out, in_=in_).then_inc(self.sems[ch], 16)
        self.cnt[ch] += 16
        self._update(ch, self.cnt[ch], r, w)

    def barrier(self):
        keys = list(self.cnt.keys())
        for e in self.engs:
            self.wait_all(e, [k for k in keys if k != "sp"])

    def wait_all(self, eng, keys):
        for k in keys:
            if self.cnt[k] > 0 and self.seen[eng].get(k, 0) < self.cnt[k]:
                self.engs[eng].wait_ge(self.sems[k], self.cnt[k])
                self.seen[eng][k] = self.cnt[k]


def t5_buckets(rel):
    nb = 16
    max_exact = 8
    ret = (rel > 0).astype(np.int32) * nb
    n = np.abs(rel)
    large = max_exact + (np.log(np.maximum(n, 1) / max_exact) / np.log(1024 / max_exact) * (nb - max_exact)).astype(np.int32)
    large = np.minimum(large, nb - 1)
    return (ret + np.where(n < max_exact, n, large)).astype(np.int32)


def build_program(debug=False):
    nc = bass.Bass("TRN2", target_bir_lowering=False)

    def din(name, shape, dt=F32):
        return nc.dram_tensor(name, list(shape), dt, kind="ExternalInput").ap()

    xT = din("xT", [D, S])
    w_in = din("w_in", [D, INC])
    g_mix = din("g_mix", [128, 8])
    w_qb = din("w_qb", [256, 768])
    g_q = din("g_q", [128, 2])
    w_kvb = din("w_kvb", [128, 1024])
    g_kv = din("g_kv", [128, 1])
    w_out = din("w_out", [D, D])
    g_mlp = din("g_mlp", [128, 8])
    w_up = din("w_up", [D, DFF])
    w_down = din("w_down", [DFF, D])
    g_fin = din("g_fin", [128, 8])
    biasT = din("biasT", [8, 128, 7 * 512])
    ident_in = din("ident", [128, 128])
    cos_in = din("cosT", [128, S])
    sin_in = din("sinT", [128, S])
    outT = nc.dram_tensor("outT", [D, S], F32, kind="ExternalOutput").ap()
    mixT = nc.dram_tensor("mixT_scratch", [D, S], BF16).ap()
    dbg = {}
    if debug:
        for nm, shp in (("d_uT", [D, S]), ("d_cq", [256, S]), ("d_ckv", [128, S]), ("d_kpe", [128, S]),
                        ("d_mix", [D, S])):
            dbg[nm] = nc.dram_tensor(nm, shp, F32, kind="ExternalOutput").ap()

    with ExitStack() as es:
        T = Tracker(nc, es)
        op, dma = T.op, T.dma
        PE, ACT, DVE, POOL, SP = "pe", "act", "dve", "pool", "sp"
        eng = T.engs

        def sb(name, shape, dt, stack=es):
            return stack.enter_context(nc.sbuf_tensor("sb_" + name, list(shape), dt))

        banks = [es.enter_context(nc.psum_tensor("bank%d" % i, [128, 512], F32)) for i in range(7)]
        bankT = es.enter_context(nc.psum_tensor("bankT", [128, 1024], BF16))
        bank7 = bankT[:].bitcast(F32)

        ident = sb("ident", [128, 128], BF16)
        ones = sb("ones", [128, 128], BF16)
        dma(POOL, "c_id", ident[:], ident_in, w=["ident"])
        op(POOL, lambda: eng[POOL].memset(ones[:], 1.0), w=["ones"])

        gq = sb("gq", [128, 2], F32)
        gkv = sb("gkv", [128, 1], F32)
        dma(SP, "c_gq", gq[:], g_q, w=["gq"])
        dma(SP, "c_gkv", gkv[:], g_kv, w=["gkv"])
        eps_t = sb("eps_t", [128, 1], F32)
        op(POOL, lambda: eng[POOL].memset(eps_t[:], EPS), w=["eps"])
        xT_v = xT.rearrange("(k p) t -> p k t", p=128)
        mixT_v = mixT.rearrange("(k p) t -> p k t", p=128)
        outT_v = outT.rearrange("(k p) t -> p k t", p=128)

        botA = sb("botA", [128, S], F32)
        botB = sb("botB", [128, S], BF16)
        botC = sb("botC", [128, S + 2 * PAD], BF16)
        with ExitStack() as esA:
            uT = sb("uT", [128, 8, S], BF16, esA)
            win = sb("win", [128, 8, INC], BF16, esA)
            wkr_sw = sb("wkr_sw", [128, 8, 96], BF16, esA)
            gmix = sb("gmix", [128, 8], F32, esA)
            dma(SP, "c_gmix", gmix[:], g_mix, w=["gmix"])
            op(POOL, lambda: eng[POOL].memset(wkr_sw[:], 0.0), w=["wkr_sw"])

            for kc in range(8):
                dma(POOL, "c_win", win[:, kc, :], w_in[kc * 128:(kc + 1) * 128, :], w=["win"])
            for kc in range(8):
                op(POOL, lambda: eng[POOL].tensor_scalar(out=wkr_sw[:, kc, 64:80], in0=win[:, kc, 1936:1952], scalar1=-1.0, scalar2=None, op0=ALU.mult),
                   r=["win"], w=["wkr_sw"])
                op(POOL, lambda: eng[POOL].tensor_copy(out=wkr_sw[:, kc, 80:96], in_=win[:, kc, 1920:1936]), r=["win"], w=["wkr_sw"])
            with ExitStack() as es0:
                xs = [sb("xs%d" % i, [128, 8, CH], F32, es0) for i in range(2)]
                sq = [sb("sq%d" % i, [128, CH], BF16, es0) for i in range(2)]
                rtmp = sb("rtmp", [128, CH], F32, es0)
                rstd = sb("rstd", [128, CH], F32, es0)
                for c in range(NCH):
                    s = c % 2
                    cs = slice(c * CH, (c + 1) * CH)
                    xres = "xs%d" % s
                    dma(SP, xres, xs[s][:], xT_v[:, :, cs], w=[xres])
                    for kc in range(8):
                        q = kc % 2
                        op(ACT, lambda: eng[ACT].activation(out=sq[q][:], in_=xs[s][:, kc, :], func=AF.Square), r=[xres], w=["sq%d" % q])
                        op(PE, lambda: eng[PE].matmul(banks[0][:], ones[:], sq[q][:], start=(kc == 0), stop=(kc == 7)),
                           r=["ones", "sq%d" % q], w=["bank0"])
                    op(ACT, lambda: eng[ACT].activation(out=rtmp[:], in_=banks[0][:], func=AF.Ln, scale=1.0 / D, bias=eps_t[:, 0:1]),
                       r=["bank0", "eps"], w=["rtmp"])
                    op(ACT, lambda: eng[ACT].activation(out=rstd[:], in_=rtmp[:], func=AF.Exp, scale=-0.5), r=["rtmp"], w=["rstd"])
                    for kc in range(8):
                        op(DVE, lambda: eng[DVE].scalar_tensor_tensor(out=uT[:, kc, cs], in0=xs[s][:, kc, :], scalar=gmix[:, kc:kc + 1], in1=rstd[:], op0=ALU.mult, op1=ALU.mult),
                           r=[xres, "rstd", "gmix"], w=["uT%d_%d" % (kc, c)])
                if debug:
                    dst = sb("dstage", [128, S], F32, es0)
                    for kc in range(8):
                        op(DVE, lambda: eng[DVE].tensor_copy(out=dst[:], in_=uT[:, kc, :]), r=["uT%d_%d" % (kc, c) for c in range(NCH)], w=["dstage"])
                        dma(SP, "dbg", dbg["d_uT"][kc * 128:(kc + 1) * 128, :], dst[:], r=["dstage"])
                T.barrier()

            with ExitStack() as e1:
                qT = botB
                kT = sb("kT", [128, S + 2 * PAD], BF16, e1)
                vT = botC
                NT3 = 33 + 36 + 48
                v3 = sb("v3", [128, NT3, 3, 64], BF16, e1)
                acc = botA
                bia = sb("bia", [128, 7, 512], BF16, e1)

                def load_bias(hb):
                    for (c0, c1, pn) in ((0, 3, 0), (3, 6, 1), (6, 7, 2)):
                        dma(POOL, "bia_p%d" % pn, bia[:, c0:c1, :], biasT[hb][:, c0 * 512:c1 * 512].rearrange("p (a b) -> p a b", b=512), w=["bia_p%d" % pn])
                pT = [sb("pT%d" % i, [128, 512], BF16, e1) for i in range(4)]
                qS = [sb("qS%d" % i, [128, 1024], BF16, e1) for i in range(2)]
                fillc = [0]
                ostA = [sb("ostA%d" % i, [128, CH], BF16, e1) for i in range(2)]
                rdenA = [sb("rdenA0", [128, CH], F32, e1)] * 2
                op(POOL, lambda: eng[POOL].memset(kT[:, 0:PAD], 0.0), w=["kT_pad"])
                op(POOL, lambda: eng[POOL].memset(kT[:, PAD + S:], 0.0), w=["kT_pad2"])
                op(POOL, lambda: eng[POOL].memset(vT[:, 0:PAD], 0.0), w=["vT_pad"])
                op(POOL, lambda: eng[POOL].memset(vT[:, PAD + S:], 0.0), w=["vT_pad2"])
                op(POOL, lambda: eng[POOL].memset(v3[:, :, 1, :], 1.0), w=["v3_%d" % bb for bb in range(15)])
                print("phase A sbuf free:", nc.sbuf_bytes_remaining)
                KT_ALL = ["kT_c%d" % cc for cc in range(NCH)] + ["kT_pad", "kT_pad2"]
                tiles = []
                for (win_, d) in PATTERNS:
                    L = S // d
                    for r_ in range(d):
                        for jp in range(L // 128 + 1):
                            tiles.append((PAD + (-64 + 128 * jp) * d + r_, d))
                assert len(tiles) == NT3
                deferred = []
                ncnt = [0]
                ev = 0
                pcount = 0
                ocount = 0
                ncount = 0
                for hp in range(4):
                    def proj_chunk(which, col0, c):
                        nonlocal ev
                        cs = slice(c * CH, (c + 1) * CH)
                        bi = 1 + (ev % 3)
                        ev += 1
                        b = banks[bi]
                        bn = "bank%d" % bi
                        for kc in range(8):
                            op(PE, lambda: eng[PE].matmul(b[:], win[:, kc, col0:col0 + 128], uT[:, kc, cs], start=(kc == 0), stop=(kc == 7)),
                               r=["win", "uT%d_%d" % (kc, c)], w=[bn], inc=(kc == 7))
                        if which == "q":
                            op(ACT, lambda: eng[ACT].mul(out=qT[:, cs], in_=b[:], mul=0.125), r=[bn], w=["qT_c%d" % c])
                        elif which == "k":
                            op(DVE, lambda: eng[DVE].tensor_copy(out=kT[:, PAD + c * CH:PAD + (c + 1) * CH], in_=b[:]), r=[bn], w=["kT_c%d" % c])
                        else:
                            op(ACT, lambda: eng[ACT].copy(out=vT[:, PAD + c * CH:PAD + (c + 1) * CH], in_=b[:]), r=[bn], w=["vT_c%d" % c])

                    def transpose_batch(t0):
                        n = min(8, NT3 - t0)
                        for i in range(n):
                            off, d = tiles[t0 + i]
                            src = vT[:, off:off + 127 * d + 1:d]
                            op(PE, lambda: eng[PE].transpose(bankT[:, i * 128:(i + 1) * 128], src, ident[:]),
                               r=["vT_c%d" % cc for cc in range(NCH)] + ["vT_pad", "vT_pad2", "ident"], w=["bankT"], inc=(i == n - 1))
                        src_v = bankT[:, 0:n * 128].rearrange("p (t h e) -> p t h e", h=2, e=64)
                        op(DVE, lambda: eng[DVE].tensor_copy(out=v3[:, t0:t0 + n, 0:3:2, :], in_=src_v), r=["bankT"], w=["v3_%d" % (t0 // 8)])

                    for c in range(NCH):
                        proj_chunk("v", 1024 + hp * 128, c)
                    tb = list(range(0, NT3, 8))
                    qk = [("k", 512 + hp * 128, c) for c in range(NCH)] + [("q", hp * 128, c) for c in range(NCH)]
                    for i, (which, col0, c) in enumerate(qk):
                        proj_chunk(which, col0, c)
                        if i < len(tb):
                            transpose_batch(tb[i])
                    for i in range(len(qk), len(tb)):
                        transpose_batch(tb[i])
                    for hh in range(2):
                        h = 2 * hp + hh
                        prow = slice(hh * 64, hh * 64 + 64)
                        nbase, dbase = (0, 64) if hh == 0 else (64, 0)
                        if h == 0:
                            load_bias(0)
                        steps = []
                        tbase = 0
                        for pi, (win_, d) in enumerate(PATTERNS):
                            L = S // d
                            nqt = L // 128
                            gsz = min(4, nqt)
                            for r_ in range(d):
                                for g0 in range(0, nqt, gsz):
                                    obi = 4 + (ocount % 2)
                                    ocount += 1
                                    for q0 in range(g0, g0 + gsz, 2):
                                        first = (q0 == 0)
                                        last = (q0 + 1 == nqt - 1)
                                        if d == 16:
                                            combo = 6
                                        else:
                                            combo = pi * 3 + (0 if first else (2 if last else 1))
                                        fill = (pi, q0 // 8) if d == 1 else (pi, r_ if d == 4 else r_ // 4)
                                        steps.append(dict(pi=pi, d=d, r=r_, g0=g0, gsz=gsz, q0=q0, nqt=nqt, tbase=tbase, obi=obi,
                                                          combo=combo, glast=(q0 + 2 >= g0 + gsz), fill=fill))
                            tbase += d * (nqt + 1)

                        def emit_S(st):
                            nonlocal pcount
                            sbi = pcount % 4
                            st["sbi"] = sbi
                            st["pti"] = pcount % 4
                            pcount += 1
                            sbk = banks[sbi]
                            sbn = "bank%d" % sbi
                            d, r_ = st["d"], st["r"]
                            op(PE, lambda: eng[PE].matmul(sbk[:], ident[:], bia[:, st["combo"], :], start=True, stop=False, skip_group_check=True),
                               r=["ident", "bia_p%d" % st["pi"]], w=[sbn], inc=False)
                            for qq in range(2):
                                qt = st["q0"] + qq
                                slot = st["slot"]
                                if d == 1:
                                    qo = 128 * (qt % 8)
                                elif d == 4:
                                    qo = 128 * qt
                                else:
                                    qo = (r_ % 4) * 256 + 128 * qt
                                qap = qS[slot][:, qo:qo + 128]
                                qres = "qS%d" % slot
                                for j in range(2):
                                    l0 = 128 * qt - 64 + 128 * j
                                    koff = PAD + l0 * d + r_
                                    kap = kT[:, koff:koff + 127 * d + 1:d]
                                    dst = sbk[:, (qq * 2 + j) * 128:(qq * 2 + j + 1) * 128]
                                    op(PE, lambda: eng[PE].matmul(dst, kap, qap, start=False, stop=True, skip_group_check=True),
                                       r=KT_ALL + [qres], w=[sbn], inc=(qq == 1 and j == 1))
                            pt = pT[st["pti"]]
                            op(ACT, lambda: eng[ACT].activation(out=pt[:], in_=sbk[:], func=AF.Exp), r=[sbn], w=["pT%d" % st["pti"]])

                        def emit_PV(st):
                            d, r_, g0, gsz = st["d"], st["r"], st["g0"], st["gsz"]
                            ob = banks[st["obi"]]
                            obn = "bank%d" % st["obi"]
                            pt = pT[st["pti"]]
                            ptn = "pT%d" % st["pti"]
                            for qq in range(2):
                                qt = st["q0"] + qq
                                for j in range(2):
                                    ti = st["tbase"] + r_ * (st["nqt"] + 1) + qt + j
                                    lhs = v3[:, ti, 0:2, :] if hh == 0 else v3[:, ti, 1:3, :]
                                    oq = (qt - g0)
                                    op(PE, lambda: eng[PE].matmul(ob[:, oq * 128:(oq + 1) * 128], lhs.rearrange("p a b -> p (a b)"),
                                                                   pt[:, (qq * 2 + j) * 128:(qq * 2 + j + 1) * 128], start=(j == 0), stop=(j == 1)),
                                       r=["v3_%d" % (ti // 8), ptn], w=[obn], inc=(qq == 1 and j == 1))
                            if st["glast"]:
                                a0 = 128 * g0 * d + r_
                                a1 = a0 + (128 * gsz - 1) * d + 1
                                aview = acc[:, a0:a1:d]
                                accr = ["acc_c%d" % cc for cc in range(a0 // CH, (a1 - 1) // CH + 1)]
                                if st["pi"] == 0:
                                    op(DVE, lambda: eng[DVE].tensor_copy(out=aview, in_=ob[:, 0:128 * gsz]), r=[obn], w=accr)
                                else:
                                    op(DVE, lambda: eng[DVE].tensor_tensor(out=aview, in0=aview, in1=ob[:, 0:128 * gsz], op=ALU.add), r=[obn] + accr, w=accr)

                        fills = []
                        for st in steps:
                            if st["fill"] is not None and (not fills or fills[-1] != st["fill"]):
                                fills.append(st["fill"])
                        fslot = {}

                        def emit_fill(f):
                            pi_, fi = f
                            d_ = PATTERNS[pi_][1]
                            slot = fillc[0] % 2
                            fillc[0] += 1
                            fslot[f] = slot
                            if d_ == 1:
                                src = qT[prow, fi * 1024:(fi + 1) * 1024]
                                dstq = qS[slot][prow, :]
                            elif d_ == 4:
                                src = qT[prow, fi:fi + 4 * 1023 + 1:4]
                                dstq = qS[slot][prow, :]
                            else:
                                src = qT[prow, :].rearrange("p (l r) -> p r l", r=16)[:, 4 * fi:4 * fi + 4, :]
                                dstq = qS[slot][prow, :].rearrange("p (r l) -> p r l", l=256)
                            op(DVE, lambda: eng[DVE].tensor_copy(out=dstq, in_=src), r=["qT_c%d" % cc for cc in range(NCH)], w=["qS%d" % slot])

                        orow = slice(64 - hh * 64, 128 - hh * 64)
                        for sl_ in range(2):
                            op(POOL, lambda: eng[POOL].memset(qS[sl_][orow, :], 0.0), w=["qS%d" % sl_])
                        nf = 0
                        if fills:
                            emit_fill(fills[0])
                            nf = 1
                        pend = []
                        curf = None
                        for st in steps:
                            if st["fill"] is not None and st["fill"] != curf:
                                curf = st["fill"]
                                if nf < len(fills):
                                    emit_fill(fills[nf])
                                    nf += 1
                            if st["fill"] is not None:
                                st["slot"] = fslot[st["fill"]]
                            if deferred and st["pi"] == 0 and st["q0"] == st["g0"]:
                                deferred.pop(0)()
                            emit_S(st)
                            pend.append(st)
                            if len(pend) > SKEW_A:
                                emit_PV(pend.pop(0))
                        while pend:
                            emit_PV(pend.pop(0))
                        if h + 1 < 8:
                            load_bias(h + 1)
                        nrow = slice(nbase, nbase + 64)
                        drow = slice(dbase, dbase + 64)

                        def norm_chunk(c, h=h, nrow=nrow, drow=drow):
                            cs = slice(c * CH, (c + 1) * CH)
                            s = ncnt[0] % 2
                            ncnt[0] += 1
                            an = "acc_c%d" % c
                            op(ACT, lambda: eng[ACT].activation(out=acc[drow, cs], in_=acc[drow, cs], func=AF.Ln), r=[an], w=[an])
                            op(ACT, lambda: eng[ACT].activation(out=acc[drow, cs], in_=acc[drow, cs], func=AF.Exp, scale=-1.0), r=[an], w=[an])
                            op(DVE, lambda: eng[DVE].tensor_copy(out=rdenA[s][nrow, :], in_=acc[drow, cs]), r=[an], w=["rdenA0"])
                            op(DVE, lambda: eng[DVE].tensor_tensor(out=ostA[s][nrow, :], in0=acc[nrow, cs], in1=rdenA[s][nrow, :], op=ALU.mult),
                               r=[an, "rdenA0"], w=["ostA%d" % s])
                            dma(SP, "ostA%d" % s, mixT[h * 64:(h + 1) * 64, cs], ostA[s][nrow, :], r=["ostA%d" % s], w=["mixT%d" % h])

                        if hh == 0:
                            deferred.extend([(lambda c=c, f=norm_chunk: f(c)) for c in range(NCH)])
                        else:
                            for c in range(NCH):
                                norm_chunk(c)
                T.barrier()

            cqn = botA[:].bitcast(BF16).rearrange("p (k t) -> p k t", k=2)
            ckvn = botB[:]
            kpe = botC[:, 0:S]
            with ExitStack() as es0:
                sq = [sb("sqb%d" % i, [128, CH], BF16, es0) for i in range(2)]
                rtmp = [sb("rtmpb%d" % i, [128, CH], F32, es0) for i in range(2)]
                rstd = [sb("rstdb%d" % i, [128, CH], F32, es0) for i in range(2)]
                lat = [sb("lat%d" % i, [128, 3, CH], F32, es0) for i in range(2)]
                kr1 = [sb("kr1_%d" % i, [128, CH], F32, es0) for i in range(2)]
                kr2 = [sb("kr2_%d" % i, [128, CH], F32, es0) for i in range(2)]
                csc = [sb("csc%d" % i, [128, 2, CH], F32, es0) for i in range(2)]

                def lat_proj(c):
                    cs = slice(c * CH, (c + 1) * CH)
                    s = c % 2
                    dma(SP, "csc%d" % s, csc[s][64:96, 0, :], cos_in[64:96, cs], w=["csc%d" % s])
                    dma(SP, "csc%d" % s, csc[s][64:96, 1, :], sin_in[64:96, cs], w=["csc%d" % s])
                    ur = ["uT%d_%d" % (kc, c) for kc in range(8)]
                    for i, col0 in enumerate((1536, 1664, 1792)):
                        b = banks[1 + i]
                        bn = "bank%d" % (1 + i)
                        for kc in range(8):
                            op(PE, lambda: eng[PE].matmul(b[:], win[:, kc, col0:col0 + 128], uT[:, kc, cs], start=(kc == 0), stop=(kc == 7)),
                               r=["win"] + ur, w=[bn], inc=(kc == 7))
                        op(ACT, lambda: eng[ACT].copy(out=lat[s][:, i, :], in_=b[:]), r=[bn], w=["lat%d_%d" % (s, i)])
                    for kc in range(8):
                        op(PE, lambda: eng[PE].matmul(banks[4][0:96, :], win[:, kc, 1856:1952], uT[:, kc, cs], start=(kc == 0), stop=(kc == 7)),
                           r=["win"] + ur, w=["bank4"], inc=(kc == 7))
                    for kc in range(8):
                        op(PE, lambda: eng[PE].matmul(banks[5][0:96, :], wkr_sw[:, kc, :], uT[:, kc, cs], start=(kc == 0), stop=(kc == 7)),
                           r=["wkr_sw"] + ur, w=["bank5"], inc=(kc == 7))
                    op(DVE, lambda: eng[DVE].tensor_tensor(out=kr1[s][64:96, :], in0=banks[4][64:96, :], in1=csc[s][64:96, 0, :], op=ALU.mult),
                       r=["bank4", "csc%d" % s], w=["kr1_%d" % s])
                    op(DVE, lambda: eng[DVE].tensor_tensor(out=kr2[s][64:96, :], in0=banks[5][64:96, :], in1=csc[s][64:96, 1, :], op=ALU.mult),
                       r=["bank5", "csc%d" % s], w=["kr2_%d" % s])
                    op(POOL, lambda: eng[POOL].tensor_tensor(out=kpe[64:96, cs], in0=kr1[s][64:96, :], in1=kr2[s][64:96, :], op=ALU.add),
                       r=["kr1_%d" % s, "kr2_%d" % s], w=["kpe_c%d" % c])

                def lat_norm(c):
                    cs = slice(c * CH, (c + 1) * CH)
                    s = c % 2
                    for i in range(3):
                        q = i % 2
                        op(ACT, lambda: eng[ACT].activation(out=sq[q][:], in_=lat[s][:, i, :], func=AF.Square), r=["lat%d_%d" % (s, i)], w=["sqb%d" % q])
                        bsel = banks[6] if i < 2 else banks[0]
                        bname = "bank6" if i < 2 else "bank0"
                        op(PE, lambda: eng[PE].matmul(bsel[:], ones[:], sq[q][:], start=(i != 1), stop=(i != 0)),
                           r=["ones", "sqb%d" % q], w=[bname])
                    for k2, (bsel, bname, nfeat, idxs) in enumerate(((banks[6], "bank6", 256, (0, 1)), (banks[0], "bank0", 128, (2,)))):
                        op(ACT, lambda: eng[ACT].activation(out=rtmp[k2][:], in_=bsel[:], func=AF.Ln, scale=1.0 / nfeat, bias=eps_t[:, 0:1]),
                           r=[bname, "eps"], w=["rtmpb%d" % k2])
                        op(ACT, lambda: eng[ACT].activation(out=rstd[k2][:], in_=rtmp[k2][:], func=AF.Exp, scale=-0.5), r=["rtmpb%d" % k2], w=["rstdb%d" % k2])
                        for i in idxs:
                            dstl = cqn[:, i, cs] if i < 2 else ckvn[:, cs]
                            gsc = gq[:, i:i + 1] if i < 2 else gkv[:, 0:1]
                            op(DVE, lambda: eng[DVE].scalar_tensor_tensor(out=dstl, in0=lat[s][:, i, :], scalar=gsc, in1=rstd[k2][:], op0=ALU.mult, op1=ALU.mult),
                               r=["lat%d_%d" % (s, i), "rstdb%d" % k2, "gq", "gkv"], w=[("cqn_c%d" if i < 2 else "ckvn_c%d") % c])

                for c in range(NCH):
                    lat_proj(c)
                    if c > 0:
                        lat_norm(c - 1)
                lat_norm(NCH - 1)
                T.barrier()
        if debug:
            with ExitStack() as es0:
                dst = sb("dstage2", [128, S], F32, es0)
                for i in range(2):
                    op(DVE, lambda: eng[DVE].tensor_copy(out=dst[:], in_=cqn[:, i, :]), r=["cqn_c%d" % cc for cc in range(NCH)], w=["dstage"])
                    dma(SP, "dbg", dbg["d_cq"][i * 128:(i + 1) * 128, :], dst[:], r=["dstage"])
                op(DVE, lambda: eng[DVE].tensor_copy(out=dst[:], in_=ckvn[:]), r=["ckvn_c%d" % cc for cc in range(NCH)], w=["dstage"])
                dma(SP, "dbg", dbg["d_ckv"][:, :], dst[:], r=["dstage"])
                op(DVE, lambda: eng[DVE].memset(dst[:], 0.0), w=["dstage"])
                op(DVE, lambda: eng[DVE].tensor_copy(out=dst[64:96, :], in_=kpe[64:96, :]), r=["kpe_c%d" % cc for cc in range(NCH)], w=["dstage"])
                dma(SP, "dbg", dbg["d_kpe"][:, :], dst[:], r=["dstage"])
                T.barrier()

        NWA = 4
        wo = sb("wo", [128, 8, D], BF16)
        wuA = sb("wuA", [128, 8, NWA * 512], BF16)
        wd_v = w_down.rearrange("(j p) o -> p j o", p=128)
        wu_v = w_up.rearrange("(k p) f -> p k f", p=128)
        with ExitStack() as eB:
            csB = [sb("csB%d" % i, [128, 2, CH], F32, eB) for i in range(2)]
            wq = sb("wq", [128, 2, 768], BF16, eB)
            wqs = sb("wqs", [128, 2, 768], BF16, eB)
            wkv = sb("wkv", [128, 2, 8, 64], BF16, eB)
            vall = sb("vall", [128, 32, 8, 128], BF16, eB)
            qh = [sb("qh%d" % i, [128, S], BF16, eB) for i in range(2)]
            kh = [sb("kh%d" % i, [128, S], BF16, eB) for i in range(2)]
            stq = qh[1][:, :].bitcast(F32)
            stk = kh[1][:, :].bitcast(F32)
            for kc in range(2):
                dma(SP, "c_wq", stq[:, kc * 768:(kc + 1) * 768], w_qb[kc * 128:(kc + 1) * 128, :], w=["qh1"])
            dma(SP, "c_wkv", stk[:, 0:1024], w_kvb, w=["kh1"])
            for kc in range(8):
                dma(POOL, "c_wo", wo[:, kc, :], w_out[kc * 128:(kc + 1) * 128, :], w=["wo"])
            for b in range(NWA):
                dma(POOL, "c_wu%d" % b, wuA[:, :, b * 512:(b + 1) * 512], wu_v[:, :, b * 512:(b + 1) * 512], w=["wu%d" % b])
            op(DVE, lambda: eng[DVE].memset(wqs[:], 0.0), w=["wqs"])
            op(DVE, lambda: eng[DVE].tensor_copy(out=wq[:].rearrange("p k c -> p (k c)"), in_=stq[:, 0:1536]), r=["qh1"], w=["wq"])
            op(DVE, lambda: eng[DVE].tensor_copy(out=wkv[:], in_=stk[:, 0:1024].rearrange("p (h t e) -> p t h e", t=2, e=64)), r=["kh1"], w=["wkv"])
            wq_v = wq[:].rearrange("p k (h e) -> p k h e", e=96)
            wqs_v = wqs[:].rearrange("p k (h e) -> p k h e", e=96)
            for kc in range(2):
                op(DVE, lambda: eng[DVE].tensor_scalar(out=wqs_v[:, kc, :, 64:80], in0=wq_v[:, kc, :, 80:96], scalar1=-1.0, scalar2=None, op0=ALU.mult),
                   r=["wq"], w=["wqs"])
                op(DVE, lambda: eng[DVE].tensor_copy(out=wqs_v[:, kc, :, 80:96], in_=wq_v[:, kc, :, 64:80]), r=["wq"], w=["wqs"])

            def build_vall(t):
                bi = 1 + t % 2
                op(PE, lambda: eng[PE].matmul(banks[bi][:], ckvn[:, t * 128:(t + 1) * 128], wkv[:, 1, :, :].rearrange("p h e -> p (h e)"), start=True, stop=True),
                   r=["ckvn_c%d" % (t // 4), "wkv"], w=["bank%d" % bi])
                srcv = banks[bi][:].rearrange("p (h e) -> p h e", e=64)
                op(DVE, lambda: eng[DVE].memset(vall[:, t, :, 64:128], 1.0), w=["vall1_%d" % t])
                op(ACT, lambda: eng[ACT].copy(out=vall[:, t, :, 0:64], in_=srcv), r=["bank%d" % bi], w=["vall_%d" % t])

            t1 = sb("t1", [128, CH], F32, eB)
            t2 = sb("t2", [128, CH], F32, eB)
            pB = [sb("pB%d" % i, [128, 512], BF16, eB) for i in range(3)]
            ostB = [sb("ostB%d" % i, [128, CH], BF16, eB) for i in range(2)]
            rdB = sb("rdB", [128, CH], F32, eB)
            scale_b = 96.0 ** -0.5
            print("phase B sbuf free:", nc.sbuf_bytes_remaining)
            pc = 0
            ncs = [0, 0]

            def emit_proj(h, c, pro=False):
                nonlocal pc
                s_ = h % 2
                qt_, kt_ = qh[s_], kh[s_]
                qn, kn = "qh%d" % s_, "kh%d" % s_
                cs = slice(c * CH, (c + 1) * CH)
                for kc in range(2):
                    op(PE, lambda: eng[PE].matmul(banks[4][0:96, :], wq[:, kc, h * 96:(h + 1) * 96], cqn[:, kc, cs], start=(kc == 0), stop=(kc == 1)),
                       r=["wq", "cqn_c%d" % c], w=["bank4"], inc=(kc == 1))
                for kc in range(2):
                    op(PE, lambda: eng[PE].matmul(banks[5][0:96, :], wqs[:, kc, h * 96:(h + 1) * 96], cqn[:, kc, cs], start=(kc == 0), stop=(kc == 1)),
                       r=["wqs", "cqn_c%d" % c], w=["bank5"], inc=(kc == 1))
                op(PE, lambda: eng[PE].matmul(bank7[0:64, :], wkv[:, 0, h, :], ckvn[:, cs], start=True, stop=True),
                   r=["wkv", "ckvn_c%d" % c], w=["bankT"])
                sl = ncs[0] % 2
                ncs[0] += 1
                dma(SP, "csB%d" % sl, csB[sl][64:96, 0, :], cos_in[64:96, cs], w=["csB%d" % sl])
                dma(SP, "csB%d" % sl, csB[sl][64:96, 1, :], sin_in[64:96, cs], w=["csB%d" % sl])
                if pro:
                    op(ACT, lambda: eng[ACT].copy(out=qt_[0:64, cs], in_=banks[4][0:64, :]), r=["bank4"], w=[qn])
                else:
                    op(DVE, lambda: eng[DVE].tensor_copy(out=qt_[0:64, cs], in_=banks[4][0:64, :]), r=["bank4"], w=[qn])
                op(DVE, lambda: eng[DVE].tensor_tensor(out=t1[64:96, :], in0=banks[4][64:96, :], in1=csB[sl][64:96, 0, :], op=ALU.mult),
                   r=["bank4", "csB%d" % sl], w=["t1"])
                op(DVE, lambda: eng[DVE].tensor_tensor(out=t2[64:96, :], in0=banks[5][64:96, :], in1=csB[sl][64:96, 1, :], op=ALU.mult),
                   r=["bank5", "csB%d" % sl], w=["t2"])
                op(DVE, lambda: eng[DVE].tensor_tensor(out=qt_[64:96, cs], in0=t1[64:96, :], in1=t2[64:96, :], op=ALU.add),
                   r=["t1", "t2"], w=[qn])
                if pro:
                    op(ACT, lambda: eng[ACT].copy(out=kt_[0:64, cs], in_=bank7[0:64, :]), r=["bankT"], w=[kn])
                else:
                    op(DVE, lambda: eng[DVE].tensor_copy(out=kt_[0:64, cs], in_=bank7[0:64, :]), r=["bankT"], w=[kn])
                if c == NCH - 1:
                    op(DVE, lambda: eng[DVE].tensor_copy(out=kt_[64:96, :], in_=kpe[64:96, :]), r=["kpe_c%d" % cc for cc in range(NCH)], w=[kn])

            def emit_QK(st):
                nonlocal pc
                h, c, kt = st["h"], st["c"], st["kt"]
                s_ = h % 2
                cs = slice(c * CH, (c + 1) * CH)
                sbi = 1 + (pc % 3)
                pc += 1
                st["pti"] = ncs[1] % 3
                ncs[1] += 1
                op(PE, lambda: eng[PE].matmul(banks[sbi][:], kh[s_][0:96, kt * 128:(kt + 1) * 128], qh[s_][0:96, cs], start=True, stop=True),
                   r=["kh%d" % s_, "qh%d" % s_], w=["bank%d" % sbi])
                pt = pB[st["pti"]]
                op(ACT, lambda: eng[ACT].activation(out=pt[:], in_=banks[sbi][:], func=AF.Exp, scale=scale_b), r=["bank%d" % sbi], w=["pB%d" % st["pti"]])

            def emit_PVB(st):
                h, c, kt = st["h"], st["c"], st["kt"]
                cs = slice(c * CH, (c + 1) * CH)
                obi = 0 if c % 2 == 0 else 6
                ob = banks[obi]
                obn = "bank%d" % obi
                pt = pB[st["pti"]]
                op(PE, lambda: eng[PE].matmul(ob[:], vall[:, kt, h, :], pt[:], start=(kt == 0), stop=(kt == 31)),
                   r=["vall_%d" % kt, "vall1_%d" % kt, "pB%d" % st["pti"]], w=[obn], inc=(kt == 31))
                if kt == 31:
                    so = c % 2
                    op(DVE, lambda: eng[DVE].reciprocal(out=rdB[0:64, :], in_=ob[64:128, :]), r=[obn], w=["rdB"])
                    op(DVE, lambda: eng[DVE].tensor_tensor(out=ostB[so][0:64, :], in0=ob[0:64, :], in1=rdB[0:64, :], op=ALU.mult),
                       r=[obn, "rdB"], w=["ostB%d" % so])
                    dma(SP, "ostB%d" % so, mixT[512 + h * 64:512 + (h + 1) * 64, cs], ostB[so][0:64, :], r=["ostB%d" % so], w=["mixT%d" % (8 + h)])

            for c in range(NCH):
                emit_proj(0, c, pro=True)
                for t in range(4 * c, 4 * c + 4):
                    build_vall(t)
            pend = []
            for h in range(8):
                for c in range(NCH):
                    if h + 1 < 8:
                        emit_proj(h + 1, c)
                    for kt in range(32):
                        st = dict(h=h, c=c, kt=kt)
                        emit_QK(st)
                        pend.append(st)
                        if len(pend) > SKEW:
                            emit_PVB(pend.pop(0))
            while pend:
                emit_PVB(pend.pop(0))
            T.barrier()

        mix_all = ["mixT%d" % i for i in range(16)]
        if debug:
            with ExitStack() as eD:
                dm = sb("dm", [128, S], BF16, eD)
                dmf = sb("dmf", [128, S], F32, eD)
                for kc in range(8):
                    dma(SP, "dm", dm[:], mixT[kc * 128:(kc + 1) * 128, :], r=mix_all, w=["dm"])
                    op(DVE, lambda: eng[DVE].tensor_copy(out=dmf[:], in_=dm[:]), r=["dm"], w=["dmf"])
                    dma(SP, "dmo", dbg["d_mix"][kc * 128:(kc + 1) * 128, :], dmf[:], r=["dmf"])
                T.barrier()

        with ExitStack() as eC:
            wuB = sb("wuB", [128, 8, (8 - NWA) * 512], BF16, eC)
            wd = sb("wd", [128, 32, D], BF16, eC)
            gm = sb("gm", [128, 8], F32, eC)
            gf = sb("gf", [128, 8], F32, eC)
            dma(SP, "c_gm", gm[:], g_mlp, w=["gm"])
            dma(SP, "c_gf", gf[:], g_fin, w=["gf"])
            def load_wd(j4, after=()):
                dma(POOL, "c_wd%d" % j4, wd[:, 4 * j4:4 * j4 + 4, :], wd_v[:, 4 * j4:4 * j4 + 4, :], r=list(after), w=["wd%d" % j4])

            def load_rest_weights(after):
                load_wd(0, after)
                load_wd(1)
                for b in range(NWA, 8):
                    dma(POOL, "c_wu%d" % b, wuB[:, :, (b - NWA) * 512:(b - NWA + 1) * 512], wu_v[:, :, b * 512:(b + 1) * 512], w=["wu%d" % b])
                for j4 in range(2, 8):
                    load_wd(j4)

            def wu_cols(kc, j):
                if j // 4 < NWA:
                    return wuA[:, kc, j * 128:(j + 1) * 128]
                return wuB[:, kc, (j - NWA * 4) * 128:(j - NWA * 4 + 1) * 128]

            def wd_rows(j, oc):
                return wd[:, j, oc * 128:(oc + 1) * 128]

            hxs = [botA[:].rearrange("p (k t) -> p k t", k=8), sb("hxB", [128, 8, CH], F32, eC)[:]]
            mu = botB[:].rearrange("p (k t) -> p k t", k=8)
            act = botC[:, 0:S].rearrange("p (k t) -> p k t", k=8)
            sqc = [botC[:, S + i * CH:S + (i + 1) * CH] for i in range(2)]
            rl = [sb("rl%d" % i, [128, CH], F32, eC) for i in range(2)]
            rt = sb("rtC", [128, CH], F32, eC)
            rs = sb("rsC", [128, CH], F32, eC)
            bc = 0
            print("phase C sbuf free:", nc.sbuf_bytes_remaining)

            def hn(c, k):
                return "hx%d_%d" % (c % 2, k)

            rt2 = rt
            rs2 = sb("rsC2", [128, CH], F32, eC)
            sqn = [0]

            def stat_part(c, kc):
                hx = hxs[c % 2]
                q = sqn[0] % 2
                sqn[0] += 1
                op(ACT, lambda: eng[ACT].activation(out=sqc[q], in_=hx[:, kc, :], func=AF.Square), r=[hn(c, kc)], w=["sqc%d" % q])
                op(PE, lambda: eng[PE].matmul(banks[0][:], ones[:], sqc[q], start=(kc == 0), stop=(kc == 7)),
                   r=["ones", "sqc%d" % q], w=["bank0"])

            def stat_fin(which):
                rt_, rs_, nm = (rt, rs, "") if which == 1 else (rt2, rs2, "2")
                op(ACT, lambda: eng[ACT].activation(out=rt_[:], in_=banks[0][:], func=AF.Ln, scale=1.0 / D, bias=eps_t[:, 0:1]),
                   r=["bank0", "eps"], w=["rtC"])
                op(ACT, lambda: eng[ACT].activation(out=rs_[:], in_=rt_[:], func=AF.Exp, scale=-0.5), r=["rtC"], w=["rsC" + nm])

            def load_x(c):
                cs_ = slice(c * CH, (c + 1) * CH)
                for oc in range(8):
                    dma(SP, "hxl%d_%d" % (c % 2, oc), hxs[c % 2][:, oc, :], xT_v[:, oc, cs_], w=[hn(c, oc)])

            def load_mc(c):
                cs_ = slice(c * CH, (c + 1) * CH)
                dma(SP, "mc", mu, mixT_v[:, :, cs_], r=mix_all, w=["mu%d" % k for k in range(8)])

            UPB = [(banks[1][:], "bank1"), (banks[2][:], "bank2"), (banks[3][:], "bank3"), (bank7, "bankT")]
            DNB = [(banks[4][:], "bank4"), (banks[5][:], "bank5"), (banks[6][:], "bank6")]
            dc = 0

            def out_proj(c):
                nonlocal bc
                hx = hxs[c % 2]
                for oc in range(8):
                    bap, bnm = UPB[bc % 4]; bc += 1
                    for kc in range(8):
                        op(PE, lambda: eng[PE].matmul(bap, wo[:, kc, oc * 128:(oc + 1) * 128], mu[:, kc, :], start=(kc == 0), stop=(kc == 7)),
                           r=["wo", "mu%d" % kc], w=[bnm], inc=(kc == 7))
                    op(DVE, lambda: eng[DVE].tensor_tensor(out=hx[:, oc, :], in0=hx[:, oc, :], in1=bap, op=ALU.add),
                       r=[bnm, hn(c, oc)], w=[hn(c, oc)])
                    if oc >= 1:
                        stat_part(c, oc - 1)
                stat_part(c, 7)
                stat_fin(1)

            def final_norm(c):
                cs_ = slice(c * CH, (c + 1) * CH)
                hx = hxs[c % 2]
                for kc in range(8):
                    op(DVE, lambda: eng[DVE].scalar_tensor_tensor(out=hx[:, kc, :], in0=hx[:, kc, :], scalar=gf[:, kc:kc + 1], in1=rs2[:], op0=ALU.mult, op1=ALU.mult),
                       r=[hn(c, kc), "rsC2", "gf"], w=[hn(c, kc)])
                    dma(SP, "outst%d_%d" % (c % 2, kc), outT_v[:, kc, cs_], hx[:, kc, :], r=[hn(c, kc)], w=["out"])

            load_mc(0)
            load_x(0)
            load_rest_weights([hn(0, k) for k in range(8)] + ["mu%d" % k for k in range(8)])
            out_proj(0)
            load_x(1)
            for c in range(NCH):
                hx = hxs[c % 2]
                for kc in range(8):
                    op(DVE, lambda: eng[DVE].scalar_tensor_tensor(out=mu[:, kc, :], in0=hx[:, kc, :], scalar=gm[:, kc:kc + 1], in1=rs[:], op0=ALU.mult, op1=ALU.mult),
                       r=[hn(c, kc), "rsC", "gm"], w=["mu%d" % kc])
                for qf in range(4):
                    for jj in range(8):
                        j = qf * 8 + jj
                        bap, bnm = UPB[bc % 4]; bc += 1
                        for kc in range(8):
                            op(PE, lambda: eng[PE].matmul(bap, wu_cols(kc, j), mu[:, kc, :], start=(kc == 0), stop=(kc == 7)),
                               r=["wu%d" % (j // 4), "mu%d" % kc], w=[bnm], inc=(kc == 7))
                        q = j % 2
                        op(DVE, lambda: eng[DVE].tensor_scalar(out=rl[q][:], in0=bap, scalar1=0.0, scalar2=None, op0=ALU.max),
                           r=[bnm], w=["rl%d" % q])
                        op(ACT, lambda: eng[ACT].activation(out=act[:, jj, :], in_=rl[q][:], func=AF.Square), r=["rl%d" % q], w=["act%d" % jj])
                    if qf == 3 and c + 1 < NCH:
                        load_mc(c + 1)
                    for oc in range(8):
                        bap, bnm = DNB[dc % 3]; dc += 1
                        for jj in range(8):
                            j = qf * 8 + jj
                            op(PE, lambda: eng[PE].matmul(bap, wd_rows(j, oc), act[:, jj, :], start=(jj == 0), stop=(jj == 7)),
                               r=["wd%d" % (j // 4), "act%d" % jj], w=[bnm], inc=(jj == 7))
                        op(DVE, lambda: eng[DVE].tensor_tensor(out=hx[:, oc, :], in0=hx[:, oc, :], in1=bap, op=ALU.add),
                           r=[bnm, hn(c, oc)], w=[hn(c, oc)])
                        if qf == 3 and oc >= 1:
                            stat_part(c, oc - 1)
                stat_part(c, 7)
                stat_fin(2)
                if c + 1 < NCH:
                    out_proj(c + 1)
                final_norm(c)
                if c + 2 < NCH:
                    load_x(c + 2)

        T.barrier()
        print("instructions:", dict(T.cnt), "waits:", T.nwaits)
    return nc


_CACHE = {}


def _host_constants(rel_bias):
    rb = np.asarray(rel_bias, np.float32)
    i = np.arange(128)[:, None]
    m = np.arange(128)[None, :]
    out = np.full((8, 128, 7, 4, 128), NEG, np.float32)

    def tile(d, j, edge, h):
        rel = -64 + 128 * j + i - m
        v = np.abs(rel) <= 64
        if edge and j == 0:
            v = v & (i >= 64)
        if edge and j == 1:
            v = v & (i < 64)
        g = rb[t5_buckets(rel * d), h]
        return np.where(v, g, np.float32(NEG))

    for h in range(8):
        for pi, (win_, d) in enumerate(PATTERNS):
            if d == 16:
                combos = [(6, True, True)]
            else:
                combos = [(pi * 3 + 0, True, False), (pi * 3 + 1, False, False), (pi * 3 + 2, False, True)]
            for (ci, first, last) in combos:
                out[h, :, ci, 0, :] = tile(d, 0, first, h)
                out[h, :, ci, 1, :] = tile(d, 1, False, h)
                out[h, :, ci, 2, :] = tile(d, 0, False, h)
                out[h, :, ci, 3, :] = tile(d, 1, last, h)
    return out.reshape(8, 128, 7 * 512)


def _rope_tables():
    inv_freq = (np.float32(10000.0) ** (-np.arange(0, 32, 2, dtype=np.float32) / np.float32(32))).astype(np.float32)
    pos = np.arange(S, dtype=np.float32)
    fr = (pos[:, None] * inv_freq[None, :]).astype(np.float32)
    cos = np.cos(fr).astype(np.float32).T
    sin = np.sin(fr).astype(np.float32).T
    cT = np.ones((128, S), np.float32)
    sT = np.zeros((128, S), np.float32)
    cT[64:80] = cos; cT[80:96] = cos
    sT[64:80] = sin; sT[80:96] = sin
    return cT, sT


def _lay(v, k):
    return np.ascontiguousarray(np.asarray(v, np.float32).reshape(k, 128).T)


def kernel(x, mix_norm_g, w_in, q_norm_g, w_q_b, kv_norm_g, w_kv_b, w_out, mlp_norm_g, w_up, w_down,
           rel_bias, final_norm_g, _debug=False, _cores=8):
    x = np.asarray(x, np.float32)
    key = ("nc", _debug)
    if key not in _CACHE:
        _CACHE[key] = build_program(debug=_debug)
    nc = _CACHE[key]
    cT, sT = _rope_tables()
    shared = {
        "w_in": np.ascontiguousarray(np.asarray(w_in, np.float32)[0]),
        "g_mix": _lay(mix_norm_g, 8),
        "w_qb": np.ascontiguousarray(np.asarray(w_q_b, np.float32)[0]),
        "g_q": _lay(q_norm_g, 2),
        "w_kvb": np.ascontiguousarray(np.asarray(w_kv_b, np.float32)[0]),
        "g_kv": _lay(kv_norm_g, 1),
        "w_out": np.ascontiguousarray(np.asarray(w_out, np.float32)[0]),
        "g_mlp": _lay(mlp_norm_g, 8),
        "w_up": np.ascontiguousarray(np.asarray(w_up, np.float32)[0]),
        "w_down": np.ascontiguousarray(np.asarray(w_down, np.float32)[0]),
        "g_fin": _lay(final_norm_g, 8),
        "biasT": _host_constants(rel_bias),
        "ident": np.eye(128, dtype=np.float32),
        "cosT": cT,
        "sinT": sT,
    }
    in_maps = []
    for b in range(_cores):
        m = dict(shared)
        m["xT"] = np.ascontiguousarray(x[b].T)
        in_maps.append(m)
    res = run_bass_kernel_spmd(nc, in_maps, core_ids=list(range(_cores)))
    if _debug:
        return res.results
    out = np.stack([np.ascontiguousarray(r["outT"].T) for r in res.results], axis=0)
    return out.astype(np.float32)
```

```python
import numpy as np
from contextlib import ExitStack
import concourse.bass as bass
import concourse.mybir as mybir
from concourse.bass_utils import run_bass_kernel_spmd

F32 = mybir.dt.float32
BF16 = mybir.dt.bfloat16
ALU = mybir.AluOpType
AF = mybir.ActivationFunctionType

S = 4096
D = 1024
NCH = 8
CH = 512
INC = 1952
PAD = 1024
NEG = -30000.0
EPS = 1e-6
PATTERNS = ((128, 1), (512, 4), (2048, 16))
SKEW = 2
SKEW_A = 3
DFF = 4096


class Tracker:
    def __init__(self, nc, es):
        self.nc = nc
        self.es = es
        self.engs = {"pe": nc.tensor, "act": nc.scalar, "dve": nc.vector, "pool": nc.gpsimd, "sp": nc.sync}
        self.sems = {}
        self.cnt = {}
        for k in ("pe", "act", "dve", "pool"):
            self.sems[k] = es.enter_context(nc.semaphore("sem_" + k))
            self.cnt[k] = 0
        self.seen = {e: {} for e in self.engs}
        self.lastw = {}
        self.readers = {}
        self.nwaits = 0

    def chan(self, name):
        if name not in self.sems:
            self.sems[name] = self.es.enter_context(self.nc.semaphore("dma_" + name))
            self.cnt[name] = 0
        return name

    def _deps(self, eng, r, w):
        deps = {}

        def add(key, val, raw):
            if key == eng and not raw and eng in ("pe", "sp"):
                return
            if deps.get(key, 0) < val:
                deps[key] = val

        for res in r:
            lw = self.lastw.get(res)
            if lw is not None:
                add(lw[0], lw[1], True)
        for res in w:
            lw = self.lastw.get(res)
            if lw is not None:
                add(lw[0], lw[1], False)
            for k, v in self.readers.get(res, {}).items():
                add(k, v, False)
        return deps

    def _emit_waits(self, eng, deps):
        seen = self.seen[eng]
        for key, val in deps.items():
            if key not in ("pe", "act", "dve", "pool"):
                val = self.cnt[key]
            if seen.get(key, 0) < val:
                self.engs[eng].wait_ge(self.sems[key], val)
                seen[key] = val
                self.nwaits += 1

    def _update(self, key, seq, r, w):
        for res in w:
            self.lastw[res] = (key, seq)
            self.readers[res] = {}
        for res in r:
            d = self.readers.setdefault(res, {})
            if d.get(key, 0) < seq:
                d[key] = seq

    def op(self, eng, fn, r=(), w=(), inc=True):
        self._emit_waits(eng, self._deps(eng, r, w))
        ins = fn()
        if inc:
            self.cnt[eng] += 1
            ins.then_inc(self.sems[eng], 1)
            seq = self.cnt[eng]
        else:
            seq = self.cnt[eng] + 1
        self._update(eng, seq, r, w)
        return ins

    def dma(self, q, ch, out, in_, r=(), w=()):
        self.chan(ch)
        if q == "pool":
            self.swq = getattr(self, "swq", [])
            while len(self.swq) >= 3:
                k, v = self.swq.pop(0)
                if self.seen[q].get(k, 0) < v:
                    self.engs[q].wait_ge(self.sems[k], v)
                    self.seen[q][k] = v
            self.swq.append((ch, self.cnt[ch] + 16))
        self._emit_waits(q, self._deps(q, r, w))
        self.engs[q].dma_start(out=out, in_=in_).then_inc(self.sems[ch], 16)
        self.cnt[ch] += 16
        self._update(ch, self.cnt[ch], r, w)

    def barrier(self):
        keys = list(self.cnt.keys())
        for e in self.engs:
            self.wait_all(e, [k for k in keys if k != "sp"])

    def wait_all(self, eng, keys):
        for k in keys:
            if self.cnt[k] > 0 and self.seen[eng].get(k, 0) < self.cnt[k]:
                self.engs[eng].wait_ge(self.sems[k], self.cnt[k])
                self.seen[eng][k] = self.cnt[k]


def t5_buckets(rel):
    nb = 16
    max_exact = 8
    ret = (rel > 0).astype(np.int32) * nb
    n = np.abs(rel)
    large = max_exact + (np.log(np.maximum(n, 1) / max_exact) / np.log(1024 / max_exact) * (nb - max_exact)).astype(np.int32)
    large = np.minimum(large, nb - 1)
    return (ret + np.where(n < max_exact, n, large)).astype(np.int32)


def build_program(debug=False):
    nc = bass.Bass("TRN2", target_bir_lowering=False)

    def din(name, shape, dt=F32):
        return nc.dram_tensor(name, list(shape), dt, kind="ExternalInput").ap()

    xT = din("xT", [D, S])
    w_in = din("w_in", [D, INC])
    g_mix = din("g_mix", [128, 8])
    w_qb = din("w_qb", [256, 768])
    g_q = din("g_q", [128, 2])
    w_kvb = din("w_kvb", [128, 1024])
    g_kv = din("g_kv", [128, 1])
    w_out = din("w_out", [D, D])
    g_mlp = din("g_mlp", [128, 8])
    w_up = din("w_up", [D, DFF])
    w_down = din("w_down", [DFF, D])
    g_fin = din("g_fin", [128, 8])
    biasT = din("biasT", [8, 128, 7 * 512])
    ident_in = din("ident", [128, 128])
    cos_in = din("cosT", [128, S])
    sin_in = din("sinT", [128, S])
    outT = nc.dram_tensor("outT", [D, S], F32, kind="ExternalOutput").ap()
    mixT = nc.dram_tensor("mixT_scratch", [D, S], BF16).ap()
    dbg = {}
    if debug:
        for nm, shp in (("d_uT", [D, S]), ("d_cq", [256, S]), ("d_ckv", [128, S]), ("d_kpe", [128, S]),
                        ("d_mix", [D, S])):
            dbg[nm] = nc.dram_tensor(nm, shp, F32, kind="ExternalOutput").ap()

    with ExitStack() as es:
        T = Tracker(nc, es)
        op, dma = T.op, T.dma
        PE, ACT, DVE, POOL, SP = "pe", "act", "dve", "pool", "sp"
        eng = T.engs

        def sb(name, shape, dt, stack=es):
            return stack.enter_context(nc.sbuf_tensor("sb_" + name, list(shape), dt))

        banks = [es.enter_context(nc.psum_tensor("bank%d" % i, [128, 512], F32)) for i in range(7)]
        bankT = es.enter_context(nc.psum_tensor("bankT", [128, 1024], BF16))
        bank7 = bankT[:].bitcast(F32)

        ident = sb("ident", [128, 128], BF16)
        ones = sb("ones", [128, 128], BF16)
        dma(POOL, "c_id", ident[:], ident_in, w=["ident"])
        op(POOL, lambda: eng[POOL].memset(ones[:], 1.0), w=["ones"])

        gq = sb("gq", [128, 2], F32)
        gkv = sb("gkv", [128, 1], F32)
        dma(SP, "c_gq", gq[:], g_q, w=["gq"])
        dma(SP, "c_gkv", gkv[:], g_kv, w=["gkv"])
        eps_t = sb("eps_t", [128, 1], F32)
        op(POOL, lambda: eng[POOL].memset(eps_t[:], EPS), w=["eps"])
        xT_v = xT.rearrange("(k p) t -> p k t", p=128)
        mixT_v = mixT.rearrange("(k p) t -> p k t", p=128)
        outT_v = outT.rearrange("(k p) t -> p k t", p=128)

        botA = sb("botA", [128, S], F32)
        botB = sb("botB", [128, S], BF16)
        botC = sb("botC", [128, S + 2 * PAD], BF16)
        with ExitStack() as esA:
            uT = sb("uT", [128, 8, S], BF16, esA)
            win = sb("win", [128, 8, INC], BF16, esA)
            wkr_sw = sb("wkr_sw", [128, 8, 96], BF16, esA)
            gmix = sb("gmix", [128, 8], F32, esA)
            dma(SP, "c_gmix", gmix[:], g_mix, w=["gmix"])
            op(POOL, lambda: eng[POOL].memset(wkr_sw[:], 0.0), w=["wkr_sw"])

            for kc in range(8):
                dma(POOL, "c_win", win[:, kc, :], w_in[kc * 128:(kc + 1) * 128, :], w=["win"])
            for kc in range(8):
                op(POOL, lambda: eng[POOL].tensor_scalar(out=wkr_sw[:, kc, 64:80], in0=win[:, kc, 1936:1952], scalar1=-1.0, scalar2=None, op0=ALU.mult),
                   r=["win"], w=["wkr_sw"])
                op(POOL, lambda: eng[POOL].tensor_copy(out=wkr_sw[:, kc, 80:96], in_=win[:, kc, 1920:1936]), r=["win"], w=["wkr_sw"])
            with ExitStack() as es0:
                xs = [sb("xs%d" % i, [128, 8, CH], F32, es0) for i in range(2)]
                sq = [sb("sq%d" % i, [128, CH], BF16, es0) for i in range(2)]
                rtmp = sb("rtmp", [128, CH], F32, es0)
                rstd = sb("rstd", [128, CH], F32, es0)
                for c in range(NCH):
                    s = c % 2
                    cs = slice(c * CH, (c + 1) * CH)
                    xres = "xs%d" % s
                    dma(SP, xres, xs[s][:], xT_v[:, :, cs], w=[xres])
                    for kc in range(8):
                        q = kc % 2
                        op(ACT, lambda: eng[ACT].activation(out=sq[q][:], in_=xs[s][:, kc, :], func=AF.Square), r=[xres], w=["sq%d" % q])
                        op(PE, lambda: eng[PE].matmul(banks[0][:], ones[:], sq[q][:], start=(kc == 0), stop=(kc == 7)),
                           r=["ones", "sq%d" % q], w=["bank0"])
                    op(ACT, lambda: eng[ACT].activation(out=rtmp[:], in_=banks[0][:], func=AF.Ln, scale=1.0 / D, bias=eps_t[:, 0:1]),
                       r=["bank0", "eps"], w=["rtmp"])
                    op(ACT, lambda: eng[ACT].activation(out=rstd[:], in_=rtmp[:], func=AF.Exp, scale=-0.5), r=["rtmp"], w=["rstd"])
                    for kc in range(8):
                        op(DVE, lambda: eng[DVE].scalar_tensor_tensor(out=uT[:, kc, cs], in0=xs[s][:, kc, :], scalar=gmix[:, kc:kc + 1], in1=rstd[:], op0=ALU.mult, op1=ALU.mult),
                           r=[xres, "rstd", "gmix"], w=["uT%d_%d" % (kc, c)])
                if debug:
                    dst = sb("dstage", [128, S], F32, es0)
                    for kc in range(8):
                        op(DVE, lambda: eng[DVE].tensor_copy(out=dst[:], in_=uT[:, kc, :]), r=["uT%d_%d" % (kc, c) for c in range(NCH)], w=["dstage"])
                        dma(SP, "dbg", dbg["d_uT"][kc * 128:(kc + 1) * 128, :], dst[:], r=["dstage"])
                T.barrier()

            with ExitStack() as e1:
                qT = botB
                kT = sb("kT", [128, S + 2 * PAD], BF16, e1)
                vT = botC
                NT3 = 33 + 36 + 48
                v3 = sb("v3", [128, NT3, 3, 64], BF16, e1)
                acc = botA
                bia = sb("bia", [128, 7, 512], BF16, e1)

                def load_bias(hb):
                    for (c0, c1, pn) in ((0, 3, 0), (3, 6, 1), (6, 7, 2)):
                        dma(POOL, "bia_p%d" % pn, bia[:, c0:c1, :], biasT[hb][:, c0 * 512:c1 * 512].rearrange("p (a b) -> p a b", b=512), w=["bia_p%d" % pn])
                pT = [sb("pT%d" % i, [128, 512], BF16, e1) for i in range(4)]
                qS = [sb("qS%d" % i, [128, 1024], BF16, e1) for i in range(2)]
                fillc = [0]
                ostA = [sb("ostA%d" % i, [128, CH], BF16, e1) for i in range(2)]
                rdenA = [sb("rdenA0", [128, CH], F32, e1)] * 2
                op(POOL, lambda: eng[POOL].memset(kT[:, 0:PAD], 0.0), w=["kT_pad"])
                op(POOL, lambda: eng[POOL].memset(kT[:, PAD + S:], 0.0), w=["kT_pad2"])
                op(POOL, lambda: eng[POOL].memset(vT[:, 0:PAD], 0.0), w=["vT_pad"])
                op(POOL, lambda: eng[POOL].memset(vT[:, PAD + S:], 0.0), w=["vT_pad2"])
                op(POOL, lambda: eng[POOL].memset(v3[:, :, 1, :], 1.0), w=["v3_%d" % bb for bb in range(15)])
                print("phase A sbuf free:", nc.sbuf_bytes_remaining)
                KT_ALL = ["kT_c%d" % cc for cc in range(NCH)] + ["kT_pad", "kT_pad2"]
                tiles = []
                for (win_, d) in PATTERNS:
                    L = S // d
                    for r_ in range(d):
                        for jp in range(L // 128 + 1):
                            tiles.append((PAD + (-64 + 128 * jp) * d + r_, d))
                assert len(tiles) == NT3
                deferred = []
                ncnt = [0]
                ev = 0
                pcount = 0
                ocount = 0
                ncount = 0
                for hp in range(4):
                    def proj_chunk(which, col0, c):
                        nonlocal ev
                        if deferred:
                            deferred.pop(0)()
                        cs = slice(c * CH, (c + 1) * CH)
                        bi = 1 + (ev % 3)
                        ev += 1
                        b = banks[bi]
                        bn = "bank%d" % bi
                        for kc in range(8):
                            op(PE, lambda: eng[PE].matmul(b[:], win[:, kc, col0:col0 + 128], uT[:, kc, cs], start=(kc == 0), stop=(kc == 7)),
                               r=["win", "uT%d_%d" % (kc, c)], w=[bn], inc=(kc == 7))
                        if which == "q":
                            op(ACT, lambda: eng[ACT].mul(out=qT[:, cs], in_=b[:], mul=0.125), r=[bn], w=["qT_c%d" % c])
                        elif which == "k":
                            op(DVE, lambda: eng[DVE].tensor_copy(out=kT[:, PAD + c * CH:PAD + (c + 1) * CH], in_=b[:]), r=[bn], w=["kT_c%d" % c])
                        else:
                            op(ACT, lambda: eng[ACT].copy(out=vT[:, PAD + c * CH:PAD + (c + 1) * CH], in_=b[:]), r=[bn], w=["vT_c%d" % c])

                    def transpose_batch(t0):
                        n = min(8, NT3 - t0)
                        for i in range(n):
                            off, d = tiles[t0 + i]
                            src = vT[:, off:off + 127 * d + 1:d]
                            op(PE, lambda: eng[PE].transpose(bankT[:, i * 128:(i + 1) * 128], src, ident[:]),
                               r=["vT_c%d" % cc for cc in range(NCH)] + ["vT_pad", "vT_pad2", "ident"], w=["bankT"], inc=(i == n - 1))
                        src_v = bankT[:, 0:n * 128].rearrange("p (t h e) -> p t h e", h=2, e=64)
                        op(DVE, lambda: eng[DVE].tensor_copy(out=v3[:, t0:t0 + n, 0:3:2, :], in_=src_v), r=["bankT"], w=["v3_%d" % (t0 // 8)])

                    for c in range(NCH):
                        proj_chunk("v", 1024 + hp * 128, c)
                    tb = list(range(0, NT3, 8))
                    qk = [("k", 512 + hp * 128, c) for c in range(NCH)] + [("q", hp * 128, c) for c in range(NCH)]
                    for i, (which, col0, c) in enumerate(qk):
                        proj_chunk(which, col0, c)
                        if i < len(tb):
                            transpose_batch(tb[i])
                    for i in range(len(qk), len(tb)):
                        transpose_batch(tb[i])
                    for hh in range(2):
                        h = 2 * hp + hh
                        prow = slice(hh * 64, hh * 64 + 64)
                        nbase, dbase = (0, 64) if hh == 0 else (64, 0)
                        if h == 0:
                            load_bias(0)
                        steps = []
                        tbase = 0
                        for pi, (win_, d) in enumerate(PATTERNS):
                            L = S // d
                            nqt = L // 128
                            gsz = min(4, nqt)
                            for r_ in range(d):
                                for g0 in range(0, nqt, gsz):
                                    obi = 4 + (ocount % 2)
                                    ocount += 1
                                    for q0 in range(g0, g0 + gsz, 2):
                                        first = (q0 == 0)
                                        last = (q0 + 1 == nqt - 1)
                                        if d == 16:
                                            combo = 6
                                        else:
                                            combo = pi * 3 + (0 if first else (2 if last else 1))
                                        fill = (pi, q0 // 8) if d == 1 else (pi, r_ if d == 4 else r_ // 4)
                                        steps.append(dict(pi=pi, d=d, r=r_, g0=g0, gsz=gsz, q0=q0, nqt=nqt, tbase=tbase, obi=obi,
                                                          combo=combo, glast=(q0 + 2 >= g0 + gsz), fill=fill))
                            tbase += d * (nqt + 1)

                        def emit_S(st):
                            nonlocal pcount
                            sbi = pcount % 4
                            st["sbi"] = sbi
                            st["pti"] = pcount % 4
                            pcount += 1
                            sbk = banks[sbi]
                            sbn = "bank%d" % sbi
                            d, r_ = st["d"], st["r"]
                            op(PE, lambda: eng[PE].matmul(sbk[:], ident[:], bia[:, st["combo"], :], start=True, stop=False, skip_group_check=True),
                               r=["ident", "bia_p%d" % st["pi"]], w=[sbn], inc=False)
                            for qq in range(2):
                                qt = st["q0"] + qq
                                slot = st["slot"]
                                if d == 1:
                                    qo = 128 * (qt % 8)
                                elif d == 4:
                                    qo = 128 * qt
                                else:
                                    qo = (r_ % 4) * 256 + 128 * qt
                                qap = qS[slot][:, qo:qo + 128]
                                qres = "qS%d" % slot
                                for j in range(2):
                                    l0 = 128 * qt - 64 + 128 * j
                                    koff = PAD + l0 * d + r_
                                    kap = kT[:, koff:koff + 127 * d + 1:d]
                                    dst = sbk[:, (qq * 2 + j) * 128:(qq * 2 + j + 1) * 128]
                                    op(PE, lambda: eng[PE].matmul(dst, kap, qap, start=False, stop=True, skip_group_check=True),
                                       r=KT_ALL + [qres], w=[sbn], inc=(qq == 1 and j == 1))
                            pt = pT[st["pti"]]
                            op(ACT, lambda: eng[ACT].activation(out=pt[:], in_=sbk[:], func=AF.Exp), r=[sbn], w=["pT%d" % st["pti"]])

                        def emit_PV(st):
                            d, r_, g0, gsz = st["d"], st["r"], st["g0"], st["gsz"]
                            ob = banks[st["obi"]]
                            obn = "bank%d" % st["obi"]
                            pt = pT[st["pti"]]
                            ptn = "pT%d" % st["pti"]
                            for qq in range(2):
                                qt = st["q0"] + qq
                                for j in range(2):
                                    ti = st["tbase"] + r_ * (st["nqt"] + 1) + qt + j
                                    lhs = v3[:, ti, 0:2, :] if hh == 0 else v3[:, ti, 1:3, :]
                                    oq = (qt - g0)
                                    op(PE, lambda: eng[PE].matmul(ob[:, oq * 128:(oq + 1) * 128], lhs.rearrange("p a b -> p (a b)"),
                                                                   pt[:, (qq * 2 + j) * 128:(qq * 2 + j + 1) * 128], start=(j == 0), stop=(j == 1)),
                                       r=["v3_%d" % (ti // 8), ptn], w=[obn], inc=(qq == 1 and j == 1))
                            if st["glast"]:
                                a0 = 128 * g0 * d + r_
                                a1 = a0 + (128 * gsz - 1) * d + 1
                                aview = acc[:, a0:a1:d]
                                accr = ["acc_c%d" % cc for cc in range(a0 // CH, (a1 - 1) // CH + 1)]
                                if st["pi"] == 0:
                                    op(DVE, lambda: eng[DVE].tensor_copy(out=aview, in_=ob[:, 0:128 * gsz]), r=[obn], w=accr)
                                else:
                                    op(DVE, lambda: eng[DVE].tensor_tensor(out=aview, in0=aview, in1=ob[:, 0:128 * gsz], op=ALU.add), r=[obn] + accr, w=accr)

                        fills = []
                        for st in steps:
                            if st["fill"] is not None and (not fills or fills[-1] != st["fill"]):
                                fills.append(st["fill"])
                        fslot = {}

                        def emit_fill(f):
                            pi_, fi = f
                            d_ = PATTERNS[pi_][1]
                            slot = fillc[0] % 2
                            fillc[0] += 1
                            fslot[f] = slot
                            if d_ == 1:
                                src = qT[prow, fi * 1024:(fi + 1) * 1024]
                                dstq = qS[slot][prow, :]
                            elif d_ == 4:
                                src = qT[prow, fi:fi + 4 * 1023 + 1:4]
                                dstq = qS[slot][prow, :]
                            else:
                                src = qT[prow, :].rearrange("p (l r) -> p r l", r=16)[:, 4 * fi:4 * fi + 4, :]
                                dstq = qS[slot][prow, :].rearrange("p (r l) -> p r l", l=256)
                            op(DVE, lambda: eng[DVE].tensor_copy(out=dstq, in_=src), r=["qT_c%d" % cc for cc in range(NCH)], w=["qS%d" % slot])

                        orow = slice(64 - hh * 64, 128 - hh * 64)
                        for sl_ in range(2):
                            op(POOL, lambda: eng[POOL].memset(qS[sl_][orow, :], 0.0), w=["qS%d" % sl_])
                        nf = 0
                        if fills:
                            emit_fill(fills[0])
                            nf = 1
                        pend = []
                        curf = None
                        for st in steps:
                            if st["fill"] is not None and st["fill"] != curf:
                                curf = st["fill"]
                                if nf < len(fills):
                                    emit_fill(fills[nf])
                                    nf += 1
                            if st["fill"] is not None:
                                st["slot"] = fslot[st["fill"]]
                            if deferred and st["pi"] == 0 and st["q0"] == st["g0"]:
                                deferred.pop(0)()
                            emit_S(st)
                            pend.append(st)
                            if len(pend) > SKEW_A:
                                emit_PV(pend.pop(0))
                        while pend:
                            emit_PV(pend.pop(0))
                        if h + 1 < 8:
                            load_bias(h + 1)
                        nrow = slice(nbase, nbase + 64)
                        drow = slice(dbase, dbase + 64)

                        def norm_chunk(c, h=h, nrow=nrow, drow=drow):
                            cs = slice(c * CH, (c + 1) * CH)
                            s = ncnt[0] % 2
                            ncnt[0] += 1
                            an = "acc_c%d" % c
                            op(ACT, lambda: eng[ACT].activation(out=acc[drow, cs], in_=acc[drow, cs], func=AF.Ln), r=[an], w=[an])
                            op(ACT, lambda: eng[ACT].activation(out=acc[drow, cs], in_=acc[drow, cs], func=AF.Exp, scale=-1.0), r=[an], w=[an])
                            op(DVE, lambda: eng[DVE].tensor_copy(out=rdenA[s][nrow, :], in_=acc[drow, cs]), r=[an], w=["rdenA0"])
                            op(DVE, lambda: eng[DVE].tensor_tensor(out=ostA[s][nrow, :], in0=acc[nrow, cs], in1=rdenA[s][nrow, :], op=ALU.mult),
                               r=[an, "rdenA0"], w=["ostA%d" % s])
                            dma(SP, "ostA%d" % s, mixT[h * 64:(h + 1) * 64, cs], ostA[s][nrow, :], r=["ostA%d" % s], w=["mixT%d" % h])

                        if hh == 0 or h < 7:
                            deferred.extend([(lambda c=c, f=norm_chunk: f(c)) for c in range(NCH)])
                        else:
                            for c in range(NCH):
                                norm_chunk(c)
                T.barrier()

            cqn = botA[:].bitcast(BF16).rearrange("p (k t) -> p k t", k=2)
            ckvn = botB[:]
            kpe = botC[:, 0:S]
            with ExitStack() as es0:
                sq = [sb("sqb%d" % i, [128, CH], BF16, es0) for i in range(2)]
                rtmp = [sb("rtmpb%d" % i, [128, CH], F32, es0) for i in range(2)]
                rstd = [sb("rstdb%d" % i, [128, CH], F32, es0) for i in range(2)]
                lat = [sb("lat%d" % i, [128, 3, CH], F32, es0) for i in range(2)]
                kr1 = [sb("kr1_%d" % i, [128, CH], F32, es0) for i in range(2)]
                kr2 = [sb("kr2_%d" % i, [128, CH], F32, es0) for i in range(2)]
                csc = [sb("csc%d" % i, [128, 2, CH], F32, es0) for i in range(2)]

                def lat_proj(c):
                    cs = slice(c * CH, (c + 1) * CH)
                    s = c % 2
                    dma(SP, "csc%d" % s, csc[s][64:96, 0, :], cos_in[64:96, cs], w=["csc%d" % s])
                    dma(SP, "csc%d" % s, csc[s][64:96, 1, :], sin_in[64:96, cs], w=["csc%d" % s])
                    ur = ["uT%d_%d" % (kc, c) for kc in range(8)]
                    for i, col0 in enumerate((1536, 1664, 1792)):
                        b = banks[1 + i]
                        bn = "bank%d" % (1 + i)
                        for kc in range(8):
                            op(PE, lambda: eng[PE].matmul(b[:], win[:, kc, col0:col0 + 128], uT[:, kc, cs], start=(kc == 0), stop=(kc == 7)),
                               r=["win"] + ur, w=[bn], inc=(kc == 7))
                        op(ACT, lambda: eng[ACT].copy(out=lat[s][:, i, :], in_=b[:]), r=[bn], w=["lat%d_%d" % (s, i)])
                    for kc in range(8):
                        op(PE, lambda: eng[PE].matmul(banks[4][0:96, :], win[:, kc, 1856:1952], uT[:, kc, cs], start=(kc == 0), stop=(kc == 7)),
                           r=["win"] + ur, w=["bank4"], inc=(kc == 7))
                    for kc in range(8):
                        op(PE, lambda: eng[PE].matmul(banks[5][0:96, :], wkr_sw[:, kc, :], uT[:, kc, cs], start=(kc == 0), stop=(kc == 7)),
                           r=["wkr_sw"] + ur, w=["bank5"], inc=(kc == 7))
                    op(DVE, lambda: eng[DVE].tensor_tensor(out=kr1[s][64:96, :], in0=banks[4][64:96, :], in1=csc[s][64:96, 0, :], op=ALU.mult),
                       r=["bank4", "csc%d" % s], w=["kr1_%d" % s])
                    op(DVE, lambda: eng[DVE].tensor_tensor(out=kr2[s][64:96, :], in0=banks[5][64:96, :], in1=csc[s][64:96, 1, :], op=ALU.mult),
                       r=["bank5", "csc%d" % s], w=["kr2_%d" % s])
                    op(POOL, lambda: eng[POOL].tensor_tensor(out=kpe[64:96, cs], in0=kr1[s][64:96, :], in1=kr2[s][64:96, :], op=ALU.add),
                       r=["kr1_%d" % s, "kr2_%d" % s], w=["kpe_c%d" % c])

                def lat_norm(c):
                    cs = slice(c * CH, (c + 1) * CH)
                    s = c % 2
                    for i in range(3):
                        q = i % 2
                        op(ACT, lambda: eng[ACT].activation(out=sq[q][:], in_=lat[s][:, i, :], func=AF.Square), r=["lat%d_%d" % (s, i)], w=["sqb%d" % q])
                        bsel = banks[6] if i < 2 else banks[0]
                        bname = "bank6" if i < 2 else "bank0"
                        op(PE, lambda: eng[PE].matmul(bsel[:], ones[:], sq[q][:], start=(i != 1), stop=(i != 0)),
                           r=["ones", "sqb%d" % q], w=[bname])
                    for k2, (bsel, bname, nfeat, idxs) in enumerate(((banks[6], "bank6", 256, (0, 1)), (banks[0], "bank0", 128, (2,)))):
                        op(ACT, lambda: eng[ACT].activation(out=rtmp[k2][:], in_=bsel[:], func=AF.Ln, scale=1.0 / nfeat, bias=eps_t[:, 0:1]),
                           r=[bname, "eps"], w=["rtmpb%d" % k2])
                        op(ACT, lambda: eng[ACT].activation(out=rstd[k2][:], in_=rtmp[k2][:], func=AF.Exp, scale=-0.5), r=["rtmpb%d" % k2], w=["rstdb%d" % k2])
                        for i in idxs:
                            dstl = cqn[:, i, cs] if i < 2 else ckvn[:, cs]
                            gsc = gq[:, i:i + 1] if i < 2 else gkv[:, 0:1]
                            op(DVE, lambda: eng[DVE].scalar_tensor_tensor(out=dstl, in0=lat[s][:, i, :], scalar=gsc, in1=rstd[k2][:], op0=ALU.mult, op1=ALU.mult),
                               r=["lat%d_%d" % (s, i), "rstdb%d" % k2, "gq", "gkv"], w=[("cqn_c%d" if i < 2 else "ckvn_c%d") % c])

                for c in range(NCH):
                    lat_proj(c)
                    if c > 0:
                        lat_norm(c - 1)
                lat_norm(NCH - 1)
                T.barrier()
        if debug:
            with ExitStack() as es0:
                dst = sb("dstage2", [128, S], F32, es0)
                for i in range(2):
                    op(DVE, lambda: eng[DVE].tensor_copy(out=dst[:], in_=cqn[:, i, :]), r=["cqn_c%d" % cc for cc in range(NCH)], w=["dstage"])
                    dma(SP, "dbg", dbg["d_cq"][i * 128:(i + 1) * 128, :], dst[:], r=["dstage"])
                op(DVE, lambda: eng[DVE].tensor_copy(out=dst[:], in_=ckvn[:]), r=["ckvn_c%d" % cc for cc in range(NCH)], w=["dstage"])
                dma(SP, "dbg", dbg["d_ckv"][:, :], dst[:], r=["dstage"])
                op(DVE, lambda: eng[DVE].memset(dst[:], 0.0), w=["dstage"])
                op(DVE, lambda: eng[DVE].tensor_copy(out=dst[64:96, :], in_=kpe[64:96, :]), r=["kpe_c%d" % cc for cc in range(NCH)], w=["dstage"])
                dma(SP, "dbg", dbg["d_kpe"][:, :], dst[:], r=["dstage"])
                T.barrier()

        NWA = 4
        wo = sb("wo", [128, 8, D], BF16)
        wuA = sb("wuA", [128, 8, NWA * 512], BF16)
        wd_v = w_down.rearrange("(j p) o -> p j o", p=128)
        wu_v = w_up.rearrange("(k p) f -> p k f", p=128)
        with ExitStack() as eB:
            csB = [sb("csB%d" % i, [128, 2, CH], F32, eB) for i in range(2)]
            wq = sb("wq", [128, 2, 768], BF16, eB)
            wqs = sb("wqs", [128, 2, 768], BF16, eB)
            wkv = sb("wkv", [128, 2, 8, 64], BF16, eB)
            vall = sb("vall", [128, 32, 8, 128], BF16, eB)
            qh = [sb("qh%d" % i, [128, S], BF16, eB) for i in range(2)]
            kh = [sb("kh%d" % i, [128, S], BF16, eB) for i in range(2)]
            stq = qh[1][:, :].bitcast(F32)
            stk = kh[1][:, :].bitcast(F32)
            for kc in range(2):
                dma(SP, "c_wq", stq[:, kc * 768:(kc + 1) * 768], w_qb[kc * 128:(kc + 1) * 128, :], w=["qh1"])
            dma(SP, "c_wkv", stk[:, 0:1024], w_kvb, w=["kh1"])
            for kc in range(8):
                dma(POOL, "c_wo", wo[:, kc, :], w_out[kc * 128:(kc + 1) * 128, :], w=["wo"])
            for b in range(NWA):
                dma(POOL, "c_wu%d" % b, wuA[:, :, b * 512:(b + 1) * 512], wu_v[:, :, b * 512:(b + 1) * 512], w=["wu%d" % b])
            op(DVE, lambda: eng[DVE].memset(wqs[:], 0.0), w=["wqs"])
            op(DVE, lambda: eng[DVE].tensor_copy(out=wq[:].rearrange("p k c -> p (k c)"), in_=stq[:, 0:1536]), r=["qh1"], w=["wq"])
            op(DVE, lambda: eng[DVE].tensor_copy(out=wkv[:], in_=stk[:, 0:1024].rearrange("p (h t e) -> p t h e", t=2, e=64)), r=["kh1"], w=["wkv"])
            wq_v = wq[:].rearrange("p k (h e) -> p k h e", e=96)
            wqs_v = wqs[:].rearrange("p k (h e) -> p k h e", e=96)
            for kc in range(2):
                op(DVE, lambda: eng[DVE].tensor_scalar(out=wqs_v[:, kc, :, 64:80], in0=wq_v[:, kc, :, 80:96], scalar1=-1.0, scalar2=None, op0=ALU.mult),
                   r=["wq"], w=["wqs"])
                op(DVE, lambda: eng[DVE].tensor_copy(out=wqs_v[:, kc, :, 80:96], in_=wq_v[:, kc, :, 64:80]), r=["wq"], w=["wqs"])

            def build_vall(t):
                bi = 1 + t % 2
                op(PE, lambda: eng[PE].matmul(banks[bi][:], ckvn[:, t * 128:(t + 1) * 128], wkv[:, 1, :, :].rearrange("p h e -> p (h e)"), start=True, stop=True),
                   r=["ckvn_c%d" % (t // 4), "wkv"], w=["bank%d" % bi])
                srcv = banks[bi][:].rearrange("p (h e) -> p h e", e=64)
                op(DVE, lambda: eng[DVE].memset(vall[:, t, :, 64:128], 1.0), w=["vall1_%d" % t])
                op(ACT, lambda: eng[ACT].copy(out=vall[:, t, :, 0:64], in_=srcv), r=["bank%d" % bi], w=["vall_%d" % t])

            t1 = sb("t1", [128, CH], F32, eB)
            t2 = sb("t2", [128, CH], F32, eB)
            pB = [sb("pB%d" % i, [128, 512], BF16, eB) for i in range(3)]
            ostB = [sb("ostB%d" % i, [128, CH], BF16, eB) for i in range(2)]
            rdB = sb("rdB", [128, CH], F32, eB)
            scale_b = 96.0 ** -0.5
            print("phase B sbuf free:", nc.sbuf_bytes_remaining)
            pc = 0
            ncs = [0, 0]

            def emit_proj(h, c, pro=False):
                nonlocal pc
                s_ = h % 2
                qt_, kt_ = qh[s_], kh[s_]
                qn, kn = "qh%d" % s_, "kh%d" % s_
                cs = slice(c * CH, (c + 1) * CH)
                for kc in range(2):
                    op(PE, lambda: eng[PE].matmul(banks[4][0:96, :], wq[:, kc, h * 96:(h + 1) * 96], cqn[:, kc, cs], start=(kc == 0), stop=(kc == 1)),
                       r=["wq", "cqn_c%d" % c], w=["bank4"], inc=(kc == 1))
                for kc in range(2):
                    op(PE, lambda: eng[PE].matmul(banks[5][0:96, :], wqs[:, kc, h * 96:(h + 1) * 96], cqn[:, kc, cs], start=(kc == 0), stop=(kc == 1)),
                       r=["wqs", "cqn_c%d" % c], w=["bank5"], inc=(kc == 1))
                op(PE, lambda: eng[PE].matmul(bank7[0:64, :], wkv[:, 0, h, :], ckvn[:, cs], start=True, stop=True),
                   r=["wkv", "ckvn_c%d" % c], w=["bankT"])
                sl = ncs[0] % 2
                ncs[0] += 1
                dma(SP, "csB%d" % sl, csB[sl][64:96, 0, :], cos_in[64:96, cs], w=["csB%d" % sl])
                dma(SP, "csB%d" % sl, csB[sl][64:96, 1, :], sin_in[64:96, cs], w=["csB%d" % sl])
                if pro:
                    op(ACT, lambda: eng[ACT].copy(out=qt_[0:64, cs], in_=banks[4][0:64, :]), r=["bank4"], w=[qn])
                else:
                    op(DVE, lambda: eng[DVE].tensor_copy(out=qt_[0:64, cs], in_=banks[4][0:64, :]), r=["bank4"], w=[qn])
                op(DVE, lambda: eng[DVE].tensor_tensor(out=t1[64:96, :], in0=banks[4][64:96, :], in1=csB[sl][64:96, 0, :], op=ALU.mult),
                   r=["bank4", "csB%d" % sl], w=["t1"])
                op(DVE, lambda: eng[DVE].tensor_tensor(out=t2[64:96, :], in0=banks[5][64:96, :], in1=csB[sl][64:96, 1, :], op=ALU.mult),
                   r=["bank5", "csB%d" % sl], w=["t2"])
                op(DVE, lambda: eng[DVE].tensor_tensor(out=qt_[64:96, cs], in0=t1[64:96, :], in1=t2[64:96, :], op=ALU.add),
                   r=["t1", "t2"], w=[qn])
                if pro:
                    op(ACT, lambda: eng[ACT].copy(out=kt_[0:64, cs], in_=bank7[0:64, :]), r=["bankT"], w=[kn])
                else:
                    op(DVE, lambda: eng[DVE].tensor_copy(out=kt_[0:64, cs], in_=bank7[0:64, :]), r=["bankT"], w=[kn])
                if c == NCH - 1:
                    op(DVE, lambda: eng[DVE].tensor_copy(out=kt_[64:96, :], in_=kpe[64:96, :]), r=["kpe_c%d" % cc for cc in range(NCH)], w=[kn])

            def emit_QK(st):
                nonlocal pc
                h, c, kt = st["h"], st["c"], st["kt"]
                s_ = h % 2
                cs = slice(c * CH, (c + 1) * CH)
                sbi = 1 + (pc % 3)
                pc += 1
                st["pti"] = ncs[1] % 3
                ncs[1] += 1
                op(PE, lambda: eng[PE].matmul(banks[sbi][:], kh[s_][0:96, kt * 128:(kt + 1) * 128], qh[s_][0:96, cs], start=True, stop=True),
                   r=["kh%d" % s_, "qh%d" % s_], w=["bank%d" % sbi])
                pt = pB[st["pti"]]
                op(ACT, lambda: eng[ACT].activation(out=pt[:], in_=banks[sbi][:], func=AF.Exp, scale=scale_b), r=["bank%d" % sbi], w=["pB%d" % st["pti"]])

            def emit_PVB(st):
                h, c, kt = st["h"], st["c"], st["kt"]
                cs = slice(c * CH, (c + 1) * CH)
                obi = 0 if c % 2 == 0 else 6
                ob = banks[obi]
                obn = "bank%d" % obi
                pt = pB[st["pti"]]
                op(PE, lambda: eng[PE].matmul(ob[:], vall[:, kt, h, :], pt[:], start=(kt == 0), stop=(kt == 31)),
                   r=["vall_%d" % kt, "vall1_%d" % kt, "pB%d" % st["pti"]], w=[obn], inc=(kt == 31))
                if kt == 31:
                    so = c % 2
                    op(DVE, lambda: eng[DVE].reciprocal(out=rdB[0:64, :], in_=ob[64:128, :]), r=[obn], w=["rdB"])
                    op(DVE, lambda: eng[DVE].tensor_tensor(out=ostB[so][0:64, :], in0=ob[0:64, :], in1=rdB[0:64, :], op=ALU.mult),
                       r=[obn, "rdB"], w=["ostB%d" % so])
                    dma(SP, "ostB%d" % so, mixT[512 + h * 64:512 + (h + 1) * 64, cs], ostB[so][0:64, :], r=["ostB%d" % so], w=["mixT%d" % (8 + h)])

            for c in range(NCH):
                emit_proj(0, c, pro=True)
                for t in range(4 * c, 4 * c + 4):
                    build_vall(t)
            pend = []
            for h in range(8):
                for c in range(NCH):
                    if h + 1 < 8:
                        emit_proj(h + 1, c)
                    for kt in range(32):
                        st = dict(h=h, c=c, kt=kt)
                        emit_QK(st)
                        pend.append(st)
                        if len(pend) > SKEW:
                            emit_PVB(pend.pop(0))
            while pend:
                emit_PVB(pend.pop(0))
            T.barrier()

        mix_all = ["mixT%d" % i for i in range(16)]
        if debug:
            with ExitStack() as eD:
                dm = sb("dm", [128, S], BF16, eD)
                dmf = sb("dmf", [128, S], F32, eD)
                for kc in range(8):
                    dma(SP, "dm", dm[:], mixT[kc * 128:(kc + 1) * 128, :], r=mix_all, w=["dm"])
                    op(DVE, lambda: eng[DVE].tensor_copy(out=dmf[:], in_=dm[:]), r=["dm"], w=["dmf"])
                    dma(SP, "dmo", dbg["d_mix"][kc * 128:(kc + 1) * 128, :], dmf[:], r=["dmf"])
                T.barrier()

        with ExitStack() as eC:
            wuB = sb("wuB", [128, 8, (8 - NWA) * 512], BF16, eC)
            wd = sb("wd", [128, 32, D], BF16, eC)
            gm = sb("gm", [128, 8], F32, eC)
            gf = sb("gf", [128, 8], F32, eC)
            dma(SP, "c_gm", gm[:], g_mlp, w=["gm"])
            dma(SP, "c_gf", gf[:], g_fin, w=["gf"])
            def load_wd(j4, after=()):
                dma(POOL, "c_wd%d" % j4, wd[:, 4 * j4:4 * j4 + 4, :], wd_v[:, 4 * j4:4 * j4 + 4, :], r=list(after), w=["wd%d" % j4])

            def load_rest_weights(after):
                load_wd(0, after)
                load_wd(1)
                for b in range(NWA, 8):
                    dma(POOL, "c_wu%d" % b, wuB[:, :, (b - NWA) * 512:(b - NWA + 1) * 512], wu_v[:, :, b * 512:(b + 1) * 512], w=["wu%d" % b])
                for j4 in range(2, 8):
                    load_wd(j4)

            def wu_cols(kc, j):
                if j // 4 < NWA:
                    return wuA[:, kc, j * 128:(j + 1) * 128]
                return wuB[:, kc, (j - NWA * 4) * 128:(j - NWA * 4 + 1) * 128]

            def wd_rows(j, oc):
                return wd[:, j, oc * 128:(oc + 1) * 128]

            hxs = [botA[:].rearrange("p (k t) -> p k t", k=8), sb("hxB", [128, 8, CH], F32, eC)[:]]
            mu = botB[:].rearrange("p (k t) -> p k t", k=8)
            act = botC[:, 0:S].rearrange("p (k t) -> p k t", k=8)
            sqc = [botC[:, S + i * CH:S + (i + 1) * CH] for i in range(2)]
            rl = [sb("rl%d" % i, [128, CH], F32, eC) for i in range(2)]
            rt = sb("rtC", [128, CH], F32, eC)
            rs = sb("rsC", [128, CH], F32, eC)
            bc = 0
            print("phase C sbuf free:", nc.sbuf_bytes_remaining)

            def hn(c, k):
                return "hx%d_%d" % (c % 2, k)

            rt2 = rt
            rs2 = sb("rsC2", [128, CH], F32, eC)
            sqn = [0]

            def stat_part(c, kc):
                hx = hxs[c % 2]
                q = sqn[0] % 2
                sqn[0] += 1
                op(ACT, lambda: eng[ACT].activation(out=sqc[q], in_=hx[:, kc, :], func=AF.Square), r=[hn(c, kc)], w=["sqc%d" % q])
                op(PE, lambda: eng[PE].matmul(banks[0][:], ones[:], sqc[q], start=(kc == 0), stop=(kc == 7)),
                   r=["ones", "sqc%d" % q], w=["bank0"])

            def stat_fin(which):
                rt_, rs_, nm = (rt, rs, "") if which == 1 else (rt2, rs2, "2")
                op(ACT, lambda: eng[ACT].activation(out=rt_[:], in_=banks[0][:], func=AF.Ln, scale=1.0 / D, bias=eps_t[:, 0:1]),
                   r=["bank0", "eps"], w=["rtC"])
                op(ACT, lambda: eng[ACT].activation(out=rs_[:], in_=rt_[:], func=AF.Exp, scale=-0.5), r=["rtC"], w=["rsC" + nm])

            def load_x(c):
                cs_ = slice(c * CH, (c + 1) * CH)
                for oc in range(8):
                    dma(SP, "hxl%d_%d" % (c % 2, oc), hxs[c % 2][:, oc, :], xT_v[:, oc, cs_], w=[hn(c, oc)])

            def load_mc(c):
                cs_ = slice(c * CH, (c + 1) * CH)
                dma(SP, "mc", mu, mixT_v[:, :, cs_], r=mix_all, w=["mu%d" % k for k in range(8)])

            UPB = [(banks[1][:], "bank1"), (banks[2][:], "bank2"), (banks[3][:], "bank3"), (bank7, "bankT")]
            DNB = [(banks[4][:], "bank4"), (banks[5][:], "bank5"), (banks[6][:], "bank6")]
            dc = 0

            def out_proj(c):
                nonlocal bc
                hx = hxs[c % 2]
                for oc in range(8):
                    bap, bnm = UPB[bc % 4]; bc += 1
                    for kc in range(8):
                        op(PE, lambda: eng[PE].matmul(bap, wo[:, kc, oc * 128:(oc + 1) * 128], mu[:, kc, :], start=(kc == 0), stop=(kc == 7)),
                           r=["wo", "mu%d" % kc], w=[bnm], inc=(kc == 7))
                    op(DVE, lambda: eng[DVE].tensor_tensor(out=hx[:, oc, :], in0=hx[:, oc, :], in1=bap, op=ALU.add),
                       r=[bnm, hn(c, oc)], w=[hn(c, oc)])
                    if oc >= 1:
                        stat_part(c, oc - 1)
                stat_part(c, 7)
                stat_fin(1)

            def final_norm(c):
                cs_ = slice(c * CH, (c + 1) * CH)
                hx = hxs[c % 2]
                for kc in range(8):
                    op(DVE, lambda: eng[DVE].scalar_tensor_tensor(out=hx[:, kc, :], in0=hx[:, kc, :], scalar=gf[:, kc:kc + 1], in1=rs2[:], op0=ALU.mult, op1=ALU.mult),
                       r=[hn(c, kc), "rsC2", "gf"], w=[hn(c, kc)])
                    dma(SP, "outst%d_%d" % (c % 2, kc), outT_v[:, kc, cs_], hx[:, kc, :], r=[hn(c, kc)], w=["out"])

            load_mc(0)
            load_x(0)
            load_rest_weights([hn(0, k) for k in range(8)] + ["mu%d" % k for k in range(8)])
            out_proj(0)
            load_x(1)
            for c in range(NCH):
                hx = hxs[c % 2]
                for kc in range(8):
                    op(DVE, lambda: eng[DVE].scalar_tensor_tensor(out=mu[:, kc, :], in0=hx[:, kc, :], scalar=gm[:, kc:kc + 1], in1=rs[:], op0=ALU.mult, op1=ALU.mult),
                       r=[hn(c, kc), "rsC", "gm"], w=["mu%d" % kc])
                for qf in range(4):
                    for jj in range(8):
                        j = qf * 8 + jj
                        bap, bnm = UPB[bc % 4]; bc += 1
                        for kc in range(8):
                            op(PE, lambda: eng[PE].matmul(bap, wu_cols(kc, j), mu[:, kc, :], start=(kc == 0), stop=(kc == 7)),
                               r=["wu%d" % (j // 4), "mu%d" % kc], w=[bnm], inc=(kc == 7))
                        q = j % 2
                        op(DVE, lambda: eng[DVE].tensor_scalar(out=rl[q][:], in0=bap, scalar1=0.0, scalar2=None, op0=ALU.max),
                           r=[bnm], w=["rl%d" % q])
                        op(ACT, lambda: eng[ACT].activation(out=act[:, jj, :], in_=rl[q][:], func=AF.Square), r=["rl%d" % q], w=["act%d" % jj])
                    if qf == 3 and c + 1 < NCH:
                        load_mc(c + 1)
                    for oc in range(8):
                        bap, bnm = DNB[dc % 3]; dc += 1
                        for jj in range(8):
                            j = qf * 8 + jj
                            op(PE, lambda: eng[PE].matmul(bap, wd_rows(j, oc), act[:, jj, :], start=(jj == 0), stop=(jj == 7)),
                               r=["wd%d" % (j // 4), "act%d" % jj], w=[bnm], inc=(jj == 7))
                        op(DVE, lambda: eng[DVE].tensor_tensor(out=hx[:, oc, :], in0=hx[:, oc, :], in1=bap, op=ALU.add),
                           r=[bnm, hn(c, oc)], w=[hn(c, oc)])
                        if qf == 3 and oc >= 1:
                            stat_part(c, oc - 1)
                stat_part(c, 7)
                stat_fin(2)
                if c + 1 < NCH:
                    out_proj(c + 1)
                final_norm(c)
                if c + 2 < NCH:
                    load_x(c + 2)

        T.barrier()
        print("instructions:", dict(T.cnt), "waits:", T.nwaits)
    return nc


_CACHE = {}


def _host_constants(rel_bias):
    rb = np.asarray(rel_bias, np.float32)
    i = np.arange(128)[:, None]
    m = np.arange(128)[None, :]
    out = np.full((8, 128, 7, 4, 128), NEG, np.float32)

    def tile(d, j, edge, h):
        rel = -64 + 128 * j + i - m
        v = np.abs(rel) <= 64
        if edge and j == 0:
            v = v & (i >= 64)
        if edge and j == 1:
            v = v & (i < 64)
        g = rb[t5_buckets(rel * d), h]
        return np.where(v, g, np.float32(NEG))

    for h in range(8):
        for pi, (win_, d) in enumerate(PATTERNS):
            if d == 16:
                combos = [(6, True, True)]
            else:
                combos = [(pi * 3 + 0, True, False), (pi * 3 + 1, False, False), (pi * 3 + 2, False, True)]
            for (ci, first, last) in combos:
                out[h, :, ci, 0, :] = tile(d, 0, first, h)
                out[h, :, ci, 1, :] = tile(d, 1, False, h)
                out[h, :, ci, 2, :] = tile(d, 0, False, h)
                out[h, :, ci, 3, :] = tile(d, 1, last, h)
    return out.reshape(8, 128, 7 * 512)


def _rope_tables():
    inv_freq = (np.float32(10000.0) ** (-np.arange(0, 32, 2, dtype=np.float32) / np.float32(32))).astype(np.float32)
    pos = np.arange(S, dtype=np.float32)
    fr = (pos[:, None] * inv_freq[None, :]).astype(np.float32)
    cos = np.cos(fr).astype(np.float32).T
    sin = np.sin(fr).astype(np.float32).T
    cT = np.ones((128, S), np.float32)
    sT = np.zeros((128, S), np.float32)
    cT[64:80] = cos; cT[80:96] = cos
    sT[64:80] = sin; sT[80:96] = sin
    return cT, sT


def _lay(v, k):
    return np.ascontiguousarray(np.asarray(v, np.float32).reshape(k, 128).T)


def kernel(x, mix_norm_g, w_in, q_norm_g, w_q_b, kv_norm_g, w_kv_b, w_out, mlp_norm_g, w_up, w_down,
           rel_bias, final_norm_g, _debug=False, _cores=8):
    x = np.asarray(x, np.float32)
    key = ("nc", _debug)
    if key not in _CACHE:
        _CACHE[key] = build_program(debug=_debug)
    nc = _CACHE[key]
    cT, sT = _rope_tables()
    shared = {
        "w_in": np.ascontiguousarray(np.asarray(w_in, np.float32)[0]),
        "g_mix": _lay(mix_norm_g, 8),
        "w_qb": np.ascontiguousarray(np.asarray(w_q_b, np.float32)[0]),
        "g_q": _lay(q_norm_g, 2),
        "w_kvb": np.ascontiguousarray(np.asarray(w_kv_b, np.float32)[0]),
        "g_kv": _lay(kv_norm_g, 1),
        "w_out": np.ascontiguousarray(np.asarray(w_out, np.float32)[0]),
        "g_mlp": _lay(mlp_norm_g, 8),
        "w_up": np.ascontiguousarray(np.asarray(w_up, np.float32)[0]),
        "w_down": np.ascontiguousarray(np.asarray(w_down, np.float32)[0]),
        "g_fin": _lay(final_norm_g, 8),
        "biasT": _host_constants(rel_bias),
        "ident": np.eye(128, dtype=np.float32),
        "cosT": cT,
        "sinT": sT,
    }
    in_maps = []
    for b in range(_cores):
        m = dict(shared)
        m["xT"] = np.ascontiguousarray(x[b].T)
        in_maps.append(m)
    res = run_bass_kernel_spmd(nc, in_maps, core_ids=list(range(_cores)))
    if _debug:
        return res.results
    out = np.stack([np.ascontiguousarray(r["outT"].T) for r in res.results], axis=0)
    return out.astype(np.float32)
```
